# Optimizing a Trainium2 kernel written in Bass

```python
import jax, jax.numpy as jnp
from jax import lax
import numpy as np


D_MODEL = 1024
BATCH = 2
SEQ = 16384
DEPTH = 4

N_MIXERS = 3
N_META = 16
BLOCK = 128
FRONT_PAD = BLOCK - N_META
EPS = 1e-6
ROPE_THETA = 10000.0

A_HEADS = 16
A_KV = 4
A_HD = 64
A_WIN = 128
A_W = A_HEADS * A_HD
A_KVW = A_KV * A_HD
A_IN = A_W + 2 * A_KVW + A_W

B_HEADS = 8
B_DK = 128
B_DV = 256
B_CONV = 4
B_QK = B_HEADS * B_DK
B_W = B_HEADS * B_DV
B_SIZES = (B_QK, B_QK, B_W, B_HEADS, B_HEADS, B_W, B_W)
B_IN = sum(B_SIZES)

C_HEADS = 8
C_DK = 128
C_DV = 128
C_CHUNK = 64
C_K = C_HEADS * C_DK
C_W = C_HEADS * C_DV
C_SIZES = (C_K, C_K, C_W, C_W)
C_IN = sum(C_SIZES)

N_A = (DEPTH + 2) // 3
N_B = (DEPTH + 1) // 3
N_C = DEPTH // 3

kernel_name = 'hybrid_swa_mlstm_hgrn2_meta'


def _split(t, sizes):
    pts = [int(p) for p in np.cumsum(sizes)[:-1]]
    return jnp.split(t, pts, axis=-1)


def rmsnorm(x, g):
    xf = x.astype(jnp.float32)
    y = xf * lax.rsqrt(jnp.mean(xf * xf, axis=-1, keepdims=True) + EPS)
    return (y * g.astype(jnp.float32)).astype(x.dtype)


def rope(x, pos):
    half = x.shape[-1] // 2
    inv = ROPE_THETA ** (-jnp.arange(half, dtype=jnp.float32) / half)
    ang = pos.astype(jnp.float32)[:, None] * inv[None, :]
    cos = jnp.cos(ang)[None, :, None, :]
    sin = jnp.sin(ang)[None, :, None, :]
    xf = x.astype(jnp.float32)
    x1, x2 = xf[..., :half], xf[..., half:]
    return jnp.concatenate([x1 * cos - x2 * sin, x2 * cos + x1 * sin], axis=-1).astype(x.dtype)


def causal_conv(x, w, b):
    K = w.shape[0]
    L = x.shape[1]
    xp = jnp.pad(x, ((0, 0), (K - 1, 0), (0, 0)))
    return sum(xp[:, j:j + L] * w[j] for j in range(K)) + b


def _to_chunks(t, size):
    b, h, lp = t.shape[:3]
    return jnp.moveaxis(t.reshape(b, h, lp // size, size, *t.shape[3:]), 2, 0)


def _from_chunks(t):
    t = jnp.moveaxis(t, 0, 2)
    return t.reshape(t.shape[0], t.shape[1], t.shape[2] * t.shape[3], t.shape[4])


def sliding_window_attention(q, k, v, sinks, valid):
    bsz, lp = q.shape[:2]
    nb = lp // BLOCK
    g = A_HEADS // A_KV
    qb = q.astype(jnp.float32).reshape(bsz, nb, BLOCK, A_KV, g, A_HD)
    kb = k.astype(jnp.float32).reshape(bsz, nb, BLOCK, A_KV, A_HD)
    vb = v.astype(jnp.float32).reshape(bsz, nb, BLOCK, A_KV, A_HD)
    prev = lambda t: jnp.pad(t, ((0, 0), (1, 0)) + ((0, 0),) * (t.ndim - 2))[:, :-1]
    kk = jnp.concatenate([prev(kb), kb], axis=2)
    vv = jnp.concatenate([prev(vb), vb], axis=2)
    vb_mask = valid.reshape(nb, BLOCK)
    kvalid = jnp.concatenate([jnp.pad(vb_mask, ((1, 0), (0, 0)))[:-1], vb_mask], axis=1)
    s = jnp.einsum('bnqhgd,bnkhd->bnhgqk', qb, kk) * (A_HD ** -0.5)
    dist = (jnp.arange(BLOCK)[:, None] + BLOCK) - jnp.arange(2 * BLOCK)[None, :]
    band = (dist >= 0) & (dist < A_WIN)
    mask = band[None] & kvalid[:, None, :]
    s = jnp.where(mask[None, :, None, None], s, -jnp.inf)
    sink = sinks.astype(jnp.float32).reshape(A_KV, g, 1)
    m = jnp.maximum(s.max(axis=-1), sink)
    e = jnp.exp(s - m[..., None])
    p = e / (e.sum(axis=-1) + jnp.exp(sink - m))[..., None]
    o = jnp.einsum('bnhgqk,bnkhd->bnqhgd', p, vv)
    return o.reshape(bsz, lp, A_W)


def attention_branch(u, w_in, q_gain, k_gain, sinks, w_out, valid, pos):
    bsz, lp, _ = u.shape
    q, k, v, z = _split(u @ w_in, (A_W, A_KVW, A_KVW, A_W))
    q = rope(rmsnorm(q.reshape(bsz, lp, A_HEADS, A_HD), q_gain), pos)
    k = rope(rmsnorm(k.reshape(bsz, lp, A_KV, A_HD), k_gain), pos)
    v = v.reshape(bsz, lp, A_KV, A_HD)
    o = sliding_window_attention(q, k, v, sinks, valid)
    return (o * jax.nn.silu(z.astype(jnp.float32))).astype(u.dtype) @ w_out


def mlstm_chunkwise(q, k, v, log_i, log_f):
    bsz, nh, _, dk = q.shape
    dv = v.shape[-1]
    causal = jnp.tril(jnp.ones((BLOCK, BLOCK), dtype=bool))

    def step(carry, inp):
        C, n, m = carry
        q_, k_, v_, li, lf = inp
        a = jnp.cumsum(lf, axis=-1)
        g = a[..., -1]
        D = jnp.where(causal, a[..., :, None] - a[..., None, :] + li[..., None, :], -jnp.inf)
        inter = a + m[..., None]
        m_t = jnp.maximum(inter, D.max(axis=-1))
        w_inter = jnp.exp(inter - m_t)
        qk = jnp.einsum('bhtd,bhsd->bhts', q_, k_) * jnp.exp(D - m_t[..., None])
        num = w_inter[..., None] * jnp.einsum('bhtd,bhde->bhte', q_, C) + jnp.einsum('bhts,bhse->bhte', qk, v_)
        den = w_inter * jnp.einsum('bhtd,bhd->bht', q_, n) + qk.sum(axis=-1)
        h = num / jnp.maximum(jnp.abs(den), jnp.exp(-m_t))[..., None]
        uu = g[..., None] - a + li
        m_new = jnp.maximum(g + m, uu.max(axis=-1))
        w_old = jnp.exp(g + m - m_new)
        w_s = jnp.exp(uu - m_new[..., None])
        C_new = w_old[..., None, None] * C + jnp.einsum('bhs,bhsd,bhse->bhde', w_s, k_, v_)
        n_new = w_old[..., None] * n + jnp.einsum('bhs,bhsd->bhd', w_s, k_)
        return (C_new, n_new, m_new), h

    init = (jnp.zeros((bsz, nh, dk, dv), jnp.float32),
            jnp.zeros((bsz, nh, dk), jnp.float32),
            jnp.zeros((bsz, nh), jnp.float32))
    xs = (_to_chunks(q, BLOCK), _to_chunks(k, BLOCK), _to_chunks(v, BLOCK),
          _to_chunks(log_i, BLOCK), _to_chunks(log_f, BLOCK))
    _, hs = lax.scan(step, init, xs)
    return _from_chunks(hs)


def mlstm_branch(u, w_in, conv_w, conv_b, gate_bias, h_gain, w_out, valid):
    bsz, lp, _ = u.shape
    q, k, v, ig, fg, og, z = _split(u @ w_in, B_SIZES)
    qk = jax.nn.silu(causal_conv(jnp.concatenate([q, k], axis=-1), conv_w, conv_b))
    q, k = qk[..., :B_QK], qk[..., B_QK:]
    heads = lambda t, d: jnp.transpose(t.astype(jnp.float32).reshape(bsz, lp, B_HEADS, d), (0, 2, 1, 3))
    qh = heads(q, B_DK)
    kh = heads(k, B_DK) * (B_DK ** -0.5)
    vh = heads(v, B_DV)
    gb = gate_bias.astype(jnp.float32)
    vmask = valid[None, :, None]
    log_i = jnp.where(vmask, ig.astype(jnp.float32) + gb[:B_HEADS], -jnp.inf)
    log_f = jnp.where(vmask, jax.nn.log_sigmoid(fg.astype(jnp.float32) + gb[B_HEADS:]), 0.0)
    h = mlstm_chunkwise(qh, kh, vh, jnp.transpose(log_i, (0, 2, 1)), jnp.transpose(log_f, (0, 2, 1)))
    h = jnp.transpose(h, (0, 2, 1, 3))
    h = jax.nn.sigmoid(og.astype(jnp.float32)).reshape(bsz, lp, B_HEADS, B_DV) * h
    h = rmsnorm(h, h_gain.reshape(B_HEADS, B_DV)).reshape(bsz, lp, B_W)
    return (h * jax.nn.silu(z.astype(jnp.float32))).astype(u.dtype) @ w_out


def hgrn2_chunkwise(q, k, v, log_f):
    bsz, nh, _, dk = q.shape
    dv = v.shape[-1]
    causal = jnp.tril(jnp.ones((C_CHUNK, C_CHUNK), dtype=bool))

    def step(S, inp):
        q_, k_, v_, lf = inp
        A = jnp.cumsum(lf, axis=2)
        o_inter = jnp.einsum('bhtd,bhde->bhte', q_ * jnp.exp(A), S)
        dec = jnp.where(causal[..., None], A[:, :, :, None, :] - A[:, :, None, :, :], -jnp.inf)
        att = jnp.einsum('bhtd,bhtsd,bhsd->bhts', q_, jnp.exp(dec), k_)
        o = o_inter + jnp.einsum('bhts,bhse->bhte', att, v_)
        A_last = A[:, :, -1]
        S_new = jnp.exp(A_last)[..., None] * S + jnp.einsum('bhsd,bhse->bhde', k_ * jnp.exp(A_last[:, :, None] - A), v_)
        return S_new, o

    xs = (_to_chunks(q, C_CHUNK), _to_chunks(k, C_CHUNK), _to_chunks(v, C_CHUNK), _to_chunks(log_f, C_CHUNK))
    _, os_ = lax.scan(step, jnp.zeros((bsz, nh, dk, dv), jnp.float32), xs)
    return _from_chunks(os_)


def hgrn2_branch(u, w_in, lower_bound, o_gain, w_out, valid):
    bsz, lp, _ = u.shape
    q, fpre, inp, z = _split(u @ w_in, C_SIZES)
    lb = lower_bound.astype(jnp.float32)
    lf = jnp.logaddexp(jnp.log(lb), jnp.log1p(-lb) + jax.nn.log_sigmoid(fpre.astype(jnp.float32)))
    kk = -jnp.expm1(lf)
    vmask = valid[None, :, None]
    lf = jnp.where(vmask, lf, 0.0)
    kk = jnp.where(vmask, kk, 0.0)
    heads = lambda t, d: jnp.transpose(t.astype(jnp.float32).reshape(bsz, lp, C_HEADS, d), (0, 2, 1, 3))
    o = hgrn2_chunkwise(heads(jax.nn.silu(q.astype(jnp.float32)), C_DK), heads(kk, C_DK),
                        heads(inp, C_DV), heads(lf, C_DK))
    o = jnp.transpose(o, (0, 2, 1, 3))
    o = rmsnorm(o, o_gain.reshape(C_HEADS, C_DV)).reshape(bsz, lp, C_W)
    return (o * jax.nn.silu(z.astype(jnp.float32))).astype(u.dtype) @ w_out


def setup_inputs(seed: int = 0) -> dict:
    key = jax.random.key(seed)
    ks = jax.random.split(key, 20)
    nrm = lambda k, shape, s: s * jax.random.normal(k, shape, jnp.float32)
    f_bias = jnp.linspace(3.0, 6.0, B_HEADS, dtype=jnp.float32)
    return {
        'x': nrm(ks[0], (BATCH, SEQ, D_MODEL), 1.0),
        'meta': nrm(ks[1], (N_META, D_MODEL), 1.0),
        'norm_gain': 1.0 + nrm(ks[2], (DEPTH, D_MODEL), 0.05),
        'a_w_in': nrm(ks[3], (N_A, D_MODEL, A_IN), D_MODEL ** -0.5),
        'a_q_gain': 1.0 + nrm(ks[4], (N_A, A_HD), 0.05),
        'a_k_gain': 1.0 + nrm(ks[5], (N_A, A_HD), 0.05),
        'a_sinks': nrm(ks[6], (N_A, A_HEADS), 0.5),
        'a_w_out': nrm(ks[7], (N_A, A_W, D_MODEL), A_W ** -0.5),
        'b_w_in': nrm(ks[8], (N_B, D_MODEL, B_IN), D_MODEL ** -0.5),
        'b_conv_w': nrm(ks[9], (N_B, B_CONV, 2 * B_QK), B_CONV ** -0.5),
        'b_conv_b': nrm(ks[10], (N_B, 2 * B_QK), 0.02),
        'b_gate_bias': jnp.concatenate([nrm(ks[11], (N_B, B_HEADS), 0.1),
                                        f_bias[None] + nrm(ks[12], (N_B, B_HEADS), 0.1)], axis=-1),
        'b_h_gain': 1.0 + nrm(ks[13], (N_B, B_W), 0.05),
        'b_w_out': nrm(ks[14], (N_B, B_W, D_MODEL), B_W ** -0.5),
        'c_w_in': nrm(ks[15], (N_C, D_MODEL, C_IN), D_MODEL ** -0.5),
        'c_gamma': nrm(ks[16], (DEPTH, C_K), 0.1),
        'c_o_gain': 1.0 + nrm(ks[17], (N_C, C_W), 0.05),
        'c_w_out': nrm(ks[18], (N_C, C_W, D_MODEL), C_W ** -0.5),
    }


def reference(x, meta, norm_gain, a_w_in, a_q_gain, a_k_gain, a_sinks, a_w_out,
              b_w_in, b_conv_w, b_conv_b, b_gate_bias, b_h_gain, b_w_out,
              c_w_in, c_gamma, c_o_gain, c_w_out):
    bsz = x.shape[0]
    h = jnp.concatenate([jnp.zeros((bsz, FRONT_PAD, D_MODEL), x.dtype),
                         jnp.broadcast_to(meta.astype(x.dtype)[None], (bsz, N_META, D_MODEL)), x], axis=1)
    lp = h.shape[1]
    idx = jnp.arange(lp)
    valid = idx >= FRONT_PAD
    pos = (idx - FRONT_PAD).astype(jnp.int32)
    P = jax.nn.softmax(c_gamma.astype(jnp.float32), axis=0)
    lower_bounds = jnp.cumsum(P, axis=0) - P
    for i in range(DEPTH):
        kind = i % N_MIXERS
        j = i // N_MIXERS
        u = rmsnorm(h, norm_gain[i])
        if kind == 0:
            y = attention_branch(u, a_w_in[j], a_q_gain[j], a_k_gain[j], a_sinks[j], a_w_out[j], valid, pos)
        elif kind == 1:
            y = mlstm_branch(u, b_w_in[j], b_conv_w[j], b_conv_b[j], b_gate_bias[j], b_h_gain[j], b_w_out[j], valid)
        else:
            y = hgrn2_branch(u, c_w_in[j], lower_bounds[i], c_o_gain[j], c_w_out[j], valid)
        h = h + jnp.where(valid[None, :, None], y.astype(h.dtype), 0.0).astype(h.dtype)
    return h[:, FRONT_PAD + N_META:]
```

```python
import math
import contextlib
import numpy as np
import concourse.bass as bass
import concourse.mybir as mybir

F32 = mybir.dt.float32
BF16 = mybir.dt.bfloat16
I32 = mybir.dt.int32
AF = mybir.ActivationFunctionType
ALU = mybir.AluOpType
AX = mybir.AxisListType


class Buf:
    __slots__ = ("name", "w", "r")

    def __init__(self, name):
        self.name = name
        self.w = None
        self.r = {}


class Eng:
    def __init__(self, name, h, sem, step=1, same_sync=False):
        self.name = name
        self.h = h
        self.sem = sem
        self.step = step
        self.count = 0
        self.seen = {}
        self.same_sync = same_sync


class Prog:
    def __init__(self, n_dma_sems=8, same_sync=False):
        self.nc = bass.Bass("TRN2", target_bir_lowering=False)
        self.es = contextlib.ExitStack()
        nc = self.nc
        self.n_instr = 0
        mk = lambda n: self.es.enter_context(nc.semaphore(n))
        self.pe = Eng("pe", nc.tensor, mk("s_pe"))
        self.act = Eng("act", nc.scalar, mk("s_act"), same_sync=same_sync)
        self.dve = Eng("dve", nc.vector, mk("s_dve"), same_sync=same_sync)
        self.pool = Eng("pool", nc.gpsimd, mk("s_pool"), same_sync=same_sync)
        self.sp = Eng("sp", nc.sync, mk("s_sp"))
        self.engs = [self.pe, self.act, self.dve, self.pool, self.sp]
        self.dma_sems = {}
        for q in ("sp", "pool", "act"):
            self.dma_sems[q] = [Eng(f"dma_{q}{i}", None, mk(f"d_{q}{i}"), step=16)
                                for i in range(n_dma_sems)]
        self.dma_rr = {"sp": 0, "pool": 0, "act": 0}
        self.cc = Eng("cc", None, mk("s_cc"))
        self.tiles = {}
        self.defer = True
        self.LAT_X = 1.0
        self.LAT_S = 0.3
        self.pending = []
        self.EST = {"pe": 0.22, "act": 0.65, "dve": 0.9, "pool": 1.5, "sp": 0.1}

    def dram(self, name, shape, dtype, kind):
        return self.nc.dram_tensor(name, list(shape), dtype, kind=kind).ap()

    def sbuf(self, name, shape, dtype):
        t = self.es.enter_context(self.nc.sbuf_tensor(name, list(shape), dtype))
        return t

    def psum(self, name, shape, dtype=F32):
        t = self.es.enter_context(self.nc.psum_tensor(name, list(shape), dtype))
        return t

    def _wait(self, eng, deps):
        best = {}
        for (e, c) in deps:
            if e is eng and not eng.same_sync:
                continue
            if best.get(e, 0) < c:
                best[e] = c
        for e, c in best.items():
            if eng.seen.get(e, 0) >= c:
                continue
            eng.h.wait_ge(e.sem, c)
            self.n_instr += 1
            eng.seen[e] = c

    @staticmethod
    def _deps(reads, writes):
        deps = []
        for b in reads:
            if b.w is not None:
                deps.append(b.w)
        for b in writes:
            if b.w is not None:
                deps.append(b.w)
            deps.extend(b.r.items())
        return deps

    def op(self, eng, fn, reads=(), writes=(), est=None):
        if self.defer:
            import sys as _s
            self.pending.append(dict(kind="op", eng=eng, fn=fn, reads=list(reads), writes=list(writes), line=_s._getframe(1).f_lineno,
                                     est=est if est is not None else self.EST[eng.name]))
            return None
        self._wait(eng, self._deps(reads, writes))
        ins = fn()
        eng.count += 1
        ins.then_inc(eng.sem, 1)
        self.n_instr += 1
        me = (eng, eng.count)
        for b in reads:
            if b.r.get(eng, 0) < eng.count:
                b.r[eng] = eng.count
        for b in writes:
            b.w = me
            b.r = {}
        return ins

    def dma(self, q, out, in_, reads=(), writes=(), est=3.0, **kw):
        if self.defer:
            import sys as _s
            self.pending.append(dict(kind="dma", q=q, out=out, in_=in_, kw=kw, reads=list(reads), writes=list(writes), est=est, line=_s._getframe(1).f_lineno,
                                     eng={"sp": self.sp, "pool": self.pool, "act": self.act}[q]))
            return None
        eng = {"sp": self.sp, "pool": self.pool, "act": self.act}[q]
        sems = self.dma_sems[q]
        ds = sems[self.dma_rr[q] % len(sems)]
        self.dma_rr[q] += 1
        deps = self._deps(reads, writes)
        if ds.count > 0:
            deps.append((ds, ds.count))
        self._wait(eng, deps)
        ins = eng.h.dma_start(out=out, in_=in_, **kw)
        ds.count += 16
        ins.then_inc(ds.sem, 16)
        self.n_instr += 1
        me = (ds, ds.count)
        for b in reads:
            if b.r.get(ds, 0) < ds.count:
                b.r[ds] = ds.count
        for b in writes:
            b.w = me
            b.r = {}
        return me

    def flush(self):
        ops = self.pending
        self.pending = []
        if not ops:
            return
        n = len(ops)
        lastw = {}
        readers = {}
        preds = [None] * n
        ext = [None] * n
        for i, o in enumerate(ops):
            p = set()
            e = []
            for b in o["reads"]:
                if id(b) in lastw:
                    p.add(lastw[id(b)])
                elif b.w is not None:
                    e.append(b.w)
            for b in o["writes"]:
                if id(b) in lastw:
                    p.add(lastw[id(b)])
                else:
                    if b.w is not None:
                        e.append(b.w)
                    e.extend(b.r.items())
                p.update(readers.get(id(b), ()))
            p.discard(i)
            preds[i] = p
            ext[i] = e
            for b in o["reads"]:
                readers.setdefault(id(b), set()).add(i)
            for b in o["writes"]:
                lastw[id(b)] = i
                readers[id(b)] = set()
        succs = [[] for _ in range(n)]
        npred = [0] * n
        for i in range(n):
            npred[i] = len(preds[i])
            for p in preds[i]:
                succs[p].append(i)
        import heapq
        qname = lambda o: o["eng"].name
        ready = {}
        fin = [0.0] * n
        dready = [0.0] * n
        for i in range(n):
            if npred[i] == 0:
                heapq.heappush(ready.setdefault(qname(ops[i]), []), (0.0, i))
        etime = {}
        elast = {}
        why = [None] * n
        dwhy = [None] * n
        order = []
        remaining = n
        while remaining:
            best = None
            for en, hp in ready.items():
                if not hp:
                    continue
                dr, i = hp[0]
                st = max(dr, etime.get(en, 0.0))
                if best is None or (st, i) < (best[0], best[2]):
                    best = (st, en, i)
            st, en, i = best
            heapq.heappop(ready[en])
            o = ops[i]
            why[i] = dwhy[i] if dready[i] >= etime.get(en, 0.0) else elast.get(en)
            elast[en] = i
            issue = o["est"] if o["kind"] == "op" else 0.15
            etime[en] = st + issue
            fin[i] = st + o["est"]
            order.append(i)
            remaining -= 1
            for j in succs[i]:
                npred[j] -= 1
                lat = self.LAT_X if ops[j]["eng"] is not ops[i]["eng"] else (self.LAT_S if ops[i]["eng"].same_sync else 0.0)
                if fin[i] + lat > dready[j]:
                    dready[j] = fin[i] + lat
                    dwhy[j] = i
                if npred[j] == 0:
                    heapq.heappush(ready.setdefault(qname(ops[j]), []), (dready[j], j))
        if getattr(self, "verbose", False):
            busy = {}
            for i in range(n):
                busy[qname(ops[i])] = busy.get(qname(ops[i]), 0.0) + ops[i]["est"]
            print(f"[flush] ops={n} sim_makespan={max(fin):.1f}us busy=" + " ".join(f"{k}:{v:.0f}" for k, v in busy.items()))
        if getattr(self, "crit", False):
            i = max(range(n), key=lambda k: fin[k])
            path = []
            while i is not None:
                path.append(i)
                i = why[i]
            path.reverse()
            import collections
            agg = collections.Counter()
            for i in path:
                agg[(qname(ops[i]), ops[i]["line"])] += ops[i]["est"]
            print("critical path: %d ops" % len(path))
            for (en, ln), t in sorted(agg.items(), key=lambda kv: -kv[1])[:25]:
                print(f"   {en:5s} line {ln:4d}  {t:8.1f}us")
        if getattr(self, "sim_only", False):
            return
        tok = [None] * n
        for i in order:
            o = ops[i]
            eng = o["eng"]
            deps = list(ext[i]) + [tok[p] for p in preds[i]]
            if o["kind"] == "op":
                self._wait(eng, deps)
                ins = o["fn"]()
                eng.count += 1
                ins.then_inc(eng.sem, 1)
                self.n_instr += 1
                tok[i] = (eng, eng.count)
            else:
                sems = self.dma_sems[o["q"]]
                ds = sems[self.dma_rr[o["q"]] % len(sems)]
                self.dma_rr[o["q"]] += 1
                if ds.count > 0:
                    deps.append((ds, ds.count))
                self._wait(eng, deps)
                ins = eng.h.dma_start(out=o["out"], in_=o["in_"], **o["kw"])
                ds.count += 16
                ins.then_inc(ds.sem, 16)
                self.n_instr += 1
                tok[i] = (ds, ds.count)
        for i, o in enumerate(ops):
            pass
        last_touch = {}
        for i in range(n):
            o = ops[i]
            for b in o["reads"]:
                last_touch.setdefault(id(b), (b, [], []))
            for b in o["writes"]:
                last_touch.setdefault(id(b), (b, [], []))
        for bid, (b, _, _) in last_touch.items():
            if bid in lastw:
                b.w = tok[lastw[bid]]
                b.r = {}
            for ridx in readers.get(bid, ()):
                e_, c_ = tok[ridx]
                if b.r.get(e_, 0) < c_:
                    b.r[e_] = c_

    def wait_all(self, eng, bufs):
        self.flush()
        deps = []
        for b in bufs:
            if b.w is not None:
                deps.append(b.w)
            deps.extend(b.r.items())
        self._wait(eng, deps)

    def cc_dram(self, name, shape, dtype):
        return self.nc.dram_tensor(name, list(shape), dtype).ap()

    def allgather(self, src, dst, n_cores=8):
        self.barrier()
        ins = self.nc.gpsimd.collective_compute("AllGather", ALU.bypass, replica_groups=[list(range(n_cores))],
                                                ins=[src.opt()], outs=[dst.opt()])
        self.cc.count += 1
        ins.then_inc(self.cc.sem)
        self.n_instr += 1
        self.barrier()

    def barrier(self):
        self.flush()
        for X in self.engs:
            deps = [(Y, Y.count) for Y in self.engs if Y is not X and Y.count]
            deps += [(d, d.count) for q in self.dma_sems.values() for d in q if d.count]
            if self.cc.count:
                deps.append((self.cc, self.cc.count))
            self._wait(X, deps)

    @contextlib.contextmanager
    def scope(self):
        outer = self.es
        self.es = contextlib.ExitStack()
        try:
            yield
        finally:
            self.barrier()
            self.es.close()
            self.es = outer

    def finish(self):
        self.flush()
        self.es.close()
        return self.nc


EPS = 1e-6
D = 1024


class Stop(Exception):
    pass

LIMIT = [None]

def ck(n):
    if LIMIT[0] == n:
        raise Stop()


class T:
    def __init__(self, P, name, shape, dtype, psum=False):
        self.t = (P.psum if psum else P.sbuf)(name, shape, dtype)
        self.b = Buf(name)

    def __getitem__(self, k):
        return self.t[k]


def load_w_bf16(P, dst, w_dram, K, N, q="pool"):
    for kc in range(K // 128):
        for c0 in range(0, N, 2048):
            c1 = min(N, c0 + 2048)
            P.dma(q, dst[:, kc, c0:c1], w_dram[kc * 128:(kc + 1) * 128, c0:c1], writes=[dst.b])


def emit_attn(P, io, NS, pb, tag="a", NB=2):
    nc = P.nc
    n = lambda s: f"{tag}_{s}"
    Win = T(P, n("Win"), [128, 8, 2560], BF16)
    Wout = T(P, n("Wout"), [128, 8, 1024], BF16)
    load_w_bf16(P, Win, io["w_in"], 1024, 2560)
    load_w_bf16(P, Wout, io["w_out"], 1024, 1024)
    ng = T(P, n("ng"), [128, 1024], F32)
    P.dma("sp", ng[:], io["norm_gain"].partition_broadcast(128), writes=[ng.b])
    qkg = T(P, n("qkg"), [128, 20, 64], F32)
    gtmp = T(P, n("gtmp"), [128, 2, 64], F32)
    P.dma("sp", gtmp[:, 0, :], io["q_gain"].partition_broadcast(128), writes=[gtmp.b])
    P.dma("sp", gtmp[:, 1, :], io["k_gain"].partition_broadcast(128), writes=[gtmp.b])
    P.op(P.dve, lambda: nc.vector.tensor_copy(out=qkg[:, 0:16, :], in_=gtmp[:, 0:1, :].broadcast_to([128, 16, 64])),
         reads=[gtmp.b], writes=[qkg.b])
    P.op(P.dve, lambda: nc.vector.tensor_copy(out=qkg[:, 16:20, :], in_=gtmp[:, 1:2, :].broadcast_to([128, 4, 64])),
         reads=[gtmp.b], writes=[qkg.b])
    esink = T(P, n("esink"), [128, 16], F32)
    P.dma("sp", esink[:], io["sinks"].partition_broadcast(128), writes=[esink.b])
    P.op(P.act, lambda: nc.scalar.activation(out=esink[:], in_=esink[:], func=AF.Exp), reads=[esink.b], writes=[esink.b])
    ident = T(P, n("ident"), [128, 128], BF16)
    P.dma("sp", ident[:], io["ident"], writes=[ident.b])
    m01 = T(P, n("m01"), [128, 2, 128], BF16)
    P.dma("sp", m01[:, 0, :], io["mask_cur"], writes=[m01.b])
    P.dma("sp", m01[:, 1, :], io["mask_prev"], writes=[m01.b])
    mcur = T(P, n("mcur"), [128, 4, 128], BF16)
    mprev = T(P, n("mprev"), [128, 4, 128], BF16)
    for mi, mt in enumerate((mcur, mprev)):
        P.op(P.dve, lambda mi=mi, mt=mt: nc.vector.tensor_scalar(out=mt[:], in0=m01[:, mi:mi + 1, :].broadcast_to([128, 4, 128]),
                                                                 scalar1=30000.0, scalar2=-30000.0, op0=ALU.mult, op1=ALU.add),
             reads=[m01.b], writes=[mt.b])
    kval = T(P, n("kval"), [128, NS], F32)
    P.dma("sp", kval[:], io["kvalid"], writes=[kval.b])
    cos2 = T(P, n("cos2"), [128, NS, 64], F32)
    sinS = T(P, n("sinS"), [128, NS, 64], F32)
    P.dma("sp", cos2[:], io["cos2"].rearrange("(s p) d -> p s d", p=128), writes=[cos2.b])
    P.dma("sp", sinS[:], io["sinS"].rearrange("(s p) d -> p s d", p=128), writes=[sinS.b])

    ck(1)
    hblk = [T(P, n(f"hblk{i}"), [128, 1024], F32) for i in range(2)]
    hnew = [T(P, n(f"hnew{i}"), [128, 1024], F32) for i in range(2)]
    st_l = [T(P, n(f"st{i}"), [128, 4], F32) for i in range(NB)]
    u_l = [T(P, n(f"u{i}"), [128, 1024], BF16) for i in range(NB)]
    uT = [T(P, n(f"uT{i}"), [128, 8, 128], BF16) for i in range(2)]
    qk_l = [T(P, n(f"qk{i}"), [128, 20, 64], F32) for i in range(NB)]
    qss_l = [T(P, n(f"qss{i}"), [128, 20], F32) for i in range(NB)]
    qrs_l = [T(P, n(f"qrs{i}"), [128, 20], F32) for i in range(NB)]
    qg_l = [T(P, n(f"qg{i}"), [128, 20, 64], F32) for i in range(NB)]
    ra_l = [T(P, n(f"ra{i}"), [128, 20, 64], F32) for i in range(NB)]
    rb_l = [T(P, n(f"rb{i}"), [128, 20, 64], F32) for i in range(1)]
    qkr_l = [T(P, n(f"qkr{i}"), [128, 20, 64], BF16) for i in range(NB)]
    sz = [T(P, n(f"sz{i}"), [128, 1024], BF16) for i in range(2)]
    Vaug = [T(P, n(f"Vaug{i}"), [128, 4, 66], BF16) for i in range(3)]
    qT = [T(P, n(f"qT{i}"), [64, 16, 128], BF16) for i in range(2)]
    kT = [T(P, n(f"kT{i}"), [64, 4, 128], BF16) for i in range(3)]
    E_l = [[T(P, n(f"E{i}_{j}"), [128, 4, 128], BF16) for i in range(8)] for j in range(1)]
    EM_l = [[T(P, n(f"EM{i}_{j}"), [128, 4, 128], BF16) for i in range(8)] for j in range(1)]
    Osb_l = [T(P, n(f"Osb{i}"), [128, 16, 65], F32) for i in range(1)]
    den_l = [T(P, n(f"den{i}"), [128, 16], F32) for i in range(NB)]
    rden_l = [T(P, n(f"rden{i}"), [128, 16], F32) for i in range(NB)]
    g1_l = [T(P, n(f"g1{i}"), [128, 16, 64], F32) for i in range(1)]
    g_l = [T(P, n(f"g{i}"), [128, 1024], BF16) for i in range(NB)]
    gT_l = [T(P, n(f"gT{i}"), [128, 8, 128], BF16) for i in range(NB)]

    tp = pb[0]
    proj = [pb[1], pb[2]]
    sc = [pb[3], pb[4]]
    ob = [pb[5], pb[6], pb[7]]
    OH = [(0, 6), (6, 12), (12, 16)]
    h_in, h_out = io["h_in"], io["h_out"]
    cnt = {"proj": 0, "sc": 0}

    def stageA(s):
        st = st_l[s % NB]; u = u_l[s % NB]; qk = qk_l[s % NB]; qss = qss_l[s % NB]; qrs = qrs_l[s % NB]; qg = qg_l[s % NB]; ra = ra_l[s % NB]; rb = rb_l[0]; qkr = qkr_l[s % NB]; Osb = Osb_l[0]; den = den_l[s % NB]; rden = rden_l[s % NB]; g1 = g1_l[0]; g = g_l[s % NB]; gT = gT_l[s % NB]
        E = E_l[0]; EM = EM_l[0]; junk = ra
        hb = hblk[s % 2]
        P.dma("sp", hb[:], h_in[s * 128:(s + 1) * 128, :], writes=[hb.b])
        P.op(P.act, lambda: nc.scalar.activation(out=junk[:].rearrange("p h d -> p (h d)")[:, 0:1024], in_=hb[:], func=AF.Square, accum_out=st[:, 0:1]),
             reads=[hb.b], writes=[junk.b, st.b])
        P.op(P.act, lambda: nc.scalar.activation(out=st[:, 1:2], in_=st[:, 0:1], func=AF.Sqrt, scale=1.0 / D, bias=EPS),
             reads=[st.b], writes=[st.b])
        P.op(P.dve, lambda: nc.vector.reciprocal(out=st[:, 2:3], in_=st[:, 1:2]), reads=[st.b], writes=[st.b])
        P.op(P.dve, lambda: nc.vector.scalar_tensor_tensor(out=u[:], in0=hb[:], scalar=st[:, 2:3], in1=ng[:],
                                                           op0=ALU.mult, op1=ALU.mult),
             reads=[hb.b, st.b, ng.b], writes=[u.b])
        tpb = tp[:].bitcast(BF16)
        for k in range(8):
            P.op(P.pe, lambda k=k: nc.tensor.transpose(tpb[:, k * 128:(k + 1) * 128], u[:, k * 128:(k + 1) * 128], ident[:]),
                 reads=[u.b, ident.b], writes=[tp.b])
        ut = uT[s % 2]
        P.op(P.act, lambda: nc.scalar.activation(out=ut[:].rearrange("p k t -> p (k t)"), in_=tpb[:, 0:1024], func=AF.Copy),
             reads=[tp.b], writes=[ut.b])
        ck(2)
        va = Vaug[s % 3]
        for j in range(5):
            bank = proj[cnt["proj"] % 2]
            cnt["proj"] += 1
            for k in range(8):
                P.op(P.pe, lambda k=k, j=j, bank=bank: nc.tensor.matmul(bank[:], lhsT=ut[:, k, :], rhs=Win[:, k, j * 512:(j + 1) * 512],
                                                                      start=(k == 0), stop=(k == 7)),
                     reads=[ut.b, Win.b], writes=[bank.b])
            ck(20 + 2 * j)
            qkf = qk[:].rearrange("p h d -> p (h d)")
            if j < 2:
                P.op(P.act, lambda j=j, bank=bank: nc.scalar.activation(out=qkf[:, j * 512:(j + 1) * 512], in_=bank[:], func=AF.Copy),
                     reads=[bank.b], writes=[qk.b])
            elif j == 2:
                P.op(P.act, lambda bank=bank: nc.scalar.activation(out=qkf[:, 1024:1280], in_=bank[:, 0:256], func=AF.Copy),
                     reads=[bank.b], writes=[qk.b])
                P.op(P.act, lambda bank=bank: nc.scalar.activation(out=va[:, :, 0:64], in_=bank[:, 256:512].rearrange("p (h d) -> p h d", h=4),
                                                                   func=AF.Copy, scale=kval[:, s:s + 1]),
                     reads=[bank.b, kval.b], writes=[va.b])
                P.op(P.dve, lambda: nc.vector.tensor_copy(out=va[:, :, 64:65], in_=kval[:, s:s + 1].unsqueeze(1).broadcast_to([128, 4, 1])),
                     reads=[kval.b], writes=[va.b])
            else:
                z = sz[s % 2]
                P.op(P.act, lambda j=j, bank=bank, z=z: nc.scalar.activation(out=z[:, (j - 3) * 512:(j - 2) * 512], in_=bank[:], func=AF.Silu),
                     reads=[bank.b], writes=[z.b])
            ck(21 + 2 * j)
        ck(3)
        P.op(P.dve, lambda: nc.vector.tensor_tensor(out=junk[:], in0=qk[:], in1=qk[:], op=ALU.mult),
             reads=[qk.b], writes=[junk.b])
        P.op(P.dve, lambda: nc.vector.tensor_reduce(out=qss[:], in_=junk[:], axis=AX.X, op=ALU.add),
             reads=[junk.b], writes=[qss.b])
        P.op(P.act, lambda: nc.scalar.activation(out=qss[:], in_=qss[:], func=AF.Sqrt, scale=1.0 / 64, bias=EPS),
             reads=[qss.b], writes=[qss.b])
        P.op(P.dve, lambda: nc.vector.reciprocal(out=qrs[:], in_=qss[:]), reads=[qss.b], writes=[qrs.b])
        P.op(P.dve, lambda: nc.vector.tensor_tensor(out=qg[:], in0=qk[:], in1=qrs[:].unsqueeze(2).broadcast_to([128, 20, 64]), op=ALU.mult),
             reads=[qk.b, qrs.b], writes=[qg.b])
        P.op(P.pool, lambda: nc.gpsimd.tensor_tensor(out=qg[:], in0=qg[:], in1=qkg[:], op=ALU.mult),
             reads=[qg.b, qkg.b], writes=[qg.b])
        P.op(P.dve, lambda: nc.vector.tensor_tensor(out=ra[:], in0=qg[:], in1=cos2[:, s:s + 1, :].broadcast_to([128, 20, 64]), op=ALU.mult),
             reads=[qg.b, cos2.b], writes=[ra.b])
        P.op(P.pool, lambda: nc.gpsimd.tensor_tensor(out=rb[:, :, 0:32], in0=qg[:, :, 32:64],
                                                      in1=sinS[:, s:s + 1, 0:32].broadcast_to([128, 20, 32]), op=ALU.mult),
             reads=[qg.b, sinS.b], writes=[rb.b])
        P.op(P.pool, lambda: nc.gpsimd.tensor_tensor(out=rb[:, :, 32:64], in0=qg[:, :, 0:32],
                                                      in1=sinS[:, s:s + 1, 32:64].broadcast_to([128, 20, 32]), op=ALU.mult),
             reads=[qg.b, sinS.b], writes=[rb.b])
        P.op(P.dve, lambda: nc.vector.tensor_tensor(out=qkr[:], in0=ra[:], in1=rb[:], op=ALU.add),
             reads=[ra.b, rb.b], writes=[qkr.b])
        ck(4)
        for hh in range(20):
            bank = ob[hh // 8]
            bb = bank[:].bitcast(BF16)
            P.op(P.pe, lambda hh=hh, bb=bb: nc.tensor.transpose(bb[0:64, (hh % 8) * 128:(hh % 8 + 1) * 128], qkr[:, hh, :], ident[:]),
                 reads=[qkr.b, ident.b], writes=[bank.b])
        qt = qT[s % 2]
        kt = kT[s % 3]
        P.op(P.act, lambda: nc.scalar.activation(out=qt[:, 0:8, :].rearrange("p h t -> p (h t)"), in_=ob[0][:].bitcast(BF16)[0:64, 0:1024], func=AF.Copy),
             reads=[ob[0].b], writes=[qt.b])
        P.op(P.dve, lambda: nc.vector.tensor_copy(out=qt[:, 8:16, :].rearrange("p h t -> p (h t)"), in_=ob[1][:].bitcast(BF16)[0:64, 0:1024]),
             reads=[ob[1].b], writes=[qt.b])
        P.op(P.act, lambda: nc.scalar.activation(out=kt[:].rearrange("p h t -> p (h t)"), in_=ob[2][:].bitcast(BF16)[0:64, 0:512], func=AF.Copy),
             reads=[ob[2].b], writes=[kt.b])

    def stageB(s):
        st = st_l[s % NB]; u = u_l[s % NB]; qk = qk_l[s % NB]; qss = qss_l[s % NB]; qrs = qrs_l[s % NB]; qg = qg_l[s % NB]; ra = ra_l[s % NB]; rb = rb_l[0]; qkr = qkr_l[s % NB]; Osb = Osb_l[0]; den = den_l[s % NB]; rden = rden_l[s % NB]; g1 = g1_l[0]; g = g_l[s % NB]; gT = gT_l[s % NB]
        E = E_l[0]; EM = EM_l[0]; junk = ra
        ck(5)
        hb = hblk[s % 2]
        qt = qT[s % 2]
        kbs = ([] if s == 0 else [(kT[(s - 1) % 3], Vaug[(s - 1) % 3], mprev)]) + [(kT[s % 3], Vaug[s % 3], mcur)]
        nkb = len(kbs)
        for j in range(4):
            for ki, (kt, va, mk) in enumerate(kbs):
                idx = j * 2 + ki
                bank = sc[cnt["sc"] % 2]
                cnt["sc"] += 1
                P.op(P.pe, lambda bank=bank, kt=kt, j=j: nc.tensor.matmul(bank[:], lhsT=kt[:, j, :], rhs=qt[:, 4 * j:4 * j + 4, :].rearrange("p h t -> p (h t)"),
                                                                       start=True, stop=False),
                     reads=[kt.b, qt.b], writes=[bank.b])
                P.op(P.pe, lambda bank=bank, mk=mk: nc.tensor.matmul(bank[:], lhsT=ident[:], rhs=mk[:].rearrange("p h t -> p (h t)"), start=False, stop=True),
                     reads=[ident.b, mk.b], writes=[bank.b])
                P.op(P.act, lambda bank=bank, idx=idx: nc.scalar.activation(out=EM[idx][:].rearrange("p h t -> p (h t)"), in_=bank[:], func=AF.Exp, scale=0.125),
                     reads=[bank.b], writes=[EM[idx].b])
        ck(6)
        for h in range(16):
            j, gq = h // 4, h % 4
            bi = 0 if h < 6 else (1 if h < 12 else 2)
            bank = ob[bi]
            off = (h - OH[bi][0]) * 65
            for ki, (kt, va, mk) in enumerate(kbs):
                idx = j * 2 + ki
                P.op(P.pe, lambda bank=bank, off=off, idx=idx, gq=gq, va=va, j=j, ki=ki: nc.tensor.matmul(
                    bank[:, off:off + 65], lhsT=EM[idx][:, gq, :], rhs=va[:, j, 0:65], start=(ki == 0), stop=(ki == nkb - 1)),
                    reads=[EM[idx].b, va.b], writes=[bank.b])
        ck(7)
        for bi, (h0, h1) in enumerate(OH):
            nh = h1 - h0
            P.op(P.act, lambda bi=bi, h0=h0, h1=h1, nh=nh: nc.scalar.activation(out=Osb[:, h0:h1, :].rearrange("p h d -> p (h d)"), in_=ob[bi][:, 0:nh * 65], func=AF.Copy),
                 reads=[ob[bi].b], writes=[Osb.b])
        P.op(P.dve, lambda: nc.vector.tensor_tensor(out=den[:].unsqueeze(2), in0=Osb[:, :, 64:65], in1=esink[:].unsqueeze(2), op=ALU.add),
             reads=[Osb.b, esink.b], writes=[den.b])
        P.op(P.dve, lambda: nc.vector.reciprocal(out=rden[:], in_=den[:]), reads=[den.b], writes=[rden.b])
        P.op(P.dve, lambda: nc.vector.tensor_tensor(out=g1[:], in0=Osb[:, :, 0:64], in1=rden[:].unsqueeze(2).broadcast_to([128, 16, 64]), op=ALU.mult),
             reads=[Osb.b, rden.b], writes=[g1.b])
        z = sz[s % 2]
        P.op(P.pool, lambda: nc.gpsimd.tensor_tensor(out=g[:], in0=g1[:].rearrange("p h d -> p (h d)"), in1=z[:], op=ALU.mult),
             reads=[g1.b, z.b], writes=[g.b])
        ck(8)
        tpb = tp[:].bitcast(BF16)
        for k in range(8):
            P.op(P.pe, lambda k=k: nc.tensor.transpose(tpb[:, k * 128:(k + 1) * 128], g[:, k * 128:(k + 1) * 128], ident[:]),
                 reads=[g.b, ident.b], writes=[tp.b])
        P.op(P.act, lambda: nc.scalar.activation(out=gT[:].rearrange("p k t -> p (k t)"), in_=tpb[:, 0:1024], func=AF.Copy),
             reads=[tp.b], writes=[gT.b])
        hn = hnew[s % 2]
        for c in range(2):
            bank = proj[cnt["proj"] % 2]
            cnt["proj"] += 1
            for k in range(8):
                P.op(P.pe, lambda k=k, c=c, bank=bank: nc.tensor.matmul(bank[:], lhsT=gT[:, k, :], rhs=Wout[:, k, c * 512:(c + 1) * 512],
                                                                      start=(k == 0), stop=(k == 7)),
                     reads=[gT.b, Wout.b], writes=[bank.b])
            P.op(P.dve, lambda c=c, bank=bank: nc.vector.scalar_tensor_tensor(out=hn[:, c * 512:(c + 1) * 512], in0=bank[:], scalar=kval[:, s:s + 1],
                                                                              in1=hb[:, c * 512:(c + 1) * 512], op0=ALU.mult, op1=ALU.add),
                 reads=[bank.b, kval.b, hb.b], writes=[hn.b])
        P.dma("sp", h_out(s) if callable(h_out) else h_out[s * 128:(s + 1) * 128, :], hn[:], reads=[hn.b])

    for s in range(NS + 1):
        if s < NS:
            stageA(s)
        if s >= 1:
            stageB(s - 1)
    return hnew


def attn_consts(NS, q, ml_dtypes):
    ident = np.eye(128, dtype=np.float32).astype(ml_dtypes.bfloat16)
    sidx = np.arange(128)[:, None]
    tidx = np.arange(128)[None, :]
    mask_cur = (sidx <= tidx).astype(np.float32).astype(ml_dtypes.bfloat16)
    mask_prev = (sidx > tidx).astype(np.float32).astype(ml_dtypes.bfloat16)
    half = 32
    inv = (np.float32(10000.0) ** (-np.arange(half, dtype=np.float32) / np.float32(half))).astype(np.float32)
    gtok = (32 * q * 128 + np.arange(NS * 128) - 112).astype(np.float32)
    ang = (gtok[:, None] * inv[None, :]).astype(np.float32)
    cos = np.cos(ang).astype(np.float32)
    sin = np.sin(ang).astype(np.float32)
    cos2 = np.concatenate([cos, cos], 1)
    sinS = np.concatenate([-sin, sin], 1)
    kvalid = np.ones((128, NS), np.float32)
    if q == 0:
        kvalid[:112, 0] = 0.0
    return dict(ident=ident, mask_cur=mask_cur, mask_prev=mask_prev, cos2=cos2, sinS=sinS, kvalid=kvalid)


def emit_hgrn(P, io, NS, pb, layer_idx=2, tag="c", state_only=False):
    nc = P.nc
    n = lambda s: f"{tag}_{s}"
    Win = T(P, n("Win"), [128, 8, 4096], BF16)
    Wout = T(P, n("Wout"), [128, 8, 1024], BF16)
    load_w_bf16(P, Win, io["w_in"], 1024, 4096)
    load_w_bf16(P, Wout, io["w_out"], 1024, 1024)
    ng = T(P, n("ng"), [128, 1024], F32)
    P.dma("sp", ng[:], io["norm_gain"].partition_broadcast(128), writes=[ng.b])
    og = T(P, n("og"), [128, 1024], F32)
    P.dma("sp", og[:], io["o_gain"].partition_broadcast(128), writes=[og.b])
    ident = T(P, n("ident"), [128, 128], BF16)
    P.dma("sp", ident[:], io["ident"], writes=[ident.b])
    bdm = T(P, n("bdm"), [128, 128], BF16)
    P.dma("sp", bdm[:], io["bdmask"], writes=[bdm.b])
    rst = T(P, n("rst"), [128, 1024], F32)
    P.dma("sp", rst[:], io["rst"], writes=[rst.b])
    rvrow = T(P, n("rvrow"), [128, 128], F32)
    P.dma("sp", rvrow[:], io["rv_row"], writes=[rvrow.b])
    rvcol = T(P, n("rvcol"), [128, NS], F32)
    P.dma("sp", rvcol[:], io["rv_col"], writes=[rvcol.b])
    gam = T(P, n("gam"), [128, 8, 4], F32)
    for l in range(4):
        P.dma("sp", gam[:, :, l], io["gamma"][l].rearrange("(h d) -> d h", d=128), writes=[gam.b], allow_slow_non_contiguous=True)
    P.op(P.act, lambda: nc.scalar.activation(out=gam[:], in_=gam[:], func=AF.Exp), reads=[gam.b], writes=[gam.b])
    lbt = T(P, n("lbt"), [128, 8, 4], F32)
    P.op(P.dve, lambda: nc.vector.tensor_reduce(out=lbt[:, :, 0], in_=gam[:], axis=AX.X, op=ALU.add), reads=[gam.b], writes=[lbt.b])
    P.op(P.dve, lambda: nc.vector.tensor_reduce(out=lbt[:, :, 1], in_=gam[:, :, 0:layer_idx], axis=AX.X, op=ALU.add), reads=[gam.b], writes=[lbt.b])
    P.op(P.dve, lambda: nc.vector.reciprocal(out=lbt[:, :, 0], in_=lbt[:, :, 0]), reads=[lbt.b], writes=[lbt.b])
    P.op(P.dve, lambda: nc.vector.tensor_tensor(out=lbt[:, :, 2], in0=lbt[:, :, 1], in1=lbt[:, :, 0], op=ALU.mult), reads=[lbt.b], writes=[lbt.b])
    P.op(P.dve, lambda: nc.vector.tensor_scalar(out=lbt[:, :, 3], in0=lbt[:, :, 2], scalar1=-1.0, scalar2=1.0, op0=ALU.mult, op1=ALU.add),
         reads=[lbt.b], writes=[lbt.b])
    S = T(P, n("S"), [128, 8, 128], F32)
    Sb = [Buf(n(f"S{h}")) for h in range(8)]
    P.dma("sp", S[:], io["state_in"], writes=[S.b] + Sb)
    SbfA = [T(P, n(f"SbfA{h}"), [128, 128], BF16) for h in range(8)]
    SbfB = [T(P, n(f"SbfB{h}"), [128, 128], BF16) for h in range(8)]
    for h in range(0 if not state_only else 8, 8):
        P.op(P.act, lambda h=h: nc.scalar.activation(out=SbfA[h][:], in_=S[:, h, :], func=AF.Copy), reads=[Sb[h]], writes=[SbfA[h].b])

    atot = T(P, n("atot"), [128, 8], F32)
    atmp = T(P, n("atmp"), [128, 8], F32)
    if state_only:
        P.op(P.dve, lambda: nc.vector.memset(atot[:], 0.0), writes=[atot.b])
    hblk = [T(P, n(f"hblk{i}"), [128, 1024], F32) for i in range(2)]
    hnew = [T(P, n(f"hnew{i}"), [128, 1024], F32) for i in range(2)]
    st = T(P, n("st"), [128, 4], F32)
    u = T(P, n("u"), [128, 1024], BF16)
    uT = [T(P, n(f"uT{i}"), [128, 8, 128], BF16) for i in range(2)]
    qs = T(P, n("qs"), [128, 8, 128], F32)
    sg = T(P, n("sg"), [128, 8, 128], F32)
    f = sg
    lf = T(P, n("lf"), [128, 8, 128], F32)
    kk = T(P, n("kk"), [128, 8, 128], F32)
    A = T(P, n("A"), [128, 8, 128], F32)
    eA = T(P, n("eA"), [128, 8, 128], F32)
    enA = T(P, n("enA"), [128, 8, 128], F32)
    eAl = T(P, n("eAl"), [128, 8, 2], F32)
    kef = T(P, n("kef"), [128, 8, 128], F32)
    qe = T(P, n("qe"), [128, 8, 128], BF16)
    ke = T(P, n("ke"), [128, 8, 128], BF16)
    kd = T(P, n("kd"), [128, 8, 128], BF16)
    kdT = T(P, n("kdT"), [128, 8, 128], BF16)
    V = T(P, n("V"), [128, 8, 128], BF16)
    sz = T(P, n("sz"), [128, 1024], BF16)
    attM = T(P, n("attM"), [128, 8, 128], BF16)
    osb = T(P, n("osb"), [128, 8, 128], F32)
    oss = T(P, n("oss"), [128, 8], F32)
    ors = T(P, n("ors"), [128, 8], F32)
    g1 = T(P, n("g1"), [128, 8, 128], F32)
    junk = T.__new__(T); junk.t = g1.t; junk.b = g1.b
    g = T(P, n("g"), [128, 1024], BF16)
    gT = T(P, n("gT"), [128, 8, 128], BF16)

    tp = pb[0]
    proj = [pb[1], pb[2]]
    attb = [pb[3], pb[4]]
    obk = [pb[5], pb[6]]
    h_in, h_out = io["h_in"], io["h_out"]
    cnt = {"proj": 0}
    fl = lambda t: t[:].rearrange("p h t -> p (h t)")

    def slot(s):
        hb = hblk[s % 2]
        P.dma("sp", hb[:], h_in[s * 128:(s + 1) * 128, :], writes=[hb.b])
        P.op(P.act, lambda: nc.scalar.activation(out=fl(junk), in_=hb[:], func=AF.Square, accum_out=st[:, 0:1]),
             reads=[hb.b], writes=[junk.b, st.b])
        P.op(P.act, lambda: nc.scalar.activation(out=st[:, 1:2], in_=st[:, 0:1], func=AF.Sqrt, scale=1.0 / D, bias=EPS),
             reads=[st.b], writes=[st.b])
        P.op(P.dve, lambda: nc.vector.reciprocal(out=st[:, 2:3], in_=st[:, 1:2]), reads=[st.b], writes=[st.b])
        P.op(P.dve, lambda: nc.vector.scalar_tensor_tensor(out=u[:], in0=hb[:], scalar=st[:, 2:3], in1=ng[:], op0=ALU.mult, op1=ALU.mult),
             reads=[hb.b, st.b, ng.b], writes=[u.b])
        tpb = tp[:].bitcast(BF16)
        for k in range(8):
            P.op(P.pe, lambda k=k: nc.tensor.transpose(tpb[:, k * 128:(k + 1) * 128], u[:, k * 128:(k + 1) * 128], ident[:]),
                 reads=[u.b, ident.b], writes=[tp.b])
        ut = uT[s % 2]
        P.op(P.act, lambda: nc.scalar.activation(out=fl(ut), in_=tpb[:, 0:1024], func=AF.Copy), reads=[tp.b], writes=[ut.b])
        for grp in range(2 if state_only else 0, 4):
            bank = proj[cnt["proj"] % 2]
            cnt["proj"] += 1
            for i in range(4):
                ti = grp * 4 + i
                typ, h = ti // 8, ti % 8
                col0 = (0 if typ == 0 else 1024) + h * 128
                for k in range(8):
                    P.op(P.pe, lambda k=k, i=i, col0=col0, bank=bank: nc.tensor.matmul(bank[:, i * 128:(i + 1) * 128], lhsT=Win[:, k, col0:col0 + 128],
                                                                                   rhs=ut[:, k, :], start=(k == 0), stop=(k == 7)),
                         reads=[ut.b, Win.b], writes=[bank.b])
            dst = qs if grp < 2 else sg
            fn = AF.Silu if grp < 2 else AF.Sigmoid
            h0 = (grp % 2) * 4
            P.op(P.act, lambda dst=dst, fn=fn, h0=h0, bank=bank: nc.scalar.activation(out=dst[:, h0:h0 + 4, :].rearrange("p h t -> p (h t)"), in_=bank[:], func=fn),
                 reads=[bank.b], writes=[dst.b])
        for c in range(2 if state_only else 4):
            bank = proj[cnt["proj"] % 2]
            cnt["proj"] += 1
            col0 = 2048 + c * 512
            for k in range(8):
                P.op(P.pe, lambda k=k, col0=col0, bank=bank: nc.tensor.matmul(bank[:], lhsT=ut[:, k, :], rhs=Win[:, k, col0:col0 + 512],
                                                                          start=(k == 0), stop=(k == 7)),
                     reads=[ut.b, Win.b], writes=[bank.b])
            if c < 2:
                P.op(P.act, lambda c=c, bank=bank: nc.scalar.activation(out=V[:, c * 4:c * 4 + 4, :].rearrange("p h t -> p (h t)"), in_=bank[:], func=AF.Copy),
                     reads=[bank.b], writes=[V.b])
            else:
                P.op(P.act, lambda c=c, bank=bank: nc.scalar.activation(out=sz[:, (c - 2) * 512:(c - 1) * 512], in_=bank[:], func=AF.Silu),
                     reads=[bank.b], writes=[sz.b])
        P.op(P.dve, lambda: nc.vector.tensor_tensor(out=f[:], in0=sg[:], in1=lbt[:, :, 3:4].broadcast_to([128, 8, 128]), op=ALU.mult),
             reads=[sg.b, lbt.b], writes=[f.b])
        P.op(P.dve, lambda: nc.vector.tensor_tensor(out=f[:], in0=f[:], in1=lbt[:, :, 2:3].broadcast_to([128, 8, 128]), op=ALU.add),
             reads=[f.b, lbt.b], writes=[f.b])
        P.op(P.act, lambda: nc.scalar.activation(out=fl(lf), in_=fl(f), func=AF.Ln), reads=[f.b], writes=[lf.b])
        P.op(P.dve, lambda: nc.vector.tensor_scalar(out=fl(kk), in0=fl(f), scalar1=-1.0, scalar2=1.0, op0=ALU.mult, op1=ALU.add),
             reads=[f.b], writes=[kk.b])
        if s == 0:
            rb = rvrow[:].unsqueeze(1).broadcast_to([128, 8, 128])
            P.op(P.dve, lambda: nc.vector.tensor_tensor(out=lf[:], in0=lf[:], in1=rb, op=ALU.mult), reads=[lf.b, rvrow.b], writes=[lf.b])
            P.op(P.dve, lambda: nc.vector.tensor_tensor(out=kk[:], in0=kk[:], in1=rb, op=ALU.mult), reads=[kk.b, rvrow.b], writes=[kk.b])
        P.op(P.dve, lambda: nc.vector.tensor_tensor_scan(out=fl(A), data0=rst[:], data1=fl(lf), initial=0.0, op0=ALU.mult, op1=ALU.add),
             reads=[rst.b, lf.b], writes=[A.b])
        P.op(P.act, lambda: nc.scalar.activation(out=fl(eA), in_=fl(A), func=AF.Exp), reads=[A.b], writes=[eA.b])
        P.op(P.act, lambda: nc.scalar.activation(out=fl(enA), in_=fl(A), func=AF.Exp, scale=-1.0), reads=[A.b], writes=[enA.b])
        Av = A[:].rearrange("p h (c t) -> p h c t", c=2)
        P.op(P.act, lambda: nc.scalar.activation(out=eAl[:].unsqueeze(3), in_=Av[:, :, :, 63:64], func=AF.Exp), reads=[A.b], writes=[eAl.b])
        if state_only:
            P.op(P.dve, lambda: nc.vector.tensor_tensor(out=atmp[:].unsqueeze(2), in0=Av[:, :, 0, 63:64], in1=Av[:, :, 1, 63:64], op=ALU.add),
                 reads=[A.b], writes=[atmp.b])
            P.op(P.dve, lambda: nc.vector.tensor_tensor(out=atot[:], in0=atot[:], in1=atmp[:], op=ALU.add), reads=[atot.b, atmp.b], writes=[atot.b])
        if not state_only:
            P.op(P.pool, lambda: nc.gpsimd.tensor_tensor(out=qe[:], in0=qs[:], in1=eA[:], op=ALU.mult), reads=[qs.b, eA.b], writes=[qe.b])
        P.op(P.dve, lambda: nc.vector.tensor_tensor(out=kef[:], in0=kk[:], in1=enA[:], op=ALU.mult), reads=[kk.b, enA.b], writes=[kef.b])
        P.op(P.pool, lambda: nc.gpsimd.tensor_copy(out=ke[:], in_=kef[:]), reads=[kef.b], writes=[ke.b])
        P.op(P.dve, lambda: nc.vector.tensor_tensor(out=kd[:].rearrange("p h (c t) -> p h c t", c=2), in0=kef[:].rearrange("p h (c t) -> p h c t", c=2),
                                                    in1=eAl[:].unsqueeze(3).broadcast_to([128, 8, 2, 64]), op=ALU.mult),
             reads=[kef.b, eAl.b], writes=[kd.b])
        for h in range(8):
            P.op(P.pe, lambda h=h: nc.tensor.transpose(tpb[:, h * 128:(h + 1) * 128], kd[:, h, :], ident[:]),
                 reads=[kd.b, ident.b], writes=[tp.b])
        P.op(P.act, lambda: nc.scalar.activation(out=fl(kdT), in_=tpb[:, 0:1024], func=AF.Copy), reads=[tp.b], writes=[kdT.b])
        for h in range(8):
            bank = proj[h // 4]
            P.op(P.pe, lambda h=h, bank=bank: nc.tensor.matmul(bank[:, (h % 4) * 128:(h % 4 + 1) * 128], lhsT=kdT[0:64, h, :], rhs=V[0:64, h, :], start=True, stop=True),
                 reads=[kdT.b, V.b], writes=[bank.b])
        for h in range(8):
            bank = proj[h // 4]
            P.op(P.dve, lambda h=h, bank=bank: nc.vector.scalar_tensor_tensor(out=S[:, h, :], in0=S[:, h, :], scalar=eAl[:, h, 0:1],
                                                                              in1=bank[:, (h % 4) * 128:(h % 4 + 1) * 128], op0=ALU.mult, op1=ALU.add),
                 reads=[Sb[h], eAl.b, bank.b], writes=[Sb[h]])
            if not state_only:
                P.op(P.act, lambda h=h: nc.scalar.activation(out=SbfB[h][:], in_=S[:, h, :], func=AF.Copy), reads=[Sb[h]], writes=[SbfB[h].b])
        if not state_only:
            for h in range(8):
                bank = attb[h // 4]
                P.op(P.pe, lambda h=h, bank=bank: nc.tensor.matmul(bank[:, (h % 4) * 128:(h % 4 + 1) * 128], lhsT=ke[:, h, :], rhs=qe[:, h, :], start=True, stop=True),
                     reads=[ke.b, qe.b], writes=[bank.b])
            for i in range(2):
                P.op(P.dve, lambda i=i: nc.vector.tensor_tensor(out=attM[:, i * 4:i * 4 + 4, :], in0=attb[i][:].rearrange("p (h t) -> p h t", h=4),
                                                                in1=bdm[:].unsqueeze(1).broadcast_to([128, 4, 128]), op=ALU.mult),
                     reads=[attb[i].b, bdm.b], writes=[attM.b])
            for h in range(8):
                bank = obk[h // 4]
                reg = slice((h % 4) * 128, (h % 4 + 1) * 128)
                P.op(P.pe, lambda h=h, bank=bank, reg=reg: nc.tensor.matmul(bank[:, reg], lhsT=attM[:, h, :], rhs=V[:, h, :], start=True, stop=False),
                     reads=[attM.b, V.b], writes=[bank.b])
                P.op(P.pe, lambda h=h, bank=bank, reg=reg: nc.tensor.matmul(bank[0:64, reg], lhsT=qe[:, h, 0:64], rhs=SbfA[h][:], start=False, stop=True),
                     reads=[qe.b, SbfA[h].b], writes=[bank.b])
                P.op(P.pe, lambda h=h, bank=bank, reg=reg: nc.tensor.matmul(bank[64:128, reg], lhsT=qe[:, h, 64:128], rhs=SbfB[h][:], start=False, stop=True),
                     reads=[qe.b, SbfB[h].b], writes=[bank.b])
        for h in range(8):
            bank = attb[h // 4]
            P.op(P.pe, lambda h=h, bank=bank: nc.tensor.matmul(bank[:, (h % 4) * 128:(h % 4 + 1) * 128], lhsT=kdT[64:128, h, :], rhs=V[64:128, h, :], start=True, stop=True),
                 reads=[kdT.b, V.b], writes=[bank.b])
        for h in range(8):
            bank = attb[h // 4]
            P.op(P.dve, lambda h=h, bank=bank: nc.vector.scalar_tensor_tensor(out=S[:, h, :], in0=S[:, h, :], scalar=eAl[:, h, 1:2],
                                                                              in1=bank[:, (h % 4) * 128:(h % 4 + 1) * 128], op0=ALU.mult, op1=ALU.add),
                 reads=[Sb[h], eAl.b, bank.b], writes=[Sb[h]])
            if not state_only:
                P.op(P.act, lambda h=h: nc.scalar.activation(out=SbfA[h][:], in_=S[:, h, :], func=AF.Copy), reads=[Sb[h]], writes=[SbfA[h].b])
        if state_only:
            return
        for i in range(2):
            P.op(P.act, lambda i=i: nc.scalar.activation(out=osb[:, i * 4:i * 4 + 4, :].rearrange("p h e -> p (h e)"), in_=obk[i][:], func=AF.Copy),
                 reads=[obk[i].b], writes=[osb.b])
        P.op(P.dve, lambda: nc.vector.tensor_tensor(out=junk[:], in0=osb[:], in1=osb[:], op=ALU.mult),
             reads=[osb.b], writes=[junk.b])
        P.op(P.dve, lambda: nc.vector.tensor_reduce(out=oss[:], in_=junk[:], axis=AX.X, op=ALU.add),
             reads=[junk.b], writes=[oss.b])
        P.op(P.act, lambda: nc.scalar.activation(out=oss[:], in_=oss[:], func=AF.Sqrt, scale=1.0 / 128, bias=EPS), reads=[oss.b], writes=[oss.b])
        P.op(P.dve, lambda: nc.vector.reciprocal(out=ors[:], in_=oss[:]), reads=[oss.b], writes=[ors.b])
        P.op(P.dve, lambda: nc.vector.tensor_tensor(out=g1[:], in0=osb[:], in1=ors[:].unsqueeze(2).broadcast_to([128, 8, 128]), op=ALU.mult),
             reads=[osb.b, ors.b], writes=[g1.b])
        P.op(P.pool, lambda: nc.gpsimd.tensor_tensor(out=fl(g1), in0=fl(g1), in1=og[:], op=ALU.mult), reads=[g1.b, og.b], writes=[g1.b])
        P.op(P.pool, lambda: nc.gpsimd.tensor_tensor(out=g[:], in0=fl(g1), in1=sz[:], op=ALU.mult), reads=[g1.b, sz.b], writes=[g.b])
        for k in range(8):
            P.op(P.pe, lambda k=k: nc.tensor.transpose(tpb[:, k * 128:(k + 1) * 128], g[:, k * 128:(k + 1) * 128], ident[:]),
                 reads=[g.b, ident.b], writes=[tp.b])
        P.op(P.act, lambda: nc.scalar.activation(out=fl(gT), in_=tpb[:, 0:1024], func=AF.Copy), reads=[tp.b], writes=[gT.b])
        hn = hnew[s % 2]
        for c in range(2):
            bank = proj[cnt["proj"] % 2]
            cnt["proj"] += 1
            for k in range(8):
                P.op(P.pe, lambda k=k, c=c, bank=bank: nc.tensor.matmul(bank[:], lhsT=gT[:, k, :], rhs=Wout[:, k, c * 512:(c + 1) * 512], start=(k == 0), stop=(k == 7)),
                     reads=[gT.b, Wout.b], writes=[bank.b])
            P.op(P.dve, lambda c=c, bank=bank: nc.vector.scalar_tensor_tensor(out=hn[:, c * 512:(c + 1) * 512], in0=bank[:], scalar=rvcol[:, s:s + 1],
                                                                              in1=hb[:, c * 512:(c + 1) * 512], op0=ALU.mult, op1=ALU.add),
                 reads=[bank.b, rvcol.b, hb.b], writes=[hn.b])
        P.dma("sp", h_out[s * 128:(s + 1) * 128, :], hn[:], reads=[hn.b])

    for s in range(NS):
        slot(s)
    P.dma("sp", io["state_out"], S[:], reads=Sb)
    if state_only:
        P.dma("sp", io["atot_out"], atot[:], reads=[atot.b])
    return hnew


def hgrn_consts(NS, q, ml_dtypes):
    ident = np.eye(128, dtype=np.float32).astype(ml_dtypes.bfloat16)
    sidx = np.arange(128)[:, None]
    tidx = np.arange(128)[None, :]
    bd = ((sidx <= tidx) & ((sidx // 64) == (tidx // 64))).astype(np.float32).astype(ml_dtypes.bfloat16)
    rst = np.ones((128, 1024), np.float32)
    rst[:, ::64] = 0.0
    rv = np.ones(128, np.float32)
    if q == 0:
        rv[:112] = 0
    else:
        rv[:] = 0
    rv_row = np.broadcast_to(rv[None, :], (128, 128)).copy()
    rv_col = np.ones((128, NS), np.float32)
    rv_col[:, 0] = rv
    return dict(ident=ident, bdmask=bd, rst=rst, rv_row=rv_row, rv_col=rv_col)


NEG = -30000.0


def _alias(t):
    a = T.__new__(T)
    a.t = t.t
    a.b = t.b
    return a


def emit_mlstm_p1(P, io, NS, pb, tag="b1", state_only=False):
    nc = P.nc
    n = lambda s: f"{tag}_{s}"
    w_in = io["w_in"]
    Wqk = T(P, n("Wqk"), [128, 8, 2048], BF16)
    Wv = T(P, n("Wv"), [128, 8, 2048], BF16)
    Wg = T(P, n("Wg"), [128, 8, 16], BF16)
    for kc in range(8):
        P.dma("pool", Wqk[:, kc, :], w_in[kc * 128:(kc + 1) * 128, 0:2048], writes=[Wqk.b])
        P.dma("pool", Wv[:, kc, :], w_in[kc * 128:(kc + 1) * 128, 2048:4096], writes=[Wv.b])
        P.dma("pool", Wg[:, kc, :], w_in[kc * 128:(kc + 1) * 128, 4096:4112], writes=[Wg.b])
    ng = T(P, n("ng"), [128, 1024], F32)
    P.dma("sp", ng[:], io["norm_gain"].partition_broadcast(128), writes=[ng.b])
    ident = T(P, n("ident"), [128, 128], BF16)
    P.dma("sp", ident[:], io["ident"], writes=[ident.b])
    identf = T(P, n("identf"), [128, 128], F32)
    P.dma("sp", identf[:], io["identf"], writes=[identf.b])
    mcur = T(P, n("mcur"), [128, 128], BF16)
    P.dma("sp", mcur[:], io["mask_cur"], writes=[mcur.b])
    rv8 = T(P, n("rv8"), [8, 128], F32)
    P.dma("sp", rv8[:], io["rv8"], writes=[rv8.b])
    nm8 = T(P, n("nm8"), [8, 128], F32)
    P.op(P.dve, lambda: nc.vector.tensor_scalar(out=nm8[:], in0=rv8[:], scalar1=-NEG, scalar2=NEG, op0=ALU.mult, op1=ALU.add),
         reads=[rv8.b], writes=[nm8.b])
    cw = T(P, n("cw"), [128, 16, 4], F32)
    for j in range(4):
        P.dma("sp", cw[:, :, j], io["conv_w"][j].rearrange("(t d) -> d t", d=128), writes=[cw.b], allow_slow_non_contiguous=True)
    cb = T(P, n("cb"), [128, 16], F32)
    P.dma("sp", cb[:], io["conv_b"].rearrange("(t d) -> d t", d=128), writes=[cb.b], allow_slow_non_contiguous=True)
    gb = T(P, n("gb"), [8, 2], F32)
    P.dma("sp", gb[:], io["gate_bias"].rearrange("(c h) -> h c", h=8), writes=[gb.b], allow_slow_non_contiguous=True)
    ngb = T(P, n("ngb"), [8, 1], F32)
    P.op(P.dve, lambda: nc.vector.tensor_scalar(out=ngb[:], in0=gb[:, 1:2], scalar1=-1.0, scalar2=0.0, op0=ALU.mult, op1=ALU.add),
         reads=[gb.b], writes=[ngb.b])
    ones8 = T(P, n("ones8"), [8, 128], F32)
    P.op(P.dve, lambda: nc.vector.memset(ones8[:], 1.0), writes=[ones8.b])
    S = T(P, n("S"), [128, 8, 257], F32)
    Sb = [Buf(n(f"S{h}")) for h in range(8)]
    P.dma("sp", S[:], io["S_in"], writes=[S.b] + Sb)
    rf = T(P, n("rf"), [8, 2], F32)
    P.dma("sp", rf[:], io["rf_in"], writes=[rf.b])
    Sbf = [T(P, n(f"Sbf{h}"), [128, 257], BF16) for h in range(8)]
    xp = T(P, n("xp"), [128, 16, 131], F32)
    P.op(P.pool, lambda: nc.gpsimd.memset(xp[:], 0.0), writes=[xp.b])

    hblk = [T(P, n(f"hblk{i}"), [128, 1024], F32) for i in range(2)]
    acc = T(P, n("acc"), [128, 16, 128], F32)
    tmp = T(P, n("tmp"), [128, 16, 128], F32)
    tmp2 = T(P, n("tmp2"), [128, 16, 128], F32)
    junk = _alias(tmp)
    st = T(P, n("st"), [128, 4], F32)
    u = T(P, n("u"), [128, 1024], BF16)
    uT = [T(P, n(f"uT{i}"), [128, 8, 128], BF16) for i in range(2)]
    qkT = T(P, n("qkT"), [128, 16, 128], BF16)
    Ktok = T(P, n("Ktok"), [128, 8, 128], BF16)
    Va = T(P, n("Va"), [128, 8, 258], BF16)
    SM = T(P, n("SM"), [128, 8, 128], BF16)
    Xs = T(P, n("Xs"), [128, 8, 257], F32)
    hm = [T(P, n(f"hm{i}"), [128, 8, 256], F32) for i in range(2)]
    G = {k: T(P, n("g_" + k), [8, 128], F32) for k in ["li", "e", "nl", "Fn", "b", "Mt", "alpha", "betap", "emm"]}
    gs = T(P, n("gs"), [8, 4], F32)
    dg = T(P, n("dg"), [8, 8], F32)
    gT = T(P, n("gT"), [128, 24], F32)
    gam = T(P, n("gam"), [128, 8], F32)
    dn = T(P, n("dn"), [128, 8], F32)
    dn2 = T(P, n("dn2"), [128, 8], F32)

    tp = pb[0]
    proj = [pb[1], pb[2]]
    stb = [pb[3], pb[4]]
    xb = [pb[5], pb[6], pb[7]]
    h_in = io["h_in"]
    cnt = {"proj": 0, "x": 0}
    fl = lambda t: t[:].rearrange("p h t -> p (h t)")
    LN128H = 0.5 * math.log(128.0)

    def slot(s):
        hb = hblk[s % 2]
        P.dma("sp", hb[:], h_in[s * 128:(s + 1) * 128, :], writes=[hb.b])
        P.op(P.act, lambda: nc.scalar.activation(out=fl(junk)[:, 0:1024], in_=hb[:], func=AF.Square, accum_out=st[:, 0:1]),
             reads=[hb.b], writes=[junk.b, st.b])
        P.op(P.act, lambda: nc.scalar.activation(out=st[:, 1:2], in_=st[:, 0:1], func=AF.Sqrt, scale=1.0 / D, bias=EPS),
             reads=[st.b], writes=[st.b])
        P.op(P.dve, lambda: nc.vector.reciprocal(out=st[:, 2:3], in_=st[:, 1:2]), reads=[st.b], writes=[st.b])
        P.op(P.dve, lambda: nc.vector.scalar_tensor_tensor(out=u[:], in0=hb[:], scalar=st[:, 2:3], in1=ng[:], op0=ALU.mult, op1=ALU.mult),
             reads=[hb.b, st.b, ng.b], writes=[u.b])
        tpb = tp[:].bitcast(BF16)
        for k in range(8):
            P.op(P.pe, lambda k=k: nc.tensor.transpose(tpb[:, k * 128:(k + 1) * 128], u[:, k * 128:(k + 1) * 128], ident[:]),
                 reads=[u.b, ident.b], writes=[tp.b])
        ut = uT[s % 2]
        P.op(P.act, lambda: nc.scalar.activation(out=fl(ut), in_=tpb[:, 0:1024], func=AF.Copy), reads=[tp.b], writes=[ut.b])
        gbank = proj[cnt["proj"] % 2]
        cnt["proj"] += 1
        for gi in range(2):
            for k in range(8):
                P.op(P.pe, lambda k=k, gi=gi: nc.tensor.matmul(gbank[0:8, gi * 128:(gi + 1) * 128], lhsT=Wg[:, k, gi * 8:(gi + 1) * 8], rhs=ut[:, k, :],
                                                               start=(k == 0), stop=(k == 7)),
                     reads=[ut.b, Wg.b], writes=[gbank.b])
        li, e, nl, Fn, b_, Mt, alpha, betap, emm = [G[k] for k in ["li", "e", "nl", "Fn", "b", "Mt", "alpha", "betap", "emm"]]
        P.op(P.act, lambda: nc.scalar.activation(out=li[:], in_=gbank[0:8, 0:128], func=AF.Identity, bias=gb[:, 0:1]),
             reads=[gbank.b, gb.b], writes=[li.b])
        P.op(P.act, lambda: nc.scalar.activation(out=e[:], in_=gbank[0:8, 128:256], func=AF.Exp, scale=-1.0, bias=ngb[:, 0:1]),
             reads=[gbank.b, ngb.b], writes=[e.b])
        P.op(P.act, lambda: nc.scalar.activation(out=nl[:], in_=e[:], func=AF.Ln, bias=1.0), reads=[e.b], writes=[nl.b])
        if s == 0:
            P.op(P.dve, lambda: nc.vector.tensor_tensor(out=li[:], in0=li[:], in1=nm8[:], op=ALU.add), reads=[li.b, nm8.b], writes=[li.b])
            P.op(P.dve, lambda: nc.vector.tensor_tensor(out=nl[:], in0=nl[:], in1=rv8[:], op=ALU.mult), reads=[nl.b, rv8.b], writes=[nl.b])
        P.op(P.dve, lambda: nc.vector.tensor_tensor_scan(out=Fn[:], data0=ones8[:], data1=nl[:], initial=rf[:, 1:2], op0=ALU.mult, op1=ALU.add),
             reads=[ones8.b, nl.b, rf.b], writes=[Fn.b])
        P.op(P.dve, lambda: nc.vector.tensor_tensor(out=b_[:], in0=li[:], in1=Fn[:], op=ALU.add), reads=[li.b, Fn.b], writes=[b_.b])
        P.op(P.dve, lambda: nc.vector.tensor_tensor_scan(out=Mt[:], data0=ones8[:], data1=b_[:], initial=rf[:, 0:1], op0=ALU.mult, op1=ALU.max),
             reads=[ones8.b, b_.b, rf.b], writes=[Mt.b])
        P.op(P.dve, lambda: nc.vector.tensor_scalar(out=gs[:, 0:1], in0=Mt[:, 127:128], scalar1=-1.0, scalar2=-LN128H, op0=ALU.mult, op1=ALU.add),
             reads=[Mt.b], writes=[gs.b])
        P.op(P.dve, lambda: nc.vector.tensor_tensor(out=gs[:, 1:2], in0=rf[:, 0:1], in1=Mt[:, 127:128], op=ALU.subtract), reads=[rf.b, Mt.b], writes=[gs.b])
        P.op(P.dve, lambda: nc.vector.tensor_copy(out=gs[:, 2:3], in_=Mt[:, 127:128]), reads=[Mt.b], writes=[gs.b])
        P.op(P.act, lambda: nc.scalar.activation(out=alpha[:], in_=b_[:], func=AF.Exp, bias=gs[:, 0:1]), reads=[b_.b, gs.b], writes=[alpha.b])
        if s == 0:
            P.op(P.dve, lambda: nc.vector.tensor_tensor(out=alpha[:], in0=alpha[:], in1=rv8[:], op=ALU.mult), reads=[alpha.b, rv8.b], writes=[alpha.b])
        if not state_only:
            P.op(P.act, lambda: nc.scalar.activation(out=betap[:], in_=Mt[:], func=AF.Exp, scale=-1.0, bias=gs[:, 2:3]), reads=[Mt.b, gs.b], writes=[betap.b])
            P.op(P.dve, lambda: nc.vector.tensor_tensor(out=emm[:], in0=Fn[:], in1=Mt[:], op=ALU.subtract), reads=[Fn.b, Mt.b], writes=[emm.b])
            P.op(P.act, lambda: nc.scalar.activation(out=emm[:], in_=emm[:], func=AF.Exp), reads=[emm.b], writes=[emm.b])
        P.op(P.act, lambda: nc.scalar.activation(out=gs[:, 1:2], in_=gs[:, 1:2], func=AF.Exp), reads=[gs.b], writes=[gs.b])
        P.op(P.dve, lambda: nc.vector.tensor_copy(out=rf[:, 0:1], in_=Mt[:, 127:128]), reads=[Mt.b, gs.b], writes=[rf.b])
        P.op(P.dve, lambda: nc.vector.tensor_copy(out=rf[:, 1:2], in_=Fn[:, 127:128]), reads=[Fn.b], writes=[rf.b])
        P.op(P.act, lambda: nc.scalar.activation(out=dg[:], in_=identf[0:8, 0:8], func=AF.Copy, scale=gs[:, 1:2]), reads=[identf.b, gs.b], writes=[dg.b])
        gtb = stb[0]
        P.op(P.pe, lambda: nc.tensor.matmul(gtb[:, 32:40], lhsT=ones8[:], rhs=dg[:], start=True, stop=True), reads=[ones8.b, dg.b], writes=[gtb.b])
        ngt = 1 if state_only else 3
        for i, t_ in enumerate([alpha, betap, emm][:ngt]):
            P.op(P.pe, lambda i=i, t_=t_: nc.tensor.transpose(gtb[:, i * 8:(i + 1) * 8], t_[:], identf[0:8, 0:8]), reads=[t_.b, identf.b], writes=[gtb.b])
        P.op(P.act, lambda: nc.scalar.activation(out=gT[:, 0:8 * ngt], in_=gtb[:, 0:8 * ngt], func=AF.Copy), reads=[gtb.b], writes=[gT.b])
        P.op(P.act, lambda: nc.scalar.activation(out=gam[:], in_=gtb[:, 32:40], func=AF.Copy), reads=[gtb.b], writes=[gam.b])
        for h in range(0 if not state_only else 8, 8):
            P.op(P.act, lambda h=h: nc.scalar.activation(out=Sbf[h][:], in_=S[:, h, :], func=AF.Copy, scale=gam[:, h:h + 1]),
                 reads=[Sb[h], gam.b], writes=[Sbf[h].b])
        t0 = 8 if state_only else 0
        for grp in range(2 if state_only else 0, 4):
            bank = proj[cnt["proj"] % 2]
            cnt["proj"] += 1
            for i in range(4):
                ti = grp * 4 + i
                for k in range(8):
                    P.op(P.pe, lambda k=k, i=i, ti=ti, bank=bank: nc.tensor.matmul(bank[:, i * 128:(i + 1) * 128], lhsT=Wqk[:, k, ti * 128:(ti + 1) * 128],
                                                                               rhs=ut[:, k, :], start=(k == 0), stop=(k == 7)),
                         reads=[ut.b, Wqk.b], writes=[bank.b])
            P.op(P.act, lambda grp=grp, bank=bank: nc.scalar.activation(out=xp[:, grp * 4:grp * 4 + 4, 3:131], in_=bank[:].rearrange("p (i t) -> p i t", i=4), func=AF.Copy),
                 reads=[bank.b], writes=[xp.b])
        nt = 16 - t0
        wj = lambda j: cw[:, t0:16, j:j + 1].broadcast_to([128, nt, 128])
        P.op(P.dve, lambda: nc.vector.tensor_tensor(out=acc[:, t0:16, :], in0=xp[:, t0:16, 0:128], in1=wj(0), op=ALU.mult), reads=[xp.b, cw.b], writes=[acc.b])
        P.op(P.pool, lambda: nc.gpsimd.tensor_tensor(out=tmp[:, t0:16, :], in0=xp[:, t0:16, 1:129], in1=wj(1), op=ALU.mult), reads=[xp.b, cw.b], writes=[tmp.b])
        P.op(P.pool, lambda: nc.gpsimd.tensor_tensor(out=tmp2[:, t0:16, :], in0=xp[:, t0:16, 2:130], in1=wj(2), op=ALU.mult), reads=[xp.b, cw.b], writes=[tmp2.b])
        P.op(P.dve, lambda: nc.vector.tensor_tensor(out=acc[:, t0:16, :], in0=acc[:, t0:16, :], in1=tmp[:, t0:16, :], op=ALU.add), reads=[acc.b, tmp.b], writes=[acc.b])
        P.op(P.pool, lambda: nc.gpsimd.tensor_tensor(out=tmp[:, t0:16, :], in0=xp[:, t0:16, 3:131], in1=wj(3), op=ALU.mult), reads=[xp.b, cw.b], writes=[tmp.b])
        P.op(P.dve, lambda: nc.vector.tensor_tensor(out=acc[:, t0:16, :], in0=acc[:, t0:16, :], in1=tmp2[:, t0:16, :], op=ALU.add), reads=[acc.b, tmp2.b], writes=[acc.b])
        P.op(P.pool, lambda: nc.gpsimd.tensor_tensor(out=tmp2[:, t0:16, :], in0=tmp[:, t0:16, :], in1=cb[:, t0:16].unsqueeze(2).broadcast_to([128, nt, 128]), op=ALU.add),
             reads=[tmp.b, cb.b], writes=[tmp2.b])
        P.op(P.dve, lambda: nc.vector.tensor_tensor(out=acc[:, t0:16, :], in0=acc[:, t0:16, :], in1=tmp2[:, t0:16, :], op=ALU.add), reads=[acc.b, tmp2.b], writes=[acc.b])
        P.op(P.act, lambda: nc.scalar.activation(out=qkT[:, t0:16, :].rearrange("p h t -> p (h t)"), in_=acc[:, t0:16, :].rearrange("p h t -> p (h t)"), func=AF.Silu),
             reads=[acc.b], writes=[qkT.b])
        P.op(P.pool, lambda: nc.gpsimd.tensor_copy(out=xp[:, t0:16, 0:3], in_=xp[:, t0:16, 128:131]), reads=[xp.b], writes=[xp.b])
        for h in range(8):
            P.op(P.pe, lambda h=h: nc.tensor.transpose(tpb[:, h * 128:(h + 1) * 128], qkT[:, 8 + h, :], ident[:]), reads=[qkT.b, ident.b], writes=[tp.b])
        P.op(P.act, lambda: nc.scalar.activation(out=fl(Ktok), in_=tpb[:, 0:1024], func=AF.Copy), reads=[tp.b], writes=[Ktok.b])
        for c in range(4):
            bank = proj[cnt["proj"] % 2]
            cnt["proj"] += 1
            for k in range(8):
                P.op(P.pe, lambda k=k, c=c, bank=bank: nc.tensor.matmul(bank[:], lhsT=ut[:, k, :], rhs=Wv[:, k, c * 512:(c + 1) * 512], start=(k == 0), stop=(k == 7)),
                     reads=[ut.b, Wv.b], writes=[bank.b])
            for hh in range(2):
                h = c * 2 + hh
                P.op(P.act, lambda h=h, hh=hh, bank=bank: nc.scalar.activation(out=Va[:, h, 0:256], in_=bank[:, hh * 256:(hh + 1) * 256], func=AF.Copy, scale=gT[:, h:h + 1]),
                     reads=[bank.b, gT.b], writes=[Va.b])
        P.op(P.dve, lambda: nc.vector.tensor_copy(out=Va[:, :, 256:257], in_=gT[:, 0:8].unsqueeze(2)), reads=[gT.b], writes=[Va.b])
        if not state_only:
            for h in range(8):
                bank = stb[h // 4]
                P.op(P.pe, lambda h=h, bank=bank: nc.tensor.matmul(bank[:, (h % 4) * 128:(h % 4 + 1) * 128], lhsT=qkT[:, 8 + h, :], rhs=qkT[:, h, :], start=True, stop=True),
                     reads=[qkT.b], writes=[bank.b])
            for i in range(2):
                P.op(P.dve, lambda i=i: nc.vector.tensor_tensor(out=SM[:, i * 4:i * 4 + 4, :], in0=stb[i][:].rearrange("p (h t) -> p h t", h=4),
                                                                in1=mcur[:].unsqueeze(1).broadcast_to([128, 4, 128]), op=ALU.mult),
                     reads=[stb[i].b, mcur.b], writes=[SM.b])
        for h in range(8):
            if not state_only:
                bx = xb[cnt["x"] % 3]
                cnt["x"] += 1
                P.op(P.pe, lambda h=h, bx=bx: nc.tensor.matmul(bx[:, 0:257], lhsT=SM[:, h, :], rhs=Va[:, h, 0:257], start=True, stop=False),
                     reads=[SM.b, Va.b], writes=[bx.b])
                P.op(P.pe, lambda h=h, bx=bx: nc.tensor.matmul(bx[:, 0:257], lhsT=qkT[:, h, :], rhs=Sbf[h][:], start=False, stop=True),
                     reads=[qkT.b, Sbf[h].b], writes=[bx.b])
                P.op(P.act, lambda h=h, bx=bx: nc.scalar.activation(out=Xs[:, h, :], in_=bx[:, 0:257], func=AF.Copy), reads=[bx.b], writes=[Xs.b])
            bu = xb[cnt["x"] % 3]
            cnt["x"] += 1
            P.op(P.pe, lambda h=h, bu=bu: nc.tensor.matmul(bu[:, 0:257], lhsT=Ktok[:, h, :], rhs=Va[:, h, 0:257], start=True, stop=True),
                 reads=[Ktok.b, Va.b], writes=[bu.b])
            P.op(P.dve, lambda h=h, bu=bu: nc.vector.scalar_tensor_tensor(out=S[:, h, :], in0=S[:, h, :], scalar=gam[:, h:h + 1], in1=bu[:, 0:257],
                                                                          op0=ALU.mult, op1=ALU.add),
                 reads=[Sb[h], gam.b, bu.b], writes=[Sb[h]])
        if state_only:
            return
        P.op(P.dve, lambda: nc.vector.tensor_tensor(out=dn[:].unsqueeze(2), in0=Xs[:, :, 256:257], in1=gT[:, 8:16].unsqueeze(2), op=ALU.mult),
             reads=[Xs.b, gT.b], writes=[dn.b])
        P.op(P.act, lambda: nc.scalar.activation(out=dn[:], in_=dn[:], func=AF.Abs), reads=[dn.b], writes=[dn.b])
        P.op(P.dve, lambda: nc.vector.tensor_tensor(out=dn[:], in0=dn[:], in1=gT[:, 16:24], op=ALU.max), reads=[dn.b, gT.b], writes=[dn.b])
        P.op(P.dve, lambda: nc.vector.reciprocal(out=dn2[:], in_=dn[:]), reads=[dn.b], writes=[dn2.b])
        P.op(P.dve, lambda: nc.vector.tensor_tensor(out=dn2[:], in0=dn2[:], in1=gT[:, 8:16], op=ALU.mult), reads=[dn2.b, gT.b], writes=[dn2.b])
        hmt = hm[s % 2]
        P.op(P.dve, lambda: nc.vector.tensor_tensor(out=hmt[:], in0=Xs[:, :, 0:256], in1=dn2[:].unsqueeze(2).broadcast_to([128, 8, 256]), op=ALU.mult),
             reads=[Xs.b, dn2.b], writes=[hmt.b])
        P.dma("sp", io["hm"][s * 128:(s + 1) * 128, :], hmt[:].rearrange("p h e -> p (h e)"), reads=[hmt.b])

    for s in range(NS):
        slot(s)
    P.dma("sp", io["S_out"], S[:], reads=Sb)
    P.dma("sp", io["rf_out"], rf[:], reads=[rf.b])
    return hm


def emit_mlstm_p2(P, io, NS, pb, tag="b2"):
    nc = P.nc
    n = lambda s: f"{tag}_{s}"
    w_in = io["w_in"]
    Woz = T(P, n("Woz"), [128, 8, 4096], BF16)
    Wout = T(P, n("Wout"), [128, 16, 1024], BF16)
    for kc in range(8):
        for c0 in (0, 2048):
            P.dma("pool", Woz[:, kc, c0:c0 + 2048], w_in[kc * 128:(kc + 1) * 128, 4112 + c0:4112 + c0 + 2048], writes=[Woz.b])
    load_w_bf16(P, Wout, io["w_out"], 2048, 1024)
    ng = T(P, n("ng"), [128, 1024], F32)
    P.dma("sp", ng[:], io["norm_gain"].partition_broadcast(128), writes=[ng.b])
    hg = T(P, n("hg"), [128, 2048], F32)
    P.dma("sp", hg[:], io["h_gain"].partition_broadcast(128), writes=[hg.b])
    ident = T(P, n("ident"), [128, 128], BF16)
    P.dma("sp", ident[:], io["ident"], writes=[ident.b])
    rvcol = T(P, n("rvcol"), [128, NS], F32)
    P.dma("sp", rvcol[:], io["rv_col"], writes=[rvcol.b])
    hblk = [T(P, n(f"hblk{i}"), [128, 1024], F32) for i in range(2)]
    hmb = [T(P, n("hmb0"), [128, 8, 256], F32)] * 2
    hnew = [T(P, n(f"hnew{i}"), [128, 1024], F32) for i in range(2)]
    st = T(P, n("st"), [128, 4], F32)
    u = T(P, n("u"), [128, 1024], BF16)
    uT = [T(P, n(f"uT{i}"), [128, 8, 128], BF16) for i in range(2)]
    so = T(P, n("so"), [128, 8, 256], BF16)
    sz = T(P, n("sz"), [128, 2048], BF16)
    t2 = T(P, n("t2"), [128, 8, 256], F32)
    sq = T(P, n("sq"), [128, 8, 256], F32)
    ss = T(P, n("ss"), [128, 8], F32)
    rs = T(P, n("rs"), [128, 8], F32)
    g = T(P, n("g"), [128, 2048], BF16)
    gT = T(P, n("gT"), [128, 16, 128], BF16)
    tp = pb[0]
    tp2 = pb[3]
    proj = [pb[1], pb[2]]
    cnt = {"proj": 0}
    fl = lambda t: t[:].rearrange("p h t -> p (h t)")
    h_in, h_out = io["h_in"], io["h_out"]

    def slot(s):
        hb = hblk[s % 2]
        hmt = hmb[s % 2]
        P.dma("sp", hb[:], h_in[s * 128:(s + 1) * 128, :], writes=[hb.b])
        P.dma("sp", fl(hmt), io["hm"][s * 128:(s + 1) * 128, :], writes=[hmt.b])
        P.op(P.act, lambda: nc.scalar.activation(out=fl(sq)[:, 0:1024], in_=hb[:], func=AF.Square, accum_out=st[:, 0:1]),
             reads=[hb.b], writes=[sq.b, st.b])
        P.op(P.act, lambda: nc.scalar.activation(out=st[:, 1:2], in_=st[:, 0:1], func=AF.Sqrt, scale=1.0 / D, bias=EPS),
             reads=[st.b], writes=[st.b])
        P.op(P.dve, lambda: nc.vector.reciprocal(out=st[:, 2:3], in_=st[:, 1:2]), reads=[st.b], writes=[st.b])
        P.op(P.dve, lambda: nc.vector.scalar_tensor_tensor(out=u[:], in0=hb[:], scalar=st[:, 2:3], in1=ng[:], op0=ALU.mult, op1=ALU.mult),
             reads=[hb.b, st.b, ng.b], writes=[u.b])
        tpb = tp[:].bitcast(BF16)
        for k in range(8):
            P.op(P.pe, lambda k=k: nc.tensor.transpose(tpb[:, k * 128:(k + 1) * 128], u[:, k * 128:(k + 1) * 128], ident[:]),
                 reads=[u.b, ident.b], writes=[tp.b])
        ut = uT[s % 2]
        P.op(P.act, lambda: nc.scalar.activation(out=fl(ut), in_=tpb[:, 0:1024], func=AF.Copy), reads=[tp.b], writes=[ut.b])
        for c in range(8):
            bank = proj[cnt["proj"] % 2]
            cnt["proj"] += 1
            for k in range(8):
                P.op(P.pe, lambda k=k, c=c, bank=bank: nc.tensor.matmul(bank[:], lhsT=ut[:, k, :], rhs=Woz[:, k, c * 512:(c + 1) * 512], start=(k == 0), stop=(k == 7)),
                     reads=[ut.b, Woz.b], writes=[bank.b])
            if c < 4:
                P.op(P.act, lambda c=c, bank=bank: nc.scalar.activation(out=fl(so)[:, c * 512:(c + 1) * 512], in_=bank[:], func=AF.Sigmoid),
                     reads=[bank.b], writes=[so.b])
            else:
                P.op(P.act, lambda c=c, bank=bank: nc.scalar.activation(out=sz[:, (c - 4) * 512:(c - 3) * 512], in_=bank[:], func=AF.Silu),
                     reads=[bank.b], writes=[sz.b])
        P.op(P.dve, lambda: nc.vector.tensor_tensor(out=t2[:], in0=hmt[:], in1=so[:], op=ALU.mult), reads=[hmt.b, so.b], writes=[t2.b])
        P.op(P.pool, lambda: nc.gpsimd.tensor_tensor(out=sq[:], in0=t2[:], in1=t2[:], op=ALU.mult), reads=[t2.b], writes=[sq.b])
        P.op(P.dve, lambda: nc.vector.tensor_reduce(out=ss[:], in_=sq[:], axis=AX.X, op=ALU.add), reads=[sq.b], writes=[ss.b])
        P.op(P.act, lambda: nc.scalar.activation(out=ss[:], in_=ss[:], func=AF.Sqrt, scale=1.0 / 256, bias=EPS), reads=[ss.b], writes=[ss.b])
        P.op(P.dve, lambda: nc.vector.reciprocal(out=rs[:], in_=ss[:]), reads=[ss.b], writes=[rs.b])
        P.op(P.dve, lambda: nc.vector.tensor_tensor(out=t2[:], in0=t2[:], in1=rs[:].unsqueeze(2).broadcast_to([128, 8, 256]), op=ALU.mult),
             reads=[t2.b, rs.b], writes=[t2.b])
        P.op(P.pool, lambda: nc.gpsimd.tensor_tensor(out=fl(t2), in0=fl(t2), in1=hg[:], op=ALU.mult), reads=[t2.b, hg.b], writes=[t2.b])
        P.op(P.pool, lambda: nc.gpsimd.tensor_tensor(out=g[:], in0=fl(t2), in1=sz[:], op=ALU.mult), reads=[t2.b, sz.b], writes=[g.b])
        tp2b = tp2[:].bitcast(BF16)
        for k in range(16):
            tb, tbuf = (tpb, tp) if k < 8 else (tp2b, tp2)
            P.op(P.pe, lambda k=k, tb=tb: nc.tensor.transpose(tb[:, (k % 8) * 128:(k % 8 + 1) * 128], g[:, k * 128:(k + 1) * 128], ident[:]),
                 reads=[g.b, ident.b], writes=[tbuf.b])
        P.op(P.act, lambda: nc.scalar.activation(out=gT[:, 0:8, :].rearrange("p k t -> p (k t)"), in_=tpb[:, 0:1024], func=AF.Copy), reads=[tp.b], writes=[gT.b])
        P.op(P.act, lambda: nc.scalar.activation(out=gT[:, 8:16, :].rearrange("p k t -> p (k t)"), in_=tp2b[:, 0:1024], func=AF.Copy), reads=[tp2.b], writes=[gT.b])
        hn = hnew[s % 2]
        for c in range(2):
            bank = proj[cnt["proj"] % 2]
            cnt["proj"] += 1
            for k in range(16):
                P.op(P.pe, lambda k=k, c=c, bank=bank: nc.tensor.matmul(bank[:], lhsT=gT[:, k, :], rhs=Wout[:, k, c * 512:(c + 1) * 512], start=(k == 0), stop=(k == 15)),
                     reads=[gT.b, Wout.b], writes=[bank.b])
            P.op(P.dve, lambda c=c, bank=bank: nc.vector.scalar_tensor_tensor(out=hn[:, c * 512:(c + 1) * 512], in0=bank[:], scalar=rvcol[:, s:s + 1],
                                                                              in1=hb[:, c * 512:(c + 1) * 512], op0=ALU.mult, op1=ALU.add),
                 reads=[bank.b, rvcol.b, hb.b], writes=[hn.b])
        P.dma("sp", h_out[s * 128:(s + 1) * 128, :], hn[:], reads=[hn.b])

    for s in range(NS):
        slot(s)
    return hnew


def mlstm_consts(NS, q, ml_dtypes):
    ident = np.eye(128, dtype=np.float32).astype(ml_dtypes.bfloat16)
    identf = np.eye(128, dtype=np.float32)
    sidx = np.arange(128)[:, None]
    tidx = np.arange(128)[None, :]
    mask_cur = (sidx <= tidx).astype(np.float32).astype(ml_dtypes.bfloat16)
    rv = np.ones(128, np.float32)
    if q == 0:
        rv[:112] = 0
    else:
        rv[:] = 0
    rv8 = np.broadcast_to(rv[None, :], (8, 128)).copy()
    rv_col = np.ones((128, NS), np.float32)
    rv_col[:, 0] = rv
    return dict(ident=ident, identf=identf, mask_cur=mask_cur, rv8=rv8, rv_col=rv_col)


def _mk_io(P, specs):
    io = {}
    for name, shape, dt, kind in specs:
        io[name] = P.dram(name, shape, dt, kind)
    return io


import ml_dtypes
from concourse.bass_utils import run_bass_kernel_spmd

NSLOT = 33


def _sel_accumulate(P, nc, tag, gath, rows_per, nrows, ncols, sel, n_terms, extra=None):
    acc = T(P, f"{tag}_acc", [nrows, ncols], F32)
    g = [T(P, f"{tag}_g{i}", [nrows, ncols], F32) for i in range(2)]
    terms = [gath[i * rows_per:i * rows_per + nrows, 0:ncols] for i in range(n_terms)]
    if extra is not None:
        terms.append(extra)
    for i, src in enumerate(terms):
        gi = g[i % 2]
        P.dma("sp", gi[:], src, writes=[gi.b])
        if i == 0:
            P.op(P.act, lambda gi=gi, i=i: nc.scalar.activation(out=acc[:], in_=gi[:], func=AF.Copy, scale=sel[0:nrows, i:i + 1]),
                 reads=[gi.b, sel.b], writes=[acc.b])
        else:
            P.op(P.dve, lambda gi=gi, i=i: nc.vector.scalar_tensor_tensor(out=acc[:], in0=gi[:], scalar=sel[0:nrows, i:i + 1], in1=acc[:],
                                                                          op0=ALU.mult, op1=ALU.add),
                 reads=[gi.b, sel.b, acc.b], writes=[acc.b])
    return acc


def build_fused(NS=NSLOT):
    P = Prog(same_sync=True)
    nc = P.nc
    I, O = "ExternalInput", "ExternalOutput"
    e = _mk_io(P, [
        ("x_in", [(NS + 1) * 128, 1024], F32, I), ("out", [(NS - 1) * 128, 1024], F32, O),
        ("norm_gain", [4, 1024], F32, I),
        ("a_w_in", [2, 1024, 2560], F32, I), ("a_q_gain", [2, 64], F32, I), ("a_k_gain", [2, 64], F32, I), ("a_sinks", [2, 16], F32, I),
        ("a_w_out", [2, 1024, 1024], F32, I),
        ("b_w_in", [1024, 8208], F32, I), ("b_conv_w", [4, 2048], F32, I), ("b_conv_b", [2048], F32, I), ("b_gate_bias", [16], F32, I),
        ("b_h_gain", [2048], F32, I), ("b_w_out", [2048, 1024], F32, I),
        ("c_w_in", [1024, 4096], F32, I), ("c_gamma", [4, 1024], F32, I), ("c_o_gain", [1024], F32, I), ("c_w_out", [1024, 1024], F32, I),
        ("ident", [128, 128], BF16, I), ("identf", [128, 128], F32, I), ("mask_cur", [128, 128], BF16, I), ("mask_prev", [128, 128], BF16, I),
        ("bdmask", [128, 128], BF16, I), ("rst", [128, 1024], F32, I),
        ("cos2_0", [(NS + 1) * 128, 64], F32, I), ("sinS_0", [(NS + 1) * 128, 64], F32, I), ("kvalid_0", [128, NS + 1], F32, I),
        ("cos2_3", [NS * 128, 64], F32, I), ("sinS_3", [NS * 128, 64], F32, I), ("kvalid_3", [128, NS], F32, I),
        ("rv8", [8, 128], F32, I), ("rv_row", [128, 128], F32, I), ("rv_col", [128, NS], F32, I),
        ("sel8", [128, 8], F32, I), ("sel9", [128, 9], F32, I), ("selp", [128, 8], F32, I),
    ])
    N = "Internal"
    hA = P.dram("hA", [(NS + 1) * 128, 1024], F32, N)
    hB = P.dram("hB", [NS * 128, 1024], F32, N)
    hC = P.dram("hC", [NS * 128, 1024], F32, N)
    hD0 = P.dram("hD0", [128, 1024], F32, N)
    hm = P.dram("hm", [NS * 128, 2048], F32, N)
    Sb_in = P.dram("Sb_in", [128, 2056], F32, N)
    rfb_in = P.dram("rfb_in", [8, 2], F32, N)
    Sc_in = P.dram("Sc_in", [128, 1024], F32, N)
    stB = P.cc_dram("stB", [136, 2056], F32)
    gB = P.cc_dram("gB", [8 * 136, 2056], F32)
    stC = P.cc_dram("stC", [128, 1032], F32)
    gC = P.cc_dram("gC", [8 * 128, 1032], F32)
    hh = P.cc_dram("hh", [128, 1024], F32)
    gH = P.cc_dram("gH", [8 * 128, 1024], F32)
    pb = [T(P, f"pb{i}", [128, 512], F32, psum=True) for i in range(8)]

    with P.scope():
        emit_attn(P, dict(h_in=e["x_in"], h_out=hA, w_in=e["a_w_in"][0], w_out=e["a_w_out"][0], norm_gain=e["norm_gain"][0],
                          q_gain=e["a_q_gain"][0], k_gain=e["a_k_gain"][0], sinks=e["a_sinks"][0], ident=e["ident"],
                          mask_cur=e["mask_cur"], mask_prev=e["mask_prev"], cos2=e["cos2_0"], sinS=e["sinS_0"], kvalid=e["kvalid_0"]),
                  NS + 1, pb, tag="a0")
    h1 = hA[128:(NS + 1) * 128, :]
    with P.scope():
        z = T(P, "zero", [128, 2056], F32)
        P.op(P.dve, lambda: nc.vector.memset(z[:], 0.0), writes=[z.b])
        P.dma("sp", Sb_in, z[:], reads=[z.b])
        rfi = T(P, "rfinit", [8, 2], F32)
        P.op(P.dve, lambda: nc.vector.memset(rfi[:, 0:1], NEG), writes=[rfi.b])
        P.op(P.dve, lambda: nc.vector.memset(rfi[:, 1:2], 0.0), writes=[rfi.b])
        P.dma("sp", rfb_in, rfi[:], reads=[rfi.b])
        P.dma("sp", Sc_in, z[:, 0:1024], reads=[z.b])
        P.dma("sp", stB[0:128, :], z[:], reads=[z.b])
        P.dma("sp", stB[128:136, :], z[0:8, :], reads=[z.b])
    iob = dict(h_in=h1, h_out=hB, w_in=e["b_w_in"], w_out=e["b_w_out"], norm_gain=e["norm_gain"][1], conv_w=e["b_conv_w"], conv_b=e["b_conv_b"],
               gate_bias=e["b_gate_bias"], h_gain=e["b_h_gain"], ident=e["ident"], identf=e["identf"], mask_cur=e["mask_cur"],
               rv8=e["rv8"], rv_col=e["rv_col"], S_in=Sb_in.rearrange("p (h e) -> p h e", h=8),
               S_out=stB[0:128, :].rearrange("p (h e) -> p h e", h=8), rf_in=rfb_in, rf_out=stB[128:136, 0:2], hm=hm)
    with P.scope():
        emit_mlstm_p1(P, iob, NS, pb, tag="b1s", state_only=True)
    P.allgather(stB, gB)
    with P.scope():
        sel = T(P, "selpb", [128, 8], F32)
        P.dma("sp", sel[:], e["selp"], writes=[sel.b])
        idf = T(P, "cb_idf", [8, 8], F32)
        P.dma("sp", idf[:], e["identf"][0:8, 0:8], writes=[idf.b])
        S = T(P, "cb_S", [128, 8, 257], F32)
        R = T(P, "cb_R", [128, 8], F32)
        Fn = T(P, "cb_Fn", [128, 8], F32)
        for t_ in (S, R, Fn):
            P.op(P.dve, lambda t_=t_: nc.vector.memset(t_[:], 0.0), writes=[t_.b])
        Sl = [T(P, f"cb_Sl{i}", [128, 8, 257], F32) for i in range(2)]
        rl = [T(P, f"cb_rl{i}", [128, 2, 8], F32) for i in range(2)]
        Rj = T(P, "cb_Rj", [128, 8], F32)
        Rn = T(P, "cb_Rn", [128, 8], F32)
        ca = T(P, "cb_ca", [128, 8], F32)
        cbb = T(P, "cb_cb", [128, 8], F32)
        nsel = T(P, "cb_nsel", [128, 8], F32)
        P.op(P.dve, lambda: nc.vector.tensor_scalar(out=nsel[:], in0=sel[:], scalar1=-NEG, scalar2=NEG, op0=ALU.mult, op1=ALU.add),
             reads=[sel.b], writes=[nsel.b])
        tmpS = T(P, "cb_tmpS", [128, 8, 257], F32)
        for i in range(8):
            sl = Sl[i % 2]; r_ = rl[i % 2]
            P.dma("sp", sl[:].rearrange("p h e -> p (h e)"), gB[i * 136:i * 136 + 128, :], writes=[sl.b])
            for c_ in range(2):
                P.dma("sp", r_[:, c_, :], gB[i * 136 + 128:i * 136 + 136, c_:c_ + 1].rearrange("h o -> (h o)").partition_broadcast(128),
                      writes=[r_.b], allow_slow_non_contiguous=True)
            P.op(P.dve, lambda r_=r_: nc.vector.tensor_tensor(out=Rj[:], in0=r_[:, 0, :], in1=Fn[:], op=ALU.add), reads=[r_.b, Fn.b], writes=[Rj.b])
            P.op(P.dve, lambda i=i: nc.vector.scalar_tensor_tensor(out=Rj[:], in0=Rj[:], scalar=sel[:, i:i + 1], in1=nsel[:, i:i + 1].broadcast_to([128, 8]),
                                                                   op0=ALU.mult, op1=ALU.add), reads=[Rj.b, sel.b, nsel.b], writes=[Rj.b])
            P.op(P.dve, lambda: nc.vector.tensor_tensor(out=Rn[:], in0=R[:], in1=Rj[:], op=ALU.max), reads=[R.b, Rj.b], writes=[Rn.b])
            P.op(P.dve, lambda: nc.vector.tensor_tensor(out=ca[:], in0=R[:], in1=Rn[:], op=ALU.subtract), reads=[R.b, Rn.b], writes=[ca.b])
            P.op(P.dve, lambda: nc.vector.tensor_tensor(out=cbb[:], in0=Rj[:], in1=Rn[:], op=ALU.subtract), reads=[Rj.b, Rn.b], writes=[cbb.b])
            P.op(P.act, lambda: nc.scalar.activation(out=ca[:], in_=ca[:], func=AF.Exp), reads=[ca.b], writes=[ca.b])
            P.op(P.act, lambda: nc.scalar.activation(out=cbb[:], in_=cbb[:], func=AF.Exp), reads=[cbb.b], writes=[cbb.b])
            P.op(P.dve, lambda: nc.vector.tensor_tensor(out=S[:], in0=S[:], in1=ca[:].unsqueeze(2).broadcast_to([128, 8, 257]), op=ALU.mult),
                 reads=[S.b, ca.b], writes=[S.b])
            P.op(P.pool, lambda sl=sl: nc.gpsimd.tensor_tensor(out=tmpS[:], in0=sl[:], in1=cbb[:].unsqueeze(2).broadcast_to([128, 8, 257]), op=ALU.mult),
                 reads=[sl.b, cbb.b], writes=[tmpS.b])
            P.op(P.dve, lambda: nc.vector.tensor_tensor(out=S[:], in0=S[:], in1=tmpS[:], op=ALU.add), reads=[S.b, tmpS.b], writes=[S.b])
            P.op(P.dve, lambda: nc.vector.tensor_copy(out=R[:], in_=Rn[:]), reads=[Rn.b], writes=[R.b])
            P.op(P.dve, lambda i=i, r_=r_: nc.vector.scalar_tensor_tensor(out=Fn[:], in0=r_[:, 1, :], scalar=sel[:, i:i + 1], in1=Fn[:], op0=ALU.mult, op1=ALU.add),
                 reads=[r_.b, sel.b, Fn.b], writes=[Fn.b])
        P.dma("sp", Sb_in, S[:].rearrange("p h e -> p (h e)"), reads=[S.b])
        rfo = T(P, "cb_rfo", [8, 2], F32)
        dg8 = T(P, "cb_dg8", [8, 8], F32)
        for c_, t_ in enumerate((R, Fn)):
            P.op(P.dve, lambda t_=t_: nc.vector.tensor_tensor(out=dg8[:], in0=t_[0:8, :], in1=idf[:], op=ALU.mult), reads=[t_.b, idf.b], writes=[dg8.b])
            P.op(P.dve, lambda c_=c_: nc.vector.tensor_reduce(out=rfo[:, c_:c_ + 1], in_=dg8[:], axis=AX.X, op=ALU.add), reads=[dg8.b], writes=[rfo.b])
        P.dma("sp", rfb_in, rfo[:], reads=[rfo.b])
    with P.scope():
        emit_mlstm_p1(P, iob, NS, pb, tag="b1f", state_only=False)
    with P.scope():
        emit_mlstm_p2(P, iob, NS, pb, tag="b2")
    ioc = dict(h_in=hB, h_out=hC, w_in=e["c_w_in"], w_out=e["c_w_out"], norm_gain=e["norm_gain"][2], gamma=e["c_gamma"], o_gain=e["c_o_gain"],
               ident=e["ident"], bdmask=e["bdmask"], rst=e["rst"], rv_row=e["rv_row"], rv_col=e["rv_col"],
               state_in=Sc_in.rearrange("p (h e) -> p h e", h=8), state_out=stC[:, 0:1024].rearrange("p (h e) -> p h e", h=8), atot_out=stC[:, 1024:1032])
    with P.scope():
        emit_hgrn(P, ioc, NS, pb, tag="cs", state_only=True)
    P.allgather(stC, gC)
    with P.scope():
        sel = T(P, "selpc", [128, 8], F32)
        P.dma("sp", sel[:], e["selp"], writes=[sel.b])
        S = T(P, "cc_S", [128, 8, 128], F32)
        P.op(P.dve, lambda: nc.vector.memset(S[:], 0.0), writes=[S.b])
        Sl = [T(P, f"cc_Sl{i}", [128, 1032], F32) for i in range(2)]
        E_ = T(P, "cc_E", [128, 8], F32)
        for i in range(8):
            sl = Sl[i % 2]
            P.dma("sp", sl[:], gC[i * 128:(i + 1) * 128, :], writes=[sl.b])
            P.op(P.act, lambda i=i, sl=sl: nc.scalar.activation(out=E_[:], in_=sl[:, 1024:1032], func=AF.Exp, scale=sel[:, i:i + 1]),
                 reads=[sl.b, sel.b], writes=[E_.b])
            P.op(P.dve, lambda: nc.vector.tensor_tensor(out=S[:], in0=S[:], in1=E_[:].unsqueeze(2).broadcast_to([128, 8, 128]), op=ALU.mult),
                 reads=[S.b, E_.b], writes=[S.b])
            P.op(P.dve, lambda i=i, sl=sl: nc.vector.scalar_tensor_tensor(out=S[:].rearrange("p h e -> p (h e)"), in0=sl[:, 0:1024], scalar=sel[:, i:i + 1],
                                                                          in1=S[:].rearrange("p h e -> p (h e)"), op0=ALU.mult, op1=ALU.add),
                 reads=[sl.b, sel.b, S.b], writes=[S.b])
        P.dma("sp", Sc_in, S[:].rearrange("p h e -> p (h e)"), reads=[S.b])
    with P.scope():
        emit_hgrn(P, ioc, NS, pb, tag="cf", state_only=False)
    with P.scope():
        P.dma("sp", hh, hC[32 * 128:33 * 128, :])
    P.allgather(hh, gH)
    with P.scope():
        sel = T(P, "selh", [128, 9], F32)
        P.dma("sp", sel[:], e["sel9"], writes=[sel.b])
        acc = _sel_accumulate(P, nc, "sh", gH, 128, 128, 1024, sel, 8, extra=hC[0:128, :])
        P.dma("sp", hC[0:128, :], acc[:], reads=[acc.b])
    with P.scope():
        outf = lambda s: (hD0 if s == 0 else e["out"][(s - 1) * 128:s * 128, :])
        emit_attn(P, dict(h_in=hC, h_out=outf, w_in=e["a_w_in"][1], w_out=e["a_w_out"][1], norm_gain=e["norm_gain"][3],
                          q_gain=e["a_q_gain"][1], k_gain=e["a_k_gain"][1], sinks=e["a_sinks"][1], ident=e["ident"],
                          mask_cur=e["mask_cur"], mask_prev=e["mask_prev"], cos2=e["cos2_3"], sinS=e["sinS_3"], kvalid=e["kvalid_3"]),
                  NS, pb, tag="a3")
    P.barrier()
    print("fused program: n_instr =", P.n_instr)
    return P.finish()


def _rope_tables(NS, blk0):
    half = 32
    inv = (np.float32(10000.0) ** (-np.arange(half, dtype=np.float32) / np.float32(half))).astype(np.float32)
    grow = blk0 * 128 + np.arange(NS * 128)
    gtok = (grow - 112).astype(np.float32)
    ang = (gtok[:, None] * inv[None, :]).astype(np.float32)
    cos = np.cos(ang).astype(np.float32)
    sin = np.sin(ang).astype(np.float32)
    kv = (grow >= 112).astype(np.float32).reshape(NS, 128).T
    return np.concatenate([cos, cos], 1), np.concatenate([-sin, sin], 1), np.ascontiguousarray(kv)


def kernel(x, meta, norm_gain, a_w_in, a_q_gain, a_k_gain, a_sinks, a_w_out,
           b_w_in, b_conv_w, b_conv_b, b_gate_bias, b_h_gain, b_w_out,
           c_w_in, c_gamma, c_o_gain, c_w_out):
    f32 = lambda a: np.ascontiguousarray(np.asarray(a, dtype=np.float32))
    x = f32(x); meta = f32(meta)
    B = x.shape[0]
    NS = NSLOT
    hpad = np.zeros((B, 130 * 128, 1024), np.float32)
    hpad[:, 128 + 112:256] = meta[None]
    hpad[:, 256:] = x
    nc = build_fused(NS)
    ca = attn_consts(NS, 0, ml_dtypes)
    cc = hgrn_consts(NS, 0, ml_dtypes)
    cb = mlstm_consts(NS, 0, ml_dtypes)
    shared = dict(norm_gain=f32(norm_gain), a_w_in=f32(a_w_in), a_q_gain=f32(a_q_gain), a_k_gain=f32(a_k_gain), a_sinks=f32(a_sinks),
                  a_w_out=f32(a_w_out), b_w_in=f32(b_w_in[0]), b_conv_w=f32(b_conv_w[0]), b_conv_b=f32(b_conv_b[0]),
                  b_gate_bias=f32(b_gate_bias[0]), b_h_gain=f32(b_h_gain[0]), b_w_out=f32(b_w_out[0]),
                  c_w_in=f32(c_w_in[0]), c_gamma=f32(c_gamma), c_o_gain=f32(c_o_gain[0]), c_w_out=f32(c_w_out[0]),
                  ident=ca["ident"], identf=cb["identf"], mask_cur=ca["mask_cur"], mask_prev=ca["mask_prev"],
                  bdmask=cc["bdmask"], rst=cc["rst"])
    in_maps = []
    cores = [(b, q) for b in range(B) for q in range(4)]
    for ci, (b, q) in enumerate(cores):
        m = dict(shared)
        r0 = 32 * q * 128
        m["x_in"] = np.ascontiguousarray(hpad[b, r0:r0 + (NS + 1) * 128])
        m["cos2_0"], m["sinS_0"], m["kvalid_0"] = _rope_tables(NS + 1, 32 * q - 1)
        m["cos2_3"], m["sinS_3"], m["kvalid_3"] = _rope_tables(NS, 32 * q)
        cq = mlstm_consts(NS, q, ml_dtypes)
        hq = hgrn_consts(NS, q, ml_dtypes)
        m["rv8"] = cq["rv8"]; m["rv_col"] = cq["rv_col"]; m["rv_row"] = hq["rv_row"]
        s8 = np.zeros((128, 8), np.float32)
        s9 = np.zeros((128, 9), np.float32)
        if q > 0:
            s8[:, ci - 1] = 1.0
            s9[:, ci - 1] = 1.0
        else:
            s9[:, 8] = 1.0
        sp_ = np.zeros((128, 8), np.float32)
        for j in range(q):
            sp_[:, 4 * b + j] = 1.0
        m["sel8"] = s8; m["sel9"] = s9; m["selp"] = sp_
        in_maps.append(m)
    res = run_bass_kernel_spmd(nc, in_maps, core_ids=list(range(8)))
    out = np.zeros((B, 16384, 1024), np.float32)
    for ci, (b, q) in enumerate(cores):
        out[b, q * 4096:(q + 1) * 4096] = res.results[ci]["out"]
    return out
```

```python
import math
import contextlib
import numpy as np
import concourse.bass as bass
import concourse.mybir as mybir

F32 = mybir.dt.float32
BF16 = mybir.dt.bfloat16
I32 = mybir.dt.int32
AF = mybir.ActivationFunctionType
ALU = mybir.AluOpType
AX = mybir.AxisListType


class Buf:
    __slots__ = ("name", "w", "r")

    def __init__(self, name):
        self.name = name
        self.w = None
        self.r = {}


class Eng:
    def __init__(self, name, h, sem, step=1, same_sync=False):
        self.name = name
        self.h = h
        self.sem = sem
        self.step = step
        self.count = 0
        self.seen = {}
        self.same_sync = same_sync


class Prog:
    def __init__(self, n_dma_sems=8, same_sync=False):
        self.nc = bass.Bass("TRN2", target_bir_lowering=False)
        self.es = contextlib.ExitStack()
        nc = self.nc
        self.n_instr = 0
        mk = lambda n: self.es.enter_context(nc.semaphore(n))
        self.pe = Eng("pe", nc.tensor, mk("s_pe"))
        self.act = Eng("act", nc.scalar, mk("s_act"), same_sync=same_sync)
        self.dve = Eng("dve", nc.vector, mk("s_dve"), same_sync=same_sync)
        self.pool = Eng("pool", nc.gpsimd, mk("s_pool"), same_sync=same_sync)
        self.sp = Eng("sp", nc.sync, mk("s_sp"))
        self.engs = [self.pe, self.act, self.dve, self.pool, self.sp]
        self.dma_sems = {}
        for q in ("sp", "pool", "act"):
            self.dma_sems[q] = [Eng(f"dma_{q}{i}", None, mk(f"d_{q}{i}"), step=16)
                                for i in range(n_dma_sems)]
        self.dma_rr = {"sp": 0, "pool": 0, "act": 0}
        self.cc = Eng("cc", None, mk("s_cc"))
        self.tiles = {}
        self.defer = True
        self.LAT_X = 1.0
        self.LAT_S = 0.3
        self.pending = []
        self.EST = {"pe": 0.22, "act": 0.65, "dve": 0.9, "pool": 1.5, "sp": 0.1}

    def dram(self, name, shape, dtype, kind):
        return self.nc.dram_tensor(name, list(shape), dtype, kind=kind).ap()

    def sbuf(self, name, shape, dtype):
        t = self.es.enter_context(self.nc.sbuf_tensor(name, list(shape), dtype))
        return t

    def psum(self, name, shape, dtype=F32):
        t = self.es.enter_context(self.nc.psum_tensor(name, list(shape), dtype))
        return t

    def _wait(self, eng, deps):
        best = {}
        for (e, c) in deps:
            if e is eng and not eng.same_sync:
                continue
            if best.get(e, 0) < c:
                best[e] = c
        for e, c in best.items():
            if eng.seen.get(e, 0) >= c:
                continue
            eng.h.wait_ge(e.sem, c)
            self.n_instr += 1
            eng.seen[e] = c

    @staticmethod
    def _deps(reads, writes):
        deps = []
        for b in reads:
            if b.w is not None:
                deps.append(b.w)
        for b in writes:
            if b.w is not None:
                deps.append(b.w)
            deps.extend(b.r.items())
        return deps

    def op(self, eng, fn, reads=(), writes=(), est=None):
        if self.defer:
            import sys as _s
            self.pending.append(dict(kind="op", eng=eng, fn=fn, reads=list(reads), writes=list(writes), line=_s._getframe(1).f_lineno,
                                     est=est if est is not None else self.EST[eng.name]))
            return None
        self._wait(eng, self._deps(reads, writes))
        ins = fn()
        eng.count += 1
        ins.then_inc(eng.sem, 1)
        self.n_instr += 1
        me = (eng, eng.count)
        for b in reads:
            if b.r.get(eng, 0) < eng.count:
                b.r[eng] = eng.count
        for b in writes:
            b.w = me
            b.r = {}
        return ins

    def dma(self, q, out, in_, reads=(), writes=(), est=3.0, **kw):
        if self.defer:
            import sys as _s
            self.pending.append(dict(kind="dma", q=q, out=out, in_=in_, kw=kw, reads=list(reads), writes=list(writes), est=est, line=_s._getframe(1).f_lineno,
                                     eng={"sp": self.sp, "pool": self.pool, "act": self.act}[q]))
            return None
        eng = {"sp": self.sp, "pool": self.pool, "act": self.act}[q]
        sems = self.dma_sems[q]
        ds = sems[self.dma_rr[q] % len(sems)]
        self.dma_rr[q] += 1
        deps = self._deps(reads, writes)
        if ds.count > 0:
            deps.append((ds, ds.count))
        self._wait(eng, deps)
        ins = eng.h.dma_start(out=out, in_=in_, **kw)
        ds.count += 16
        ins.then_inc(ds.sem, 16)
        self.n_instr += 1
        me = (ds, ds.count)
        for b in reads:
            if b.r.get(ds, 0) < ds.count:
                b.r[ds] = ds.count
        for b in writes:
            b.w = me
            b.r = {}
        return me

    def flush(self):
        ops = self.pending
        self.pending = []
        if not ops:
            return
        n = len(ops)
        lastw = {}
        readers = {}
        preds = [None] * n
        ext = [None] * n
        for i, o in enumerate(ops):
            p = set()
            e = []
            for b in o["reads"]:
                if id(b) in lastw:
                    p.add(lastw[id(b)])
                elif b.w is not None:
                    e.append(b.w)
            for b in o["writes"]:
                if id(b) in lastw:
                    p.add(lastw[id(b)])
                else:
                    if b.w is not None:
                        e.append(b.w)
                    e.extend(b.r.items())
                p.update(readers.get(id(b), ()))
            p.discard(i)
            preds[i] = p
            ext[i] = e
            for b in o["reads"]:
                readers.setdefault(id(b), set()).add(i)
            for b in o["writes"]:
                lastw[id(b)] = i
                readers[id(b)] = set()
        succs = [[] for _ in range(n)]
        npred = [0] * n
        for i in range(n):
            npred[i] = len(preds[i])
            for p in preds[i]:
                succs[p].append(i)
        import heapq
        qname = lambda o: o["eng"].name
        ready = {}
        fin = [0.0] * n
        dready = [0.0] * n
        for i in range(n):
            if npred[i] == 0:
                heapq.heappush(ready.setdefault(qname(ops[i]), []), (0.0, i))
        etime = {}
        elast = {}
        why = [None] * n
        dwhy = [None] * n
        order = []
        remaining = n
        while remaining:
            best = None
            for en, hp in ready.items():
                if not hp:
                    continue
                dr, i = hp[0]
                st = max(dr, etime.get(en, 0.0))
                if best is None or (st, i) < (best[0], best[2]):
                    best = (st, en, i)
            st, en, i = best
            heapq.heappop(ready[en])
            o = ops[i]
            why[i] = dwhy[i] if dready[i] >= etime.get(en, 0.0) else elast.get(en)
            elast[en] = i
            issue = o["est"] if o["kind"] == "op" else 0.15
            etime[en] = st + issue
            fin[i] = st + o["est"]
            order.append(i)
            remaining -= 1
            for j in succs[i]:
                npred[j] -= 1
                lat = self.LAT_X if ops[j]["eng"] is not ops[i]["eng"] else (self.LAT_S if ops[i]["eng"].same_sync else 0.0)
                if fin[i] + lat > dready[j]:
                    dready[j] = fin[i] + lat
                    dwhy[j] = i
                if npred[j] == 0:
                    heapq.heappush(ready.setdefault(qname(ops[j]), []), (dready[j], j))
        if getattr(self, "verbose", False):
            busy = {}
            for i in range(n):
                busy[qname(ops[i])] = busy.get(qname(ops[i]), 0.0) + ops[i]["est"]
            print(f"[flush] ops={n} sim_makespan={max(fin):.1f}us busy=" + " ".join(f"{k}:{v:.0f}" for k, v in busy.items()))
        if getattr(self, "crit", False):
            i = max(range(n), key=lambda k: fin[k])
            path = []
            while i is not None:
                path.append(i)
                i = why[i]
            path.reverse()
            import collections
            agg = collections.Counter()
            for i in path:
                agg[(qname(ops[i]), ops[i]["line"])] += ops[i]["est"]
            print("critical path: %d ops" % len(path))
            for (en, ln), t in sorted(agg.items(), key=lambda kv: -kv[1])[:25]:
                print(f"   {en:5s} line {ln:4d}  {t:8.1f}us")
        if getattr(self, "sim_only", False):
            return
        tok = [None] * n
        for i in order:
            o = ops[i]
            eng = o["eng"]
            deps = list(ext[i]) + [tok[p] for p in preds[i]]
            if o["kind"] == "op":
                self._wait(eng, deps)
                ins = o["fn"]()
                eng.count += 1
                ins.then_inc(eng.sem, 1)
                self.n_instr += 1
                tok[i] = (eng, eng.count)
            else:
                sems = self.dma_sems[o["q"]]
                ds = sems[self.dma_rr[o["q"]] % len(sems)]
                self.dma_rr[o["q"]] += 1
                if ds.count > 0:
                    deps.append((ds, ds.count))
                self._wait(eng, deps)
                ins = eng.h.dma_start(out=o["out"], in_=o["in_"], **o["kw"])
                ds.count += 16
                ins.then_inc(ds.sem, 16)
                self.n_instr += 1
                tok[i] = (ds, ds.count)
        for i, o in enumerate(ops):
            pass
        last_touch = {}
        for i in range(n):
            o = ops[i]
            for b in o["reads"]:
                last_touch.setdefault(id(b), (b, [], []))
            for b in o["writes"]:
                last_touch.setdefault(id(b), (b, [], []))
        for bid, (b, _, _) in last_touch.items():
            if bid in lastw:
                b.w = tok[lastw[bid]]
                b.r = {}
            for ridx in readers.get(bid, ()):
                e_, c_ = tok[ridx]
                if b.r.get(e_, 0) < c_:
                    b.r[e_] = c_

    def wait_all(self, eng, bufs):
        self.flush()
        deps = []
        for b in bufs:
            if b.w is not None:
                deps.append(b.w)
            deps.extend(b.r.items())
        self._wait(eng, deps)

    def cc_dram(self, name, shape, dtype):
        return self.nc.dram_tensor(name, list(shape), dtype).ap()

    def allgather(self, src, dst, n_cores=8):
        self.barrier()
        ins = self.nc.gpsimd.collective_compute("AllGather", ALU.bypass, replica_groups=[list(range(n_cores))],
                                                ins=[src.opt()], outs=[dst.opt()])
        self.cc.count += 1
        ins.then_inc(self.cc.sem)
        self.n_instr += 1
        self.barrier()

    def barrier(self):
        self.flush()
        for X in self.engs:
            deps = [(Y, Y.count) for Y in self.engs if Y is not X and Y.count]
            deps += [(d, d.count) for q in self.dma_sems.values() for d in q if d.count]
            if self.cc.count:
                deps.append((self.cc, self.cc.count))
            self._wait(X, deps)

    @contextlib.contextmanager
    def scope(self):
        outer = self.es
        self.es = contextlib.ExitStack()
        try:
            yield
        finally:
            self.barrier()
            self.es.close()
            self.es = outer

    def finish(self):
        self.flush()
        self.es.close()
        return self.nc


EPS = 1e-6
D = 1024


class Stop(Exception):
    pass

LIMIT = [None]

def ck(n):
    if LIMIT[0] == n:
        raise Stop()


class T:
    def __init__(self, P, name, shape, dtype, psum=False):
        self.t = (P.psum if psum else P.sbuf)(name, shape, dtype)
        self.b = Buf(name)

    def __getitem__(self, k):
        return self.t[k]


def load_w_bf16(P, dst, w_dram, K, N, q="pool"):
    for kc in range(K // 128):
        for c0 in range(0, N, 2048):
            c1 = min(N, c0 + 2048)
            P.dma(q, dst[:, kc, c0:c1], w_dram[kc * 128:(kc + 1) * 128, c0:c1], writes=[dst.b])


def emit_attn(P, io, NS, pb, tag="a", NB=2):
    nc = P.nc
    n = lambda s: f"{tag}_{s}"
    Win = T(P, n("Win"), [128, 8, 2560], BF16)
    Wout = T(P, n("Wout"), [128, 8, 1024], BF16)
    load_w_bf16(P, Win, io["w_in"], 1024, 2560)
    load_w_bf16(P, Wout, io["w_out"], 1024, 1024)
    ng = T(P, n("ng"), [128, 1024], F32)
    P.dma("sp", ng[:], io["norm_gain"].partition_broadcast(128), writes=[ng.b])
    qkg = T(P, n("qkg"), [128, 20, 64], F32)
    gtmp = T(P, n("gtmp"), [128, 2, 64], F32)
    P.dma("sp", gtmp[:, 0, :], io["q_gain"].partition_broadcast(128), writes=[gtmp.b])
    P.dma("sp", gtmp[:, 1, :], io["k_gain"].partition_broadcast(128), writes=[gtmp.b])
    P.op(P.dve, lambda: nc.vector.tensor_copy(out=qkg[:, 0:16, :], in_=gtmp[:, 0:1, :].broadcast_to([128, 16, 64])),
         reads=[gtmp.b], writes=[qkg.b])
    P.op(P.dve, lambda: nc.vector.tensor_copy(out=qkg[:, 16:20, :], in_=gtmp[:, 1:2, :].broadcast_to([128, 4, 64])),
         reads=[gtmp.b], writes=[qkg.b])
    esink = T(P, n("esink"), [128, 16], F32)
    P.dma("sp", esink[:], io["sinks"].partition_broadcast(128), writes=[esink.b])
    P.op(P.act, lambda: nc.scalar.activation(out=esink[:], in_=esink[:], func=AF.Exp), reads=[esink.b], writes=[esink.b])
    ident = T(P, n("ident"), [128, 128], BF16)
    P.dma("sp", ident[:], io["ident"], writes=[ident.b])
    m01 = T(P, n("m01"), [128, 2, 128], BF16)
    P.dma("sp", m01[:, 0, :], io["mask_cur"], writes=[m01.b])
    P.dma("sp", m01[:, 1, :], io["mask_prev"], writes=[m01.b])
    mcur = T(P, n("mcur"), [128, 4, 128], BF16)
    mprev = T(P, n("mprev"), [128, 4, 128], BF16)
    for mi, mt in enumerate((mcur, mprev)):
        P.op(P.dve, lambda mi=mi, mt=mt: nc.vector.tensor_scalar(out=mt[:], in0=m01[:, mi:mi + 1, :].broadcast_to([128, 4, 128]),
                                                                 scalar1=30000.0, scalar2=-30000.0, op0=ALU.mult, op1=ALU.add),
             reads=[m01.b], writes=[mt.b])
    kval = T(P, n("kval"), [128, NS], F32)
    P.dma("sp", kval[:], io["kvalid"], writes=[kval.b])
    cos2 = T(P, n("cos2"), [128, NS, 64], F32)
    sinS = T(P, n("sinS"), [128, NS, 64], F32)
    P.dma("sp", cos2[:], io["cos2"].rearrange("(s p) d -> p s d", p=128), writes=[cos2.b])
    P.dma("sp", sinS[:], io["sinS"].rearrange("(s p) d -> p s d", p=128), writes=[sinS.b])

    ck(1)
    hblk = [T(P, n(f"hblk{i}"), [128, 1024], F32) for i in range(2)]
    hnew = [T(P, n(f"hnew{i}"), [128, 1024], F32) for i in range(2)]
    st_l = [T(P, n(f"st{i}"), [128, 4], F32) for i in range(NB)]
    u_l = [T(P, n(f"u{i}"), [128, 1024], BF16) for i in range(NB)]
    uT = [T(P, n(f"uT{i}"), [128, 8, 128], BF16) for i in range(2)]
    qk_l = [T(P, n(f"qk{i}"), [128, 20, 64], F32) for i in range(NB)]
    qss_l = [T(P, n(f"qss{i}"), [128, 20], F32) for i in range(NB)]
    qrs_l = [T(P, n(f"qrs{i}"), [128, 20], F32) for i in range(NB)]
    qg_l = [T(P, n(f"qg{i}"), [128, 20, 64], F32) for i in range(NB)]
    ra_l = [T(P, n(f"ra{i}"), [128, 20, 64], F32) for i in range(NB)]
    rb_l = [T(P, n(f"rb{i}"), [128, 20, 64], F32) for i in range(1)]
    qkr_l = [T(P, n(f"qkr{i}"), [128, 20, 64], BF16) for i in range(NB)]
    sz = [T(P, n(f"sz{i}"), [128, 1024], BF16) for i in range(2)]
    Vaug = [T(P, n(f"Vaug{i}"), [128, 4, 66], BF16) for i in range(3)]
    qT = [T(P, n(f"qT{i}"), [64, 16, 128], BF16) for i in range(2)]
    kT = [T(P, n(f"kT{i}"), [64, 4, 128], BF16) for i in range(3)]
    E_l = [[T(P, n(f"E{i}_{j}"), [128, 4, 128], BF16) for i in range(8)] for j in range(1)]
    EM_l = [[T(P, n(f"EM{i}_{j}"), [128, 4, 128], BF16) for i in range(8)] for j in range(1)]
    Osb_l = [T(P, n(f"Osb{i}"), [128, 16, 65], F32) for i in range(1)]
    den_l = [T(P, n(f"den{i}"), [128, 16], F32) for i in range(NB)]
    rden_l = [T(P, n(f"rden{i}"), [128, 16], F32) for i in range(NB)]
    g1_l = [T(P, n(f"g1{i}"), [128, 16, 64], F32) for i in range(1)]
    g_l = [T(P, n(f"g{i}"), [128, 1024], BF16) for i in range(NB)]
    gT_l = [T(P, n(f"gT{i}"), [128, 8, 128], BF16) for i in range(NB)]

    tp = pb[0]
    proj = [pb[1], pb[2]]
    sc = [pb[3], pb[4]]
    ob = [pb[5], pb[6], pb[7]]
    qtb = [pb[1], pb[2], pb[0]]
    OH = [(0, 6), (6, 12), (12, 16)]
    h_in, h_out = io["h_in"], io["h_out"]
    cnt = {"proj": 0, "sc": 0}

    def stageA(s):
        st = st_l[s % NB]; u = u_l[s % NB]; qk = qk_l[s % NB]; qss = qss_l[s % NB]; qrs = qrs_l[s % NB]; qg = qg_l[s % NB]; ra = ra_l[s % NB]; rb = rb_l[0]; qkr = qkr_l[s % NB]; Osb = Osb_l[0]; den = den_l[s % NB]; rden = rden_l[s % NB]; g1 = g1_l[0]; g = g_l[s % NB]; gT = gT_l[s % NB]
        E = E_l[0]; EM = EM_l[0]; junk = ra
        hb = hblk[s % 2]
        P.dma("sp", hb[:], h_in[s * 128:(s + 1) * 128, :], writes=[hb.b])
        P.op(P.act, lambda: nc.scalar.activation(out=junk[:].rearrange("p h d -> p (h d)")[:, 0:1024], in_=hb[:], func=AF.Square, accum_out=st[:, 0:1]),
             reads=[hb.b], writes=[junk.b, st.b])
        P.op(P.act, lambda: nc.scalar.activation(out=st[:, 1:2], in_=st[:, 0:1], func=AF.Sqrt, scale=1.0 / D, bias=EPS),
             reads=[st.b], writes=[st.b])
        P.op(P.dve, lambda: nc.vector.reciprocal(out=st[:, 2:3], in_=st[:, 1:2]), reads=[st.b], writes=[st.b])
        P.op(P.dve, lambda: nc.vector.scalar_tensor_tensor(out=u[:], in0=hb[:], scalar=st[:, 2:3], in1=ng[:],
                                                           op0=ALU.mult, op1=ALU.mult),
             reads=[hb.b, st.b, ng.b], writes=[u.b])
        tpb = tp[:].bitcast(BF16)
        for k in range(8):
            P.op(P.pe, lambda k=k: nc.tensor.transpose(tpb[:, k * 128:(k + 1) * 128], u[:, k * 128:(k + 1) * 128], ident[:]),
                 reads=[u.b, ident.b], writes=[tp.b])
        ut = uT[s % 2]
        P.op(P.act, lambda: nc.scalar.activation(out=ut[:].rearrange("p k t -> p (k t)"), in_=tpb[:, 0:1024], func=AF.Copy),
             reads=[tp.b], writes=[ut.b])
        ck(2)
        va = Vaug[s % 3]
        for j in range(5):
            bank = proj[cnt["proj"] % 2]
            cnt["proj"] += 1
            for k in range(8):
                P.op(P.pe, lambda k=k, j=j, bank=bank: nc.tensor.matmul(bank[:], lhsT=ut[:, k, :], rhs=Win[:, k, j * 512:(j + 1) * 512],
                                                                      start=(k == 0), stop=(k == 7)),
                     reads=[ut.b, Win.b], writes=[bank.b])
            ck(20 + 2 * j)
            qkf = qk[:].rearrange("p h d -> p (h d)")
            if j < 2:
                P.op(P.act, lambda j=j, bank=bank: nc.scalar.activation(out=qkf[:, j * 512:(j + 1) * 512], in_=bank[:], func=AF.Copy),
                     reads=[bank.b], writes=[qk.b])
            elif j == 2:
                P.op(P.act, lambda bank=bank: nc.scalar.activation(out=qkf[:, 1024:1280], in_=bank[:, 0:256], func=AF.Copy),
                     reads=[bank.b], writes=[qk.b])
                P.op(P.act, lambda bank=bank: nc.scalar.activation(out=va[:, :, 0:64], in_=bank[:, 256:512].rearrange("p (h d) -> p h d", h=4),
                                                                   func=AF.Copy, scale=kval[:, s:s + 1]),
                     reads=[bank.b, kval.b], writes=[va.b])
                P.op(P.dve, lambda: nc.vector.tensor_copy(out=va[:, :, 64:65], in_=kval[:, s:s + 1].unsqueeze(1).broadcast_to([128, 4, 1])),
                     reads=[kval.b], writes=[va.b])
            else:
                z = sz[s % 2]
                P.op(P.act, lambda j=j, bank=bank, z=z: nc.scalar.activation(out=z[:, (j - 3) * 512:(j - 2) * 512], in_=bank[:], func=AF.Silu),
                     reads=[bank.b], writes=[z.b])
            ck(21 + 2 * j)
        ck(3)
        P.op(P.dve, lambda: nc.vector.tensor_tensor(out=junk[:], in0=qk[:], in1=qk[:], op=ALU.mult),
             reads=[qk.b], writes=[junk.b])
        P.op(P.dve, lambda: nc.vector.tensor_reduce(out=qss[:], in_=junk[:], axis=AX.X, op=ALU.add),
             reads=[junk.b], writes=[qss.b])
        P.op(P.act, lambda: nc.scalar.activation(out=qss[:], in_=qss[:], func=AF.Sqrt, scale=1.0 / 64, bias=EPS),
             reads=[qss.b], writes=[qss.b])
        P.op(P.dve, lambda: nc.vector.reciprocal(out=qrs[:], in_=qss[:]), reads=[qss.b], writes=[qrs.b])
        P.op(P.dve, lambda: nc.vector.tensor_tensor(out=qg[:], in0=qk[:], in1=qrs[:].unsqueeze(2).broadcast_to([128, 20, 64]), op=ALU.mult),
             reads=[qk.b, qrs.b], writes=[qg.b])
        P.op(P.pool, lambda: nc.gpsimd.tensor_tensor(out=qg[:], in0=qg[:], in1=qkg[:], op=ALU.mult),
             reads=[qg.b, qkg.b], writes=[qg.b])
        P.op(P.dve, lambda: nc.vector.tensor_tensor(out=ra[:], in0=qg[:], in1=cos2[:, s:s + 1, :].broadcast_to([128, 20, 64]), op=ALU.mult),
             reads=[qg.b, cos2.b], writes=[ra.b])
        P.op(P.pool, lambda: nc.gpsimd.tensor_tensor(out=rb[:, :, 0:32], in0=qg[:, :, 32:64],
                                                      in1=sinS[:, s:s + 1, 0:32].broadcast_to([128, 20, 32]), op=ALU.mult),
             reads=[qg.b, sinS.b], writes=[rb.b])
        P.op(P.pool, lambda: nc.gpsimd.tensor_tensor(out=rb[:, :, 32:64], in0=qg[:, :, 0:32],
                                                      in1=sinS[:, s:s + 1, 32:64].broadcast_to([128, 20, 32]), op=ALU.mult),
             reads=[qg.b, sinS.b], writes=[rb.b])
        P.op(P.dve, lambda: nc.vector.tensor_tensor(out=qkr[:], in0=ra[:], in1=rb[:], op=ALU.add),
             reads=[ra.b, rb.b], writes=[qkr.b])
        ck(4)
        for hh in range(20):
            bank = qtb[hh // 8]
            bb = bank[:].bitcast(BF16)
            P.op(P.pe, lambda hh=hh, bb=bb: nc.tensor.transpose(bb[0:64, (hh % 8) * 128:(hh % 8 + 1) * 128], qkr[:, hh, :], ident[:]),
                 reads=[qkr.b, ident.b], writes=[bank.b])
        qt = qT[s % 2]
        kt = kT[s % 3]
        P.op(P.act, lambda: nc.scalar.activation(out=qt[:, 0:8, :].rearrange("p h t -> p (h t)"), in_=qtb[0][:].bitcast(BF16)[0:64, 0:1024], func=AF.Copy),
             reads=[qtb[0].b], writes=[qt.b])
        P.op(P.dve, lambda: nc.vector.tensor_copy(out=qt[:, 8:16, :].rearrange("p h t -> p (h t)"), in_=qtb[1][:].bitcast(BF16)[0:64, 0:1024]),
             reads=[qtb[1].b], writes=[qt.b])
        P.op(P.act, lambda: nc.scalar.activation(out=kt[:].rearrange("p h t -> p (h t)"), in_=qtb[2][:].bitcast(BF16)[0:64, 0:512], func=AF.Copy),
             reads=[qtb[2].b], writes=[kt.b])

    def stageB(s):
        st = st_l[s % NB]; u = u_l[s % NB]; qk = qk_l[s % NB]; qss = qss_l[s % NB]; qrs = qrs_l[s % NB]; qg = qg_l[s % NB]; ra = ra_l[s % NB]; rb = rb_l[0]; qkr = qkr_l[s % NB]; Osb = Osb_l[0]; den = den_l[s % NB]; rden = rden_l[s % NB]; g1 = g1_l[0]; g = g_l[s % NB]; gT = gT_l[s % NB]
        E = E_l[0]; EM = EM_l[0]; junk = ra
        ck(5)
        hb = hblk[s % 2]
        qt = qT[s % 2]
        kbs = ([] if s == 0 else [(kT[(s - 1) % 3], Vaug[(s - 1) % 3], mprev)]) + [(kT[s % 3], Vaug[s % 3], mcur)]
        nkb = len(kbs)
        for j in range(4):
            for ki, (kt, va, mk) in enumerate(kbs):
                idx = j * 2 + ki
                bank = sc[cnt["sc"] % 2]
                cnt["sc"] += 1
                P.op(P.pe, lambda bank=bank, kt=kt, j=j: nc.tensor.matmul(bank[:], lhsT=kt[:, j, :], rhs=qt[:, 4 * j:4 * j + 4, :].rearrange("p h t -> p (h t)"),
                                                                       start=True, stop=False),
                     reads=[kt.b, qt.b], writes=[bank.b])
                P.op(P.pe, lambda bank=bank, mk=mk: nc.tensor.matmul(bank[:], lhsT=ident[:], rhs=mk[:].rearrange("p h t -> p (h t)"), start=False, stop=True),
                     reads=[ident.b, mk.b], writes=[bank.b])
                P.op(P.act, lambda bank=bank, idx=idx: nc.scalar.activation(out=EM[idx][:].rearrange("p h t -> p (h t)"), in_=bank[:], func=AF.Exp, scale=0.125),
                     reads=[bank.b], writes=[EM[idx].b])
        ck(6)
        for h in range(16):
            j, gq = h // 4, h % 4
            bi = 0 if h < 6 else (1 if h < 12 else 2)
            bank = ob[bi]
            off = (h - OH[bi][0]) * 65
            for ki, (kt, va, mk) in enumerate(kbs):
                idx = j * 2 + ki
                P.op(P.pe, lambda bank=bank, off=off, idx=idx, gq=gq, va=va, j=j, ki=ki: nc.tensor.matmul(
                    bank[:, off:off + 65], lhsT=EM[idx][:, gq, :], rhs=va[:, j, 0:65], start=(ki == 0), stop=(ki == nkb - 1)),
                    reads=[EM[idx].b, va.b], writes=[bank.b])
        ck(7)
        for bi, (h0, h1) in enumerate(OH):
            nh = h1 - h0
            P.op(P.act, lambda bi=bi, h0=h0, h1=h1, nh=nh: nc.scalar.activation(out=Osb[:, h0:h1, :].rearrange("p h d -> p (h d)"), in_=ob[bi][:, 0:nh * 65], func=AF.Copy),
                 reads=[ob[bi].b], writes=[Osb.b])
        P.op(P.dve, lambda: nc.vector.tensor_tensor(out=den[:].unsqueeze(2), in0=Osb[:, :, 64:65], in1=esink[:].unsqueeze(2), op=ALU.add),
             reads=[Osb.b, esink.b], writes=[den.b])
        P.op(P.dve, lambda: nc.vector.reciprocal(out=rden[:], in_=den[:]), reads=[den.b], writes=[rden.b])
        P.op(P.dve, lambda: nc.vector.tensor_tensor(out=g1[:], in0=Osb[:, :, 0:64], in1=rden[:].unsqueeze(2).broadcast_to([128, 16, 64]), op=ALU.mult),
             reads=[Osb.b, rden.b], writes=[g1.b])
        z = sz[s % 2]
        P.op(P.pool, lambda: nc.gpsimd.tensor_tensor(out=g[:], in0=g1[:].rearrange("p h d -> p (h d)"), in1=z[:], op=ALU.mult),
             reads=[g1.b, z.b], writes=[g.b])
        ck(8)
        tpb = tp[:].bitcast(BF16)
        for k in range(8):
            P.op(P.pe, lambda k=k: nc.tensor.transpose(tpb[:, k * 128:(k + 1) * 128], g[:, k * 128:(k + 1) * 128], ident[:]),
                 reads=[g.b, ident.b], writes=[tp.b])
        P.op(P.act, lambda: nc.scalar.activation(out=gT[:].rearrange("p k t -> p (k t)"), in_=tpb[:, 0:1024], func=AF.Copy),
             reads=[tp.b], writes=[gT.b])
        hn = hnew[s % 2]
        for c in range(2):
            bank = proj[cnt["proj"] % 2]
            cnt["proj"] += 1
            for k in range(8):
                P.op(P.pe, lambda k=k, c=c, bank=bank: nc.tensor.matmul(bank[:], lhsT=gT[:, k, :], rhs=Wout[:, k, c * 512:(c + 1) * 512],
                                                                      start=(k == 0), stop=(k == 7)),
                     reads=[gT.b, Wout.b], writes=[bank.b])
            P.op(P.dve, lambda c=c, bank=bank: nc.vector.scalar_tensor_tensor(out=hn[:, c * 512:(c + 1) * 512], in0=bank[:], scalar=kval[:, s:s + 1],
                                                                              in1=hb[:, c * 512:(c + 1) * 512], op0=ALU.mult, op1=ALU.add),
                 reads=[bank.b, kval.b, hb.b], writes=[hn.b])
        P.dma("sp", h_out(s) if callable(h_out) else h_out[s * 128:(s + 1) * 128, :], hn[:], reads=[hn.b])

    for s in range(NS + 1):
        if s < NS:
            stageA(s)
        if s >= 1:
            stageB(s - 1)
    return hnew


def attn_consts(NS, q, ml_dtypes):
    ident = np.eye(128, dtype=np.float32).astype(ml_dtypes.bfloat16)
    sidx = np.arange(128)[:, None]
    tidx = np.arange(128)[None, :]
    mask_cur = (sidx <= tidx).astype(np.float32).astype(ml_dtypes.bfloat16)
    mask_prev = (sidx > tidx).astype(np.float32).astype(ml_dtypes.bfloat16)
    half = 32
    inv = (np.float32(10000.0) ** (-np.arange(half, dtype=np.float32) / np.float32(half))).astype(np.float32)
    gtok = (32 * q * 128 + np.arange(NS * 128) - 112).astype(np.float32)
    ang = (gtok[:, None] * inv[None, :]).astype(np.float32)
    cos = np.cos(ang).astype(np.float32)
    sin = np.sin(ang).astype(np.float32)
    cos2 = np.concatenate([cos, cos], 1)
    sinS = np.concatenate([-sin, sin], 1)
    kvalid = np.ones((128, NS), np.float32)
    if q == 0:
        kvalid[:112, 0] = 0.0
    return dict(ident=ident, mask_cur=mask_cur, mask_prev=mask_prev, cos2=cos2, sinS=sinS, kvalid=kvalid)


def emit_hgrn(P, io, NS, pb, layer_idx=2, tag="c", state_only=False):
    nc = P.nc
    n = lambda s: f"{tag}_{s}"
    Win = T(P, n("Win"), [128, 8, 4096], BF16)
    Wout = T(P, n("Wout"), [128, 8, 1024], BF16)
    load_w_bf16(P, Win, io["w_in"], 1024, 4096)
    load_w_bf16(P, Wout, io["w_out"], 1024, 1024)
    ng = T(P, n("ng"), [128, 1024], F32)
    P.dma("sp", ng[:], io["norm_gain"].partition_broadcast(128), writes=[ng.b])
    og = T(P, n("og"), [128, 1024], F32)
    P.dma("sp", og[:], io["o_gain"].partition_broadcast(128), writes=[og.b])
    ident = T(P, n("ident"), [128, 128], BF16)
    P.dma("sp", ident[:], io["ident"], writes=[ident.b])
    bdm = T(P, n("bdm"), [128, 128], BF16)
    P.dma("sp", bdm[:], io["bdmask"], writes=[bdm.b])
    rst = T(P, n("rst"), [128, 1024], F32)
    P.dma("sp", rst[:], io["rst"], writes=[rst.b])
    rvrow = T(P, n("rvrow"), [128, 128], F32)
    P.dma("sp", rvrow[:], io["rv_row"], writes=[rvrow.b])
    rvcol = T(P, n("rvcol"), [128, NS], F32)
    P.dma("sp", rvcol[:], io["rv_col"], writes=[rvcol.b])
    gam = T(P, n("gam"), [128, 8, 4], F32)
    for l in range(4):
        P.dma("sp", gam[:, :, l], io["gamma"][l].rearrange("(h d) -> d h", d=128), writes=[gam.b], allow_slow_non_contiguous=True)
    P.op(P.act, lambda: nc.scalar.activation(out=gam[:], in_=gam[:], func=AF.Exp), reads=[gam.b], writes=[gam.b])
    lbt = T(P, n("lbt"), [128, 8, 4], F32)
    P.op(P.dve, lambda: nc.vector.tensor_reduce(out=lbt[:, :, 0], in_=gam[:], axis=AX.X, op=ALU.add), reads=[gam.b], writes=[lbt.b])
    P.op(P.dve, lambda: nc.vector.tensor_reduce(out=lbt[:, :, 1], in_=gam[:, :, 0:layer_idx], axis=AX.X, op=ALU.add), reads=[gam.b], writes=[lbt.b])
    P.op(P.dve, lambda: nc.vector.reciprocal(out=lbt[:, :, 0], in_=lbt[:, :, 0]), reads=[lbt.b], writes=[lbt.b])
    P.op(P.dve, lambda: nc.vector.tensor_tensor(out=lbt[:, :, 2], in0=lbt[:, :, 1], in1=lbt[:, :, 0], op=ALU.mult), reads=[lbt.b], writes=[lbt.b])
    P.op(P.dve, lambda: nc.vector.tensor_scalar(out=lbt[:, :, 3], in0=lbt[:, :, 2], scalar1=-1.0, scalar2=1.0, op0=ALU.mult, op1=ALU.add),
         reads=[lbt.b], writes=[lbt.b])
    S = T(P, n("S"), [128, 8, 128], F32)
    Sb = [Buf(n(f"S{h}")) for h in range(8)]
    P.dma("sp", S[:], io["state_in"], writes=[S.b] + Sb)
    SbfA = [T(P, n(f"SbfA{h}"), [128, 128], BF16) for h in range(8)]
    SbfB = [T(P, n(f"SbfB{h}"), [128, 128], BF16) for h in range(8)]
    for h in range(0 if not state_only else 8, 8):
        P.op(P.act, lambda h=h: nc.scalar.activation(out=SbfA[h][:], in_=S[:, h, :], func=AF.Copy), reads=[Sb[h]], writes=[SbfA[h].b])

    atot = T(P, n("atot"), [128, 8], F32)
    atmp = T(P, n("atmp"), [128, 8], F32)
    if state_only:
        P.op(P.dve, lambda: nc.vector.memset(atot[:], 0.0), writes=[atot.b])
    hblk = [T(P, n(f"hblk{i}"), [128, 1024], F32) for i in range(2)]
    hnew = [T(P, n(f"hnew{i}"), [128, 1024], F32) for i in range(2)]
    st = T(P, n("st"), [128, 4], F32)
    u = T(P, n("u"), [128, 1024], BF16)
    uT = [T(P, n(f"uT{i}"), [128, 8, 128], BF16) for i in range(2)]
    qs = T(P, n("qs"), [128, 8, 128], F32)
    sg = T(P, n("sg"), [128, 8, 128], F32)
    f = sg
    lf = T(P, n("lf"), [128, 8, 128], F32)
    kk = T(P, n("kk"), [128, 8, 128], F32)
    A = T(P, n("A"), [128, 8, 128], F32)
    eA = T(P, n("eA"), [128, 8, 128], F32)
    enA = T(P, n("enA"), [128, 8, 128], F32)
    eAl = T(P, n("eAl"), [128, 8, 2], F32)
    kef = T(P, n("kef"), [128, 8, 128], F32)
    qe = T(P, n("qe"), [128, 8, 128], BF16)
    ke = T(P, n("ke"), [128, 8, 128], BF16)
    kd = T(P, n("kd"), [128, 8, 128], BF16)
    kdT = T(P, n("kdT"), [128, 8, 128], BF16)
    V = T(P, n("V"), [128, 8, 128], BF16)
    sz = T(P, n("sz"), [128, 1024], BF16)
    attM = T(P, n("attM"), [128, 8, 128], BF16)
    osb = T(P, n("osb"), [128, 8, 128], F32)
    oss = T(P, n("oss"), [128, 8], F32)
    ors = T(P, n("ors"), [128, 8], F32)
    g1 = T(P, n("g1"), [128, 8, 128], F32)
    junk = T.__new__(T); junk.t = g1.t; junk.b = g1.b
    g = T(P, n("g"), [128, 1024], BF16)
    gT = T(P, n("gT"), [128, 8, 128], BF16)

    tp = pb[0]
    proj = [pb[1], pb[2]]
    attb = [pb[3], pb[4]]
    obk = [pb[5], pb[6]]
    h_in, h_out = io["h_in"], io["h_out"]
    cnt = {"proj": 0}
    fl = lambda t: t[:].rearrange("p h t -> p (h t)")

    def slot(s):
        hb = hblk[s % 2]
        P.dma("sp", hb[:], h_in[s * 128:(s + 1) * 128, :], writes=[hb.b])
        P.op(P.act, lambda: nc.scalar.activation(out=fl(junk), in_=hb[:], func=AF.Square, accum_out=st[:, 0:1]),
             reads=[hb.b], writes=[junk.b, st.b])
        P.op(P.act, lambda: nc.scalar.activation(out=st[:, 1:2], in_=st[:, 0:1], func=AF.Sqrt, scale=1.0 / D, bias=EPS),
             reads=[st.b], writes=[st.b])
        P.op(P.dve, lambda: nc.vector.reciprocal(out=st[:, 2:3], in_=st[:, 1:2]), reads=[st.b], writes=[st.b])
        P.op(P.dve, lambda: nc.vector.scalar_tensor_tensor(out=u[:], in0=hb[:], scalar=st[:, 2:3], in1=ng[:], op0=ALU.mult, op1=ALU.mult),
             reads=[hb.b, st.b, ng.b], writes=[u.b])
        tpb = tp[:].bitcast(BF16)
        for k in range(8):
            P.op(P.pe, lambda k=k: nc.tensor.transpose(tpb[:, k * 128:(k + 1) * 128], u[:, k * 128:(k + 1) * 128], ident[:]),
                 reads=[u.b, ident.b], writes=[tp.b])
        ut = uT[s % 2]
        P.op(P.act, lambda: nc.scalar.activation(out=fl(ut), in_=tpb[:, 0:1024], func=AF.Copy), reads=[tp.b], writes=[ut.b])
        for grp in range(2 if state_only else 0, 4):
            bank = proj[cnt["proj"] % 2]
            cnt["proj"] += 1
            for i in range(4):
                ti = grp * 4 + i
                typ, h = ti // 8, ti % 8
                col0 = (0 if typ == 0 else 1024) + h * 128
                for k in range(8):
                    P.op(P.pe, lambda k=k, i=i, col0=col0, bank=bank: nc.tensor.matmul(bank[:, i * 128:(i + 1) * 128], lhsT=Win[:, k, col0:col0 + 128],
                                                                                   rhs=ut[:, k, :], start=(k == 0), stop=(k == 7)),
                         reads=[ut.b, Win.b], writes=[bank.b])
            dst = qs if grp < 2 else sg
            fn = AF.Silu if grp < 2 else AF.Sigmoid
            h0 = (grp % 2) * 4
            P.op(P.act, lambda dst=dst, fn=fn, h0=h0, bank=bank: nc.scalar.activation(out=dst[:, h0:h0 + 4, :].rearrange("p h t -> p (h t)"), in_=bank[:], func=fn),
                 reads=[bank.b], writes=[dst.b])
        for c in range(2 if state_only else 4):
            bank = proj[cnt["proj"] % 2]
            cnt["proj"] += 1
            col0 = 2048 + c * 512
            for k in range(8):
                P.op(P.pe, lambda k=k, col0=col0, bank=bank: nc.tensor.matmul(bank[:], lhsT=ut[:, k, :], rhs=Win[:, k, col0:col0 + 512],
                                                                          start=(k == 0), stop=(k == 7)),
                     reads=[ut.b, Win.b], writes=[bank.b])
            if c < 2:
                P.op(P.act, lambda c=c, bank=bank: nc.scalar.activation(out=V[:, c * 4:c * 4 + 4, :].rearrange("p h t -> p (h t)"), in_=bank[:], func=AF.Copy),
                     reads=[bank.b], writes=[V.b])
            else:
                P.op(P.act, lambda c=c, bank=bank: nc.scalar.activation(out=sz[:, (c - 2) * 512:(c - 1) * 512], in_=bank[:], func=AF.Silu),
                     reads=[bank.b], writes=[sz.b])
        P.op(P.dve, lambda: nc.vector.tensor_tensor(out=f[:], in0=sg[:], in1=lbt[:, :, 3:4].broadcast_to([128, 8, 128]), op=ALU.mult),
             reads=[sg.b, lbt.b], writes=[f.b])
        P.op(P.dve, lambda: nc.vector.tensor_tensor(out=f[:], in0=f[:], in1=lbt[:, :, 2:3].broadcast_to([128, 8, 128]), op=ALU.add),
             reads=[f.b, lbt.b], writes=[f.b])
        P.op(P.act, lambda: nc.scalar.activation(out=fl(lf), in_=fl(f), func=AF.Ln), reads=[f.b], writes=[lf.b])
        P.op(P.dve, lambda: nc.vector.tensor_scalar(out=fl(kk), in0=fl(f), scalar1=-1.0, scalar2=1.0, op0=ALU.mult, op1=ALU.add),
             reads=[f.b], writes=[kk.b])
        if s == 0:
            rb = rvrow[:].unsqueeze(1).broadcast_to([128, 8, 128])
            P.op(P.dve, lambda: nc.vector.tensor_tensor(out=lf[:], in0=lf[:], in1=rb, op=ALU.mult), reads=[lf.b, rvrow.b], writes=[lf.b])
            P.op(P.dve, lambda: nc.vector.tensor_tensor(out=kk[:], in0=kk[:], in1=rb, op=ALU.mult), reads=[kk.b, rvrow.b], writes=[kk.b])
        P.op(P.dve, lambda: nc.vector.tensor_tensor_scan(out=fl(A), data0=rst[:], data1=fl(lf), initial=0.0, op0=ALU.mult, op1=ALU.add),
             reads=[rst.b, lf.b], writes=[A.b])
        P.op(P.act, lambda: nc.scalar.activation(out=fl(eA), in_=fl(A), func=AF.Exp), reads=[A.b], writes=[eA.b])
        P.op(P.act, lambda: nc.scalar.activation(out=fl(enA), in_=fl(A), func=AF.Exp, scale=-1.0), reads=[A.b], writes=[enA.b])
        Av = A[:].rearrange("p h (c t) -> p h c t", c=2)
        P.op(P.act, lambda: nc.scalar.activation(out=eAl[:].unsqueeze(3), in_=Av[:, :, :, 63:64], func=AF.Exp), reads=[A.b], writes=[eAl.b])
        if state_only:
            P.op(P.dve, lambda: nc.vector.tensor_tensor(out=atmp[:].unsqueeze(2), in0=Av[:, :, 0, 63:64], in1=Av[:, :, 1, 63:64], op=ALU.add),
                 reads=[A.b], writes=[atmp.b])
            P.op(P.dve, lambda: nc.vector.tensor_tensor(out=atot[:], in0=atot[:], in1=atmp[:], op=ALU.add), reads=[atot.b, atmp.b], writes=[atot.b])
        if not state_only:
            P.op(P.pool, lambda: nc.gpsimd.tensor_tensor(out=qe[:], in0=qs[:], in1=eA[:], op=ALU.mult), reads=[qs.b, eA.b], writes=[qe.b])
        P.op(P.dve, lambda: nc.vector.tensor_tensor(out=kef[:], in0=kk[:], in1=enA[:], op=ALU.mult), reads=[kk.b, enA.b], writes=[kef.b])
        P.op(P.pool, lambda: nc.gpsimd.tensor_copy(out=ke[:], in_=kef[:]), reads=[kef.b], writes=[ke.b])
        P.op(P.dve, lambda: nc.vector.tensor_tensor(out=kd[:].rearrange("p h (c t) -> p h c t", c=2), in0=kef[:].rearrange("p h (c t) -> p h c t", c=2),
                                                    in1=eAl[:].unsqueeze(3).broadcast_to([128, 8, 2, 64]), op=ALU.mult),
             reads=[kef.b, eAl.b], writes=[kd.b])
        for h in range(8):
            P.op(P.pe, lambda h=h: nc.tensor.transpose(tpb[:, h * 128:(h + 1) * 128], kd[:, h, :], ident[:]),
                 reads=[kd.b, ident.b], writes=[tp.b])
        P.op(P.act, lambda: nc.scalar.activation(out=fl(kdT), in_=tpb[:, 0:1024], func=AF.Copy), reads=[tp.b], writes=[kdT.b])
        for h in range(8):
            bank = proj[h // 4]
            P.op(P.pe, lambda h=h, bank=bank: nc.tensor.matmul(bank[:, (h % 4) * 128:(h % 4 + 1) * 128], lhsT=kdT[0:64, h, :], rhs=V[0:64, h, :], start=True, stop=True),
                 reads=[kdT.b, V.b], writes=[bank.b])
        for h in range(8):
            bank = proj[h // 4]
            P.op(P.dve, lambda h=h, bank=bank: nc.vector.scalar_tensor_tensor(out=S[:, h, :], in0=S[:, h, :], scalar=eAl[:, h, 0:1],
                                                                              in1=bank[:, (h % 4) * 128:(h % 4 + 1) * 128], op0=ALU.mult, op1=ALU.add),
                 reads=[Sb[h], eAl.b, bank.b], writes=[Sb[h]])
            if not state_only:
                P.op(P.act, lambda h=h: nc.scalar.activation(out=SbfB[h][:], in_=S[:, h, :], func=AF.Copy), reads=[Sb[h]], writes=[SbfB[h].b])
        if not state_only:
            for h in range(8):
                bank = attb[h // 4]
                P.op(P.pe, lambda h=h, bank=bank: nc.tensor.matmul(bank[:, (h % 4) * 128:(h % 4 + 1) * 128], lhsT=ke[:, h, :], rhs=qe[:, h, :], start=True, stop=True),
                     reads=[ke.b, qe.b], writes=[bank.b])
            for i in range(2):
                P.op(P.dve, lambda i=i: nc.vector.tensor_tensor(out=attM[:, i * 4:i * 4 + 4, :], in0=attb[i][:].rearrange("p (h t) -> p h t", h=4),
                                                                in1=bdm[:].unsqueeze(1).broadcast_to([128, 4, 128]), op=ALU.mult),
                     reads=[attb[i].b, bdm.b], writes=[attM.b])
            for h in range(8):
                bank = obk[h // 4]
                reg = slice((h % 4) * 128, (h % 4 + 1) * 128)
                P.op(P.pe, lambda h=h, bank=bank, reg=reg: nc.tensor.matmul(bank[:, reg], lhsT=attM[:, h, :], rhs=V[:, h, :], start=True, stop=False),
                     reads=[attM.b, V.b], writes=[bank.b])
                P.op(P.pe, lambda h=h, bank=bank, reg=reg: nc.tensor.matmul(bank[0:64, reg], lhsT=qe[:, h, 0:64], rhs=SbfA[h][:], start=False, stop=True),
                     reads=[qe.b, SbfA[h].b], writes=[bank.b])
                P.op(P.pe, lambda h=h, bank=bank, reg=reg: nc.tensor.matmul(bank[64:128, reg], lhsT=qe[:, h, 64:128], rhs=SbfB[h][:], start=False, stop=True),
                     reads=[qe.b, SbfB[h].b], writes=[bank.b])
        for h in range(8):
            bank = attb[h // 4]
            P.op(P.pe, lambda h=h, bank=bank: nc.tensor.matmul(bank[:, (h % 4) * 128:(h % 4 + 1) * 128], lhsT=kdT[64:128, h, :], rhs=V[64:128, h, :], start=True, stop=True),
                 reads=[kdT.b, V.b], writes=[bank.b])
        for h in range(8):
            bank = attb[h // 4]
            P.op(P.dve, lambda h=h, bank=bank: nc.vector.scalar_tensor_tensor(out=S[:, h, :], in0=S[:, h, :], scalar=eAl[:, h, 1:2],
                                                                              in1=bank[:, (h % 4) * 128:(h % 4 + 1) * 128], op0=ALU.mult, op1=ALU.add),
                 reads=[Sb[h], eAl.b, bank.b], writes=[Sb[h]])
            if not state_only:
                P.op(P.act, lambda h=h: nc.scalar.activation(out=SbfA[h][:], in_=S[:, h, :], func=AF.Copy), reads=[Sb[h]], writes=[SbfA[h].b])
        if state_only:
            return
        for i in range(2):
            P.op(P.act, lambda i=i: nc.scalar.activation(out=osb[:, i * 4:i * 4 + 4, :].rearrange("p h e -> p (h e)"), in_=obk[i][:], func=AF.Copy),
                 reads=[obk[i].b], writes=[osb.b])
        P.op(P.dve, lambda: nc.vector.tensor_tensor(out=junk[:], in0=osb[:], in1=osb[:], op=ALU.mult),
             reads=[osb.b], writes=[junk.b])
        P.op(P.dve, lambda: nc.vector.tensor_reduce(out=oss[:], in_=junk[:], axis=AX.X, op=ALU.add),
             reads=[junk.b], writes=[oss.b])
        P.op(P.act, lambda: nc.scalar.activation(out=oss[:], in_=oss[:], func=AF.Sqrt, scale=1.0 / 128, bias=EPS), reads=[oss.b], writes=[oss.b])
        P.op(P.dve, lambda: nc.vector.reciprocal(out=ors[:], in_=oss[:]), reads=[oss.b], writes=[ors.b])
        P.op(P.dve, lambda: nc.vector.tensor_tensor(out=g1[:], in0=osb[:], in1=ors[:].unsqueeze(2).broadcast_to([128, 8, 128]), op=ALU.mult),
             reads=[osb.b, ors.b], writes=[g1.b])
        P.op(P.pool, lambda: nc.gpsimd.tensor_tensor(out=fl(g1), in0=fl(g1), in1=og[:], op=ALU.mult), reads=[g1.b, og.b], writes=[g1.b])
        P.op(P.pool, lambda: nc.gpsimd.tensor_tensor(out=g[:], in0=fl(g1), in1=sz[:], op=ALU.mult), reads=[g1.b, sz.b], writes=[g.b])
        for k in range(8):
            P.op(P.pe, lambda k=k: nc.tensor.transpose(tpb[:, k * 128:(k + 1) * 128], g[:, k * 128:(k + 1) * 128], ident[:]),
                 reads=[g.b, ident.b], writes=[tp.b])
        P.op(P.act, lambda: nc.scalar.activation(out=fl(gT), in_=tpb[:, 0:1024], func=AF.Copy), reads=[tp.b], writes=[gT.b])
        hn = hnew[s % 2]
        for c in range(2):
            bank = proj[cnt["proj"] % 2]
            cnt["proj"] += 1
            for k in range(8):
                P.op(P.pe, lambda k=k, c=c, bank=bank: nc.tensor.matmul(bank[:], lhsT=gT[:, k, :], rhs=Wout[:, k, c * 512:(c + 1) * 512], start=(k == 0), stop=(k == 7)),
                     reads=[gT.b, Wout.b], writes=[bank.b])
            P.op(P.dve, lambda c=c, bank=bank: nc.vector.scalar_tensor_tensor(out=hn[:, c * 512:(c + 1) * 512], in0=bank[:], scalar=rvcol[:, s:s + 1],
                                                                              in1=hb[:, c * 512:(c + 1) * 512], op0=ALU.mult, op1=ALU.add),
                 reads=[bank.b, rvcol.b, hb.b], writes=[hn.b])
        P.dma("sp", h_out[s * 128:(s + 1) * 128, :], hn[:], reads=[hn.b])

    for s in range(NS):
        slot(s)
    P.dma("sp", io["state_out"], S[:], reads=Sb)
    if state_only:
        P.dma("sp", io["atot_out"], atot[:], reads=[atot.b])
    return hnew


def hgrn_consts(NS, q, ml_dtypes):
    ident = np.eye(128, dtype=np.float32).astype(ml_dtypes.bfloat16)
    sidx = np.arange(128)[:, None]
    tidx = np.arange(128)[None, :]
    bd = ((sidx <= tidx) & ((sidx // 64) == (tidx // 64))).astype(np.float32).astype(ml_dtypes.bfloat16)
    rst = np.ones((128, 1024), np.float32)
    rst[:, ::64] = 0.0
    rv = np.ones(128, np.float32)
    if q == 0:
        rv[:112] = 0
    else:
        rv[:] = 0
    rv_row = np.broadcast_to(rv[None, :], (128, 128)).copy()
    rv_col = np.ones((128, NS), np.float32)
    rv_col[:, 0] = rv
    return dict(ident=ident, bdmask=bd, rst=rst, rv_row=rv_row, rv_col=rv_col)


NEG = -30000.0


def _alias(t):
    a = T.__new__(T)
    a.t = t.t
    a.b = t.b
    return a


def emit_mlstm_p1(P, io, NS, pb, tag="b1", state_only=False):
    nc = P.nc
    n = lambda s: f"{tag}_{s}"
    w_in = io["w_in"]
    Wqk = T(P, n("Wqk"), [128, 8, 2048], BF16)
    Wv = T(P, n("Wv"), [128, 8, 2048], BF16)
    Wg = T(P, n("Wg"), [128, 8, 16], BF16)
    for kc in range(8):
        P.dma("pool", Wqk[:, kc, :], w_in[kc * 128:(kc + 1) * 128, 0:2048], writes=[Wqk.b])
        P.dma("pool", Wv[:, kc, :], w_in[kc * 128:(kc + 1) * 128, 2048:4096], writes=[Wv.b])
        P.dma("pool", Wg[:, kc, :], w_in[kc * 128:(kc + 1) * 128, 4096:4112], writes=[Wg.b])
    ng = T(P, n("ng"), [128, 1024], F32)
    P.dma("sp", ng[:], io["norm_gain"].partition_broadcast(128), writes=[ng.b])
    ident = T(P, n("ident"), [128, 128], BF16)
    P.dma("sp", ident[:], io["ident"], writes=[ident.b])
    identf = T(P, n("identf"), [128, 128], F32)
    P.dma("sp", identf[:], io["identf"], writes=[identf.b])
    mcur = T(P, n("mcur"), [128, 128], BF16)
    P.dma("sp", mcur[:], io["mask_cur"], writes=[mcur.b])
    rv8 = T(P, n("rv8"), [8, 128], F32)
    P.dma("sp", rv8[:], io["rv8"], writes=[rv8.b])
    nm8 = T(P, n("nm8"), [8, 128], F32)
    P.op(P.dve, lambda: nc.vector.tensor_scalar(out=nm8[:], in0=rv8[:], scalar1=-NEG, scalar2=NEG, op0=ALU.mult, op1=ALU.add),
         reads=[rv8.b], writes=[nm8.b])
    cw = T(P, n("cw"), [128, 16, 4], F32)
    for j in range(4):
        P.dma("sp", cw[:, :, j], io["conv_w"][j].rearrange("(t d) -> d t", d=128), writes=[cw.b], allow_slow_non_contiguous=True)
    cb = T(P, n("cb"), [128, 16], F32)
    P.dma("sp", cb[:], io["conv_b"].rearrange("(t d) -> d t", d=128), writes=[cb.b], allow_slow_non_contiguous=True)
    gb = T(P, n("gb"), [8, 2], F32)
    P.dma("sp", gb[:], io["gate_bias"].rearrange("(c h) -> h c", h=8), writes=[gb.b], allow_slow_non_contiguous=True)
    ngb = T(P, n("ngb"), [8, 1], F32)
    P.op(P.dve, lambda: nc.vector.tensor_scalar(out=ngb[:], in0=gb[:, 1:2], scalar1=-1.0, scalar2=0.0, op0=ALU.mult, op1=ALU.add),
         reads=[gb.b], writes=[ngb.b])
    ones8 = T(P, n("ones8"), [8, 128], F32)
    P.op(P.dve, lambda: nc.vector.memset(ones8[:], 1.0), writes=[ones8.b])
    S = T(P, n("S"), [128, 8, 257], F32)
    Sb = [Buf(n(f"S{h}")) for h in range(8)]
    P.dma("sp", S[:], io["S_in"], writes=[S.b] + Sb)
    rf = T(P, n("rf"), [8, 2], F32)
    P.dma("sp", rf[:], io["rf_in"], writes=[rf.b])
    Sbf = [T(P, n(f"Sbf{h}"), [128, 257], BF16) for h in range(8)]
    xp = T(P, n("xp"), [128, 16, 131], F32)
    P.op(P.pool, lambda: nc.gpsimd.memset(xp[:], 0.0), writes=[xp.b])

    hblk = [T(P, n(f"hblk{i}"), [128, 1024], F32) for i in range(2)]
    acc = T(P, n("acc"), [128, 16, 128], F32)
    tmp = T(P, n("tmp"), [128, 16, 128], F32)
    tmp2 = T(P, n("tmp2"), [128, 16, 128], F32)
    junk = _alias(tmp)
    st = T(P, n("st"), [128, 4], F32)
    u = T(P, n("u"), [128, 1024], BF16)
    uT = [T(P, n(f"uT{i}"), [128, 8, 128], BF16) for i in range(2)]
    qkT = T(P, n("qkT"), [128, 16, 128], BF16)
    Ktok = T(P, n("Ktok"), [128, 8, 128], BF16)
    Va = T(P, n("Va"), [128, 8, 258], BF16)
    SM = T(P, n("SM"), [128, 8, 128], BF16)
    Xs = T(P, n("Xs"), [128, 8, 257], F32)
    hm = [T(P, n(f"hm{i}"), [128, 8, 256], F32) for i in range(2)]
    G = {k: T(P, n("g_" + k), [8, 128], F32) for k in ["li", "e", "nl", "Fn", "b", "Mt", "alpha", "betap", "emm"]}
    gs = T(P, n("gs"), [8, 4], F32)
    dg = T(P, n("dg"), [8, 8], F32)
    gT = T(P, n("gT"), [128, 24], F32)
    gam = T(P, n("gam"), [128, 8], F32)
    dn = T(P, n("dn"), [128, 8], F32)
    dn2 = T(P, n("dn2"), [128, 8], F32)

    tp = pb[0]
    proj = [pb[1], pb[2]]
    stb = [pb[3], pb[4]]
    xb = [pb[5], pb[6], pb[7]]
    h_in = io["h_in"]
    cnt = {"proj": 0, "x": 0}
    fl = lambda t: t[:].rearrange("p h t -> p (h t)")
    LN128H = 0.5 * math.log(128.0)

    def slot(s):
        hb = hblk[s % 2]
        P.dma("sp", hb[:], h_in[s * 128:(s + 1) * 128, :], writes=[hb.b])
        P.op(P.act, lambda: nc.scalar.activation(out=fl(junk)[:, 0:1024], in_=hb[:], func=AF.Square, accum_out=st[:, 0:1]),
             reads=[hb.b], writes=[junk.b, st.b])
        P.op(P.act, lambda: nc.scalar.activation(out=st[:, 1:2], in_=st[:, 0:1], func=AF.Sqrt, scale=1.0 / D, bias=EPS),
             reads=[st.b], writes=[st.b])
        P.op(P.dve, lambda: nc.vector.reciprocal(out=st[:, 2:3], in_=st[:, 1:2]), reads=[st.b], writes=[st.b])
        P.op(P.dve, lambda: nc.vector.scalar_tensor_tensor(out=u[:], in0=hb[:], scalar=st[:, 2:3], in1=ng[:], op0=ALU.mult, op1=ALU.mult),
             reads=[hb.b, st.b, ng.b], writes=[u.b])
        tpb = tp[:].bitcast(BF16)
        for k in range(8):
            P.op(P.pe, lambda k=k: nc.tensor.transpose(tpb[:, k * 128:(k + 1) * 128], u[:, k * 128:(k + 1) * 128], ident[:]),
                 reads=[u.b, ident.b], writes=[tp.b])
        ut = uT[s % 2]
        P.op(P.act, lambda: nc.scalar.activation(out=fl(ut), in_=tpb[:, 0:1024], func=AF.Copy), reads=[tp.b], writes=[ut.b])
        gbank = proj[cnt["proj"] % 2]
        cnt["proj"] += 1
        for gi in range(2):
            for k in range(8):
                P.op(P.pe, lambda k=k, gi=gi: nc.tensor.matmul(gbank[0:8, gi * 128:(gi + 1) * 128], lhsT=Wg[:, k, gi * 8:(gi + 1) * 8], rhs=ut[:, k, :],
                                                               start=(k == 0), stop=(k == 7)),
                     reads=[ut.b, Wg.b], writes=[gbank.b])
        li, e, nl, Fn, b_, Mt, alpha, betap, emm = [G[k] for k in ["li", "e", "nl", "Fn", "b", "Mt", "alpha", "betap", "emm"]]
        P.op(P.act, lambda: nc.scalar.activation(out=li[:], in_=gbank[0:8, 0:128], func=AF.Identity, bias=gb[:, 0:1]),
             reads=[gbank.b, gb.b], writes=[li.b])
        P.op(P.act, lambda: nc.scalar.activation(out=e[:], in_=gbank[0:8, 128:256], func=AF.Exp, scale=-1.0, bias=ngb[:, 0:1]),
             reads=[gbank.b, ngb.b], writes=[e.b])
        P.op(P.act, lambda: nc.scalar.activation(out=nl[:], in_=e[:], func=AF.Ln, bias=1.0), reads=[e.b], writes=[nl.b])
        if s == 0:
            P.op(P.dve, lambda: nc.vector.tensor_tensor(out=li[:], in0=li[:], in1=nm8[:], op=ALU.add), reads=[li.b, nm8.b], writes=[li.b])
            P.op(P.dve, lambda: nc.vector.tensor_tensor(out=nl[:], in0=nl[:], in1=rv8[:], op=ALU.mult), reads=[nl.b, rv8.b], writes=[nl.b])
        P.op(P.dve, lambda: nc.vector.tensor_tensor_scan(out=Fn[:], data0=ones8[:], data1=nl[:], initial=rf[:, 1:2], op0=ALU.mult, op1=ALU.add),
             reads=[ones8.b, nl.b, rf.b], writes=[Fn.b])
        P.op(P.dve, lambda: nc.vector.tensor_tensor(out=b_[:], in0=li[:], in1=Fn[:], op=ALU.add), reads=[li.b, Fn.b], writes=[b_.b])
        P.op(P.dve, lambda: nc.vector.tensor_tensor_scan(out=Mt[:], data0=ones8[:], data1=b_[:], initial=rf[:, 0:1], op0=ALU.mult, op1=ALU.max),
             reads=[ones8.b, b_.b, rf.b], writes=[Mt.b])
        P.op(P.dve, lambda: nc.vector.tensor_scalar(out=gs[:, 0:1], in0=Mt[:, 127:128], scalar1=-1.0, scalar2=-LN128H, op0=ALU.mult, op1=ALU.add),
             reads=[Mt.b], writes=[gs.b])
        P.op(P.dve, lambda: nc.vector.tensor_tensor(out=gs[:, 1:2], in0=rf[:, 0:1], in1=Mt[:, 127:128], op=ALU.subtract), reads=[rf.b, Mt.b], writes=[gs.b])
        P.op(P.dve, lambda: nc.vector.tensor_copy(out=gs[:, 2:3], in_=Mt[:, 127:128]), reads=[Mt.b], writes=[gs.b])
        P.op(P.act, lambda: nc.scalar.activation(out=alpha[:], in_=b_[:], func=AF.Exp, bias=gs[:, 0:1]), reads=[b_.b, gs.b], writes=[alpha.b])
        if s == 0:
            P.op(P.dve, lambda: nc.vector.tensor_tensor(out=alpha[:], in0=alpha[:], in1=rv8[:], op=ALU.mult), reads=[alpha.b, rv8.b], writes=[alpha.b])
        if not state_only:
            P.op(P.act, lambda: nc.scalar.activation(out=betap[:], in_=Mt[:], func=AF.Exp, scale=-1.0, bias=gs[:, 2:3]), reads=[Mt.b, gs.b], writes=[betap.b])
            P.op(P.dve, lambda: nc.vector.tensor_tensor(out=emm[:], in0=Fn[:], in1=Mt[:], op=ALU.subtract), reads=[Fn.b, Mt.b], writes=[emm.b])
            P.op(P.act, lambda: nc.scalar.activation(out=emm[:], in_=emm[:], func=AF.Exp), reads=[emm.b], writes=[emm.b])
        P.op(P.act, lambda: nc.scalar.activation(out=gs[:, 1:2], in_=gs[:, 1:2], func=AF.Exp), reads=[gs.b], writes=[gs.b])
        P.op(P.dve, lambda: nc.vector.tensor_copy(out=rf[:, 0:1], in_=Mt[:, 127:128]), reads=[Mt.b, gs.b], writes=[rf.b])
        P.op(P.dve, lambda: nc.vector.tensor_copy(out=rf[:, 1:2], in_=Fn[:, 127:128]), reads=[Fn.b], writes=[rf.b])
        P.op(P.act, lambda: nc.scalar.activation(out=dg[:], in_=identf[0:8, 0:8], func=AF.Copy, scale=gs[:, 1:2]), reads=[identf.b, gs.b], writes=[dg.b])
        gtb = stb[0]
        P.op(P.pe, lambda: nc.tensor.matmul(gtb[:, 32:40], lhsT=ones8[:], rhs=dg[:], start=True, stop=True), reads=[ones8.b, dg.b], writes=[gtb.b])
        ngt = 1 if state_only else 3
        for i, t_ in enumerate([alpha, betap, emm][:ngt]):
            P.op(P.pe, lambda i=i, t_=t_: nc.tensor.transpose(gtb[:, i * 8:(i + 1) * 8], t_[:], identf[0:8, 0:8]), reads=[t_.b, identf.b], writes=[gtb.b])
        P.op(P.act, lambda: nc.scalar.activation(out=gT[:, 0:8 * ngt], in_=gtb[:, 0:8 * ngt], func=AF.Copy), reads=[gtb.b], writes=[gT.b])
        P.op(P.act, lambda: nc.scalar.activation(out=gam[:], in_=gtb[:, 32:40], func=AF.Copy), reads=[gtb.b], writes=[gam.b])
        for h in range(0 if not state_only else 8, 8):
            P.op(P.act, lambda h=h: nc.scalar.activation(out=Sbf[h][:], in_=S[:, h, :], func=AF.Copy, scale=gam[:, h:h + 1]),
                 reads=[Sb[h], gam.b], writes=[Sbf[h].b])
        t0 = 8 if state_only else 0
        for grp in range(2 if state_only else 0, 4):
            bank = proj[cnt["proj"] % 2]
            cnt["proj"] += 1
            for i in range(4):
                ti = grp * 4 + i
                for k in range(8):
                    P.op(P.pe, lambda k=k, i=i, ti=ti, bank=bank: nc.tensor.matmul(bank[:, i * 128:(i + 1) * 128], lhsT=Wqk[:, k, ti * 128:(ti + 1) * 128],
                                                                               rhs=ut[:, k, :], start=(k == 0), stop=(k == 7)),
                         reads=[ut.b, Wqk.b], writes=[bank.b])
            P.op(P.act, lambda grp=grp, bank=bank: nc.scalar.activation(out=xp[:, grp * 4:grp * 4 + 4, 3:131], in_=bank[:].rearrange("p (i t) -> p i t", i=4), func=AF.Copy),
                 reads=[bank.b], writes=[xp.b])
        nt = 16 - t0
        wj = lambda j: cw[:, t0:16, j:j + 1].broadcast_to([128, nt, 128])
        P.op(P.dve, lambda: nc.vector.tensor_tensor(out=acc[:, t0:16, :], in0=xp[:, t0:16, 0:128], in1=wj(0), op=ALU.mult), reads=[xp.b, cw.b], writes=[acc.b])
        P.op(P.pool, lambda: nc.gpsimd.tensor_tensor(out=tmp[:, t0:16, :], in0=xp[:, t0:16, 1:129], in1=wj(1), op=ALU.mult), reads=[xp.b, cw.b], writes=[tmp.b])
        P.op(P.dve, lambda: nc.vector.tensor_tensor(out=tmp2[:, t0:16, :], in0=xp[:, t0:16, 2:130], in1=wj(2), op=ALU.mult), reads=[xp.b, cw.b], writes=[tmp2.b])
        P.op(P.dve, lambda: nc.vector.tensor_tensor(out=acc[:, t0:16, :], in0=acc[:, t0:16, :], in1=tmp[:, t0:16, :], op=ALU.add), reads=[acc.b, tmp.b], writes=[acc.b])
        P.op(P.pool, lambda: nc.gpsimd.tensor_tensor(out=tmp[:, t0:16, :], in0=xp[:, t0:16, 3:131], in1=wj(3), op=ALU.mult), reads=[xp.b, cw.b], writes=[tmp.b])
        P.op(P.dve, lambda: nc.vector.tensor_tensor(out=acc[:, t0:16, :], in0=acc[:, t0:16, :], in1=tmp2[:, t0:16, :], op=ALU.add), reads=[acc.b, tmp2.b], writes=[acc.b])
        P.op(P.dve, lambda: nc.vector.tensor_tensor(out=tmp2[:, t0:16, :], in0=tmp[:, t0:16, :], in1=cb[:, t0:16].unsqueeze(2).broadcast_to([128, nt, 128]), op=ALU.add),
             reads=[tmp.b, cb.b], writes=[tmp2.b])
        P.op(P.dve, lambda: nc.vector.tensor_tensor(out=acc[:, t0:16, :], in0=acc[:, t0:16, :], in1=tmp2[:, t0:16, :], op=ALU.add), reads=[acc.b, tmp2.b], writes=[acc.b])
        P.op(P.act, lambda: nc.scalar.activation(out=qkT[:, t0:16, :].rearrange("p h t -> p (h t)"), in_=acc[:, t0:16, :].rearrange("p h t -> p (h t)"), func=AF.Silu),
             reads=[acc.b], writes=[qkT.b])
        P.op(P.pool, lambda: nc.gpsimd.tensor_copy(out=xp[:, t0:16, 0:3], in_=xp[:, t0:16, 128:131]), reads=[xp.b], writes=[xp.b])
        for h in range(8):
            P.op(P.pe, lambda h=h: nc.tensor.transpose(tpb[:, h * 128:(h + 1) * 128], qkT[:, 8 + h, :], ident[:]), reads=[qkT.b, ident.b], writes=[tp.b])
        P.op(P.act, lambda: nc.scalar.activation(out=fl(Ktok), in_=tpb[:, 0:1024], func=AF.Copy), reads=[tp.b], writes=[Ktok.b])
        for c in range(4):
            bank = proj[cnt["proj"] % 2]
            cnt["proj"] += 1
            for k in range(8):
                P.op(P.pe, lambda k=k, c=c, bank=bank: nc.tensor.matmul(bank[:], lhsT=ut[:, k, :], rhs=Wv[:, k, c * 512:(c + 1) * 512], start=(k == 0), stop=(k == 7)),
                     reads=[ut.b, Wv.b], writes=[bank.b])
            for hh in range(2):
                h = c * 2 + hh
                P.op(P.act, lambda h=h, hh=hh, bank=bank: nc.scalar.activation(out=Va[:, h, 0:256], in_=bank[:, hh * 256:(hh + 1) * 256], func=AF.Copy, scale=gT[:, h:h + 1]),
                     reads=[bank.b, gT.b], writes=[Va.b])
        P.op(P.dve, lambda: nc.vector.tensor_copy(out=Va[:, :, 256:257], in_=gT[:, 0:8].unsqueeze(2)), reads=[gT.b], writes=[Va.b])
        if not state_only:
            for h in range(8):
                bank = stb[h // 4]
                P.op(P.pe, lambda h=h, bank=bank: nc.tensor.matmul(bank[:, (h % 4) * 128:(h % 4 + 1) * 128], lhsT=qkT[:, 8 + h, :], rhs=qkT[:, h, :], start=True, stop=True),
                     reads=[qkT.b], writes=[bank.b])
            for i in range(2):
                P.op(P.dve, lambda i=i: nc.vector.tensor_tensor(out=SM[:, i * 4:i * 4 + 4, :], in0=stb[i][:].rearrange("p (h t) -> p h t", h=4),
                                                                in1=mcur[:].unsqueeze(1).broadcast_to([128, 4, 128]), op=ALU.mult),
                     reads=[stb[i].b, mcur.b], writes=[SM.b])
        for h in range(8):
            if not state_only:
                bx = xb[cnt["x"] % 3]
                cnt["x"] += 1
                P.op(P.pe, lambda h=h, bx=bx: nc.tensor.matmul(bx[:, 0:257], lhsT=SM[:, h, :], rhs=Va[:, h, 0:257], start=True, stop=False),
                     reads=[SM.b, Va.b], writes=[bx.b])
                P.op(P.pe, lambda h=h, bx=bx: nc.tensor.matmul(bx[:, 0:257], lhsT=qkT[:, h, :], rhs=Sbf[h][:], start=False, stop=True),
                     reads=[qkT.b, Sbf[h].b], writes=[bx.b])
                P.op(P.act, lambda h=h, bx=bx: nc.scalar.activation(out=Xs[:, h, :], in_=bx[:, 0:257], func=AF.Copy), reads=[bx.b], writes=[Xs.b])
            bu = xb[cnt["x"] % 3]
            cnt["x"] += 1
            P.op(P.pe, lambda h=h, bu=bu: nc.tensor.matmul(bu[:, 0:257], lhsT=Ktok[:, h, :], rhs=Va[:, h, 0:257], start=True, stop=True),
                 reads=[Ktok.b, Va.b], writes=[bu.b])
            P.op(P.dve, lambda h=h, bu=bu: nc.vector.scalar_tensor_tensor(out=S[:, h, :], in0=S[:, h, :], scalar=gam[:, h:h + 1], in1=bu[:, 0:257],
                                                                          op0=ALU.mult, op1=ALU.add),
                 reads=[Sb[h], gam.b, bu.b], writes=[Sb[h]])
        if state_only:
            return
        P.op(P.dve, lambda: nc.vector.tensor_tensor(out=dn[:].unsqueeze(2), in0=Xs[:, :, 256:257], in1=gT[:, 8:16].unsqueeze(2), op=ALU.mult),
             reads=[Xs.b, gT.b], writes=[dn.b])
        P.op(P.act, lambda: nc.scalar.activation(out=dn[:], in_=dn[:], func=AF.Abs), reads=[dn.b], writes=[dn.b])
        P.op(P.dve, lambda: nc.vector.tensor_tensor(out=dn[:], in0=dn[:], in1=gT[:, 16:24], op=ALU.max), reads=[dn.b, gT.b], writes=[dn.b])
        P.op(P.dve, lambda: nc.vector.reciprocal(out=dn2[:], in_=dn[:]), reads=[dn.b], writes=[dn2.b])
        P.op(P.dve, lambda: nc.vector.tensor_tensor(out=dn2[:], in0=dn2[:], in1=gT[:, 8:16], op=ALU.mult), reads=[dn2.b, gT.b], writes=[dn2.b])
        hmt = hm[s % 2]
        P.op(P.dve, lambda: nc.vector.tensor_tensor(out=hmt[:], in0=Xs[:, :, 0:256], in1=dn2[:].unsqueeze(2).broadcast_to([128, 8, 256]), op=ALU.mult),
             reads=[Xs.b, dn2.b], writes=[hmt.b])
        P.dma("sp", io["hm"][s * 128:(s + 1) * 128, :], hmt[:].rearrange("p h e -> p (h e)"), reads=[hmt.b])

    for s in range(NS):
        slot(s)
    P.dma("sp", io["S_out"], S[:], reads=Sb)
    P.dma("sp", io["rf_out"], rf[:], reads=[rf.b])
    return hm


def emit_mlstm_p2(P, io, NS, pb, tag="b2"):
    nc = P.nc
    n = lambda s: f"{tag}_{s}"
    w_in = io["w_in"]
    Woz = T(P, n("Woz"), [128, 8, 4096], BF16)
    Wout = T(P, n("Wout"), [128, 16, 1024], BF16)
    for kc in range(8):
        for c0 in (0, 2048):
            P.dma("pool", Woz[:, kc, c0:c0 + 2048], w_in[kc * 128:(kc + 1) * 128, 4112 + c0:4112 + c0 + 2048], writes=[Woz.b])
    load_w_bf16(P, Wout, io["w_out"], 2048, 1024)
    ng = T(P, n("ng"), [128, 1024], F32)
    P.dma("sp", ng[:], io["norm_gain"].partition_broadcast(128), writes=[ng.b])
    hg = T(P, n("hg"), [128, 2048], F32)
    P.dma("sp", hg[:], io["h_gain"].partition_broadcast(128), writes=[hg.b])
    ident = T(P, n("ident"), [128, 128], BF16)
    P.dma("sp", ident[:], io["ident"], writes=[ident.b])
    rvcol = T(P, n("rvcol"), [128, NS], F32)
    P.dma("sp", rvcol[:], io["rv_col"], writes=[rvcol.b])
    hblk = [T(P, n(f"hblk{i}"), [128, 1024], F32) for i in range(2)]
    hmb = [T(P, n("hmb0"), [128, 8, 256], F32)] * 2
    hnew = [T(P, n(f"hnew{i}"), [128, 1024], F32) for i in range(2)]
    st = T(P, n("st"), [128, 4], F32)
    u = T(P, n("u"), [128, 1024], BF16)
    uT = [T(P, n(f"uT{i}"), [128, 8, 128], BF16) for i in range(2)]
    so = T(P, n("so"), [128, 8, 256], BF16)
    sz = T(P, n("sz"), [128, 2048], BF16)
    t2 = T(P, n("t2"), [128, 8, 256], F32)
    sq = T(P, n("sq"), [128, 8, 256], F32)
    ss = T(P, n("ss"), [128, 8], F32)
    rs = T(P, n("rs"), [128, 8], F32)
    g = T(P, n("g"), [128, 2048], BF16)
    gT = T(P, n("gT"), [128, 16, 128], BF16)
    tp = pb[0]
    tp2 = pb[3]
    proj = [pb[1], pb[2]]
    cnt = {"proj": 0}
    fl = lambda t: t[:].rearrange("p h t -> p (h t)")
    h_in, h_out = io["h_in"], io["h_out"]

    def slot(s):
        hb = hblk[s % 2]
        hmt = hmb[s % 2]
        P.dma("sp", hb[:], h_in[s * 128:(s + 1) * 128, :], writes=[hb.b])
        P.dma("sp", fl(hmt), io["hm"][s * 128:(s + 1) * 128, :], writes=[hmt.b])
        P.op(P.act, lambda: nc.scalar.activation(out=fl(sq)[:, 0:1024], in_=hb[:], func=AF.Square, accum_out=st[:, 0:1]),
             reads=[hb.b], writes=[sq.b, st.b])
        P.op(P.act, lambda: nc.scalar.activation(out=st[:, 1:2], in_=st[:, 0:1], func=AF.Sqrt, scale=1.0 / D, bias=EPS),
             reads=[st.b], writes=[st.b])
        P.op(P.dve, lambda: nc.vector.reciprocal(out=st[:, 2:3], in_=st[:, 1:2]), reads=[st.b], writes=[st.b])
        P.op(P.dve, lambda: nc.vector.scalar_tensor_tensor(out=u[:], in0=hb[:], scalar=st[:, 2:3], in1=ng[:], op0=ALU.mult, op1=ALU.mult),
             reads=[hb.b, st.b, ng.b], writes=[u.b])
        tpb = tp[:].bitcast(BF16)
        for k in range(8):
            P.op(P.pe, lambda k=k: nc.tensor.transpose(tpb[:, k * 128:(k + 1) * 128], u[:, k * 128:(k + 1) * 128], ident[:]),
                 reads=[u.b, ident.b], writes=[tp.b])
        ut = uT[s % 2]
        P.op(P.act, lambda: nc.scalar.activation(out=fl(ut), in_=tpb[:, 0:1024], func=AF.Copy), reads=[tp.b], writes=[ut.b])
        for c in range(8):
            bank = proj[cnt["proj"] % 2]
            cnt["proj"] += 1
            for k in range(8):
                P.op(P.pe, lambda k=k, c=c, bank=bank: nc.tensor.matmul(bank[:], lhsT=ut[:, k, :], rhs=Woz[:, k, c * 512:(c + 1) * 512], start=(k == 0), stop=(k == 7)),
                     reads=[ut.b, Woz.b], writes=[bank.b])
            if c < 4:
                P.op(P.act, lambda c=c, bank=bank: nc.scalar.activation(out=fl(so)[:, c * 512:(c + 1) * 512], in_=bank[:], func=AF.Sigmoid),
                     reads=[bank.b], writes=[so.b])
            else:
                P.op(P.act, lambda c=c, bank=bank: nc.scalar.activation(out=sz[:, (c - 4) * 512:(c - 3) * 512], in_=bank[:], func=AF.Silu),
                     reads=[bank.b], writes=[sz.b])
        P.op(P.dve, lambda: nc.vector.tensor_tensor(out=t2[:], in0=hmt[:], in1=so[:], op=ALU.mult), reads=[hmt.b, so.b], writes=[t2.b])
        P.op(P.dve, lambda: nc.vector.tensor_tensor(out=sq[:], in0=t2[:], in1=t2[:], op=ALU.mult), reads=[t2.b], writes=[sq.b])
        P.op(P.dve, lambda: nc.vector.tensor_reduce(out=ss[:], in_=sq[:], axis=AX.X, op=ALU.add), reads=[sq.b], writes=[ss.b])
        P.op(P.act, lambda: nc.scalar.activation(out=ss[:], in_=ss[:], func=AF.Sqrt, scale=1.0 / 256, bias=EPS), reads=[ss.b], writes=[ss.b])
        P.op(P.dve, lambda: nc.vector.reciprocal(out=rs[:], in_=ss[:]), reads=[ss.b], writes=[rs.b])
        P.op(P.dve, lambda: nc.vector.tensor_tensor(out=t2[:], in0=t2[:], in1=rs[:].unsqueeze(2).broadcast_to([128, 8, 256]), op=ALU.mult),
             reads=[t2.b, rs.b], writes=[t2.b])
        P.op(P.dve, lambda: nc.vector.tensor_tensor(out=fl(t2), in0=fl(t2), in1=hg[:], op=ALU.mult), reads=[t2.b, hg.b], writes=[t2.b])
        P.op(P.pool, lambda: nc.gpsimd.tensor_tensor(out=g[:], in0=fl(t2), in1=sz[:], op=ALU.mult), reads=[t2.b, sz.b], writes=[g.b])
        tp2b = tp2[:].bitcast(BF16)
        for k in range(16):
            tb, tbuf = (tpb, tp) if k < 8 else (tp2b, tp2)
            P.op(P.pe, lambda k=k, tb=tb: nc.tensor.transpose(tb[:, (k % 8) * 128:(k % 8 + 1) * 128], g[:, k * 128:(k + 1) * 128], ident[:]),
                 reads=[g.b, ident.b], writes=[tbuf.b])
        P.op(P.act, lambda: nc.scalar.activation(out=gT[:, 0:8, :].rearrange("p k t -> p (k t)"), in_=tpb[:, 0:1024], func=AF.Copy), reads=[tp.b], writes=[gT.b])
        P.op(P.act, lambda: nc.scalar.activation(out=gT[:, 8:16, :].rearrange("p k t -> p (k t)"), in_=tp2b[:, 0:1024], func=AF.Copy), reads=[tp2.b], writes=[gT.b])
        hn = hnew[s % 2]
        for c in range(2):
            bank = proj[cnt["proj"] % 2]
            cnt["proj"] += 1
            for k in range(16):
                P.op(P.pe, lambda k=k, c=c, bank=bank: nc.tensor.matmul(bank[:], lhsT=gT[:, k, :], rhs=Wout[:, k, c * 512:(c + 1) * 512], start=(k == 0), stop=(k == 15)),
                     reads=[gT.b, Wout.b], writes=[bank.b])
            P.op(P.dve, lambda c=c, bank=bank: nc.vector.scalar_tensor_tensor(out=hn[:, c * 512:(c + 1) * 512], in0=bank[:], scalar=rvcol[:, s:s + 1],
                                                                              in1=hb[:, c * 512:(c + 1) * 512], op0=ALU.mult, op1=ALU.add),
                 reads=[bank.b, rvcol.b, hb.b], writes=[hn.b])
        P.dma("sp", h_out[s * 128:(s + 1) * 128, :], hn[:], reads=[hn.b])

    for s in range(NS):
        slot(s)
    return hnew


def mlstm_consts(NS, q, ml_dtypes):
    ident = np.eye(128, dtype=np.float32).astype(ml_dtypes.bfloat16)
    identf = np.eye(128, dtype=np.float32)
    sidx = np.arange(128)[:, None]
    tidx = np.arange(128)[None, :]
    mask_cur = (sidx <= tidx).astype(np.float32).astype(ml_dtypes.bfloat16)
    rv = np.ones(128, np.float32)
    if q == 0:
        rv[:112] = 0
    else:
        rv[:] = 0
    rv8 = np.broadcast_to(rv[None, :], (8, 128)).copy()
    rv_col = np.ones((128, NS), np.float32)
    rv_col[:, 0] = rv
    return dict(ident=ident, identf=identf, mask_cur=mask_cur, rv8=rv8, rv_col=rv_col)


def _mk_io(P, specs):
    io = {}
    for name, shape, dt, kind in specs:
        io[name] = P.dram(name, shape, dt, kind)
    return io


import ml_dtypes
from concourse.bass_utils import run_bass_kernel_spmd

NSLOT = 33


def _sel_accumulate(P, nc, tag, gath, rows_per, nrows, ncols, sel, n_terms, extra=None):
    acc = T(P, f"{tag}_acc", [nrows, ncols], F32)
    g = [T(P, f"{tag}_g{i}", [nrows, ncols], F32) for i in range(2)]
    terms = [gath[i * rows_per:i * rows_per + nrows, 0:ncols] for i in range(n_terms)]
    if extra is not None:
        terms.append(extra)
    for i, src in enumerate(terms):
        gi = g[i % 2]
        P.dma("sp", gi[:], src, writes=[gi.b])
        if i == 0:
            P.op(P.act, lambda gi=gi, i=i: nc.scalar.activation(out=acc[:], in_=gi[:], func=AF.Copy, scale=sel[0:nrows, i:i + 1]),
                 reads=[gi.b, sel.b], writes=[acc.b])
        else:
            P.op(P.dve, lambda gi=gi, i=i: nc.vector.scalar_tensor_tensor(out=acc[:], in0=gi[:], scalar=sel[0:nrows, i:i + 1], in1=acc[:],
                                                                          op0=ALU.mult, op1=ALU.add),
                 reads=[gi.b, sel.b, acc.b], writes=[acc.b])
    return acc


def build_fused(NS=NSLOT):
    P = Prog(same_sync=True)
    nc = P.nc
    I, O = "ExternalInput", "ExternalOutput"
    e = _mk_io(P, [
        ("x_in", [(NS + 1) * 128, 1024], F32, I), ("out", [(NS - 1) * 128, 1024], F32, O),
        ("norm_gain", [4, 1024], F32, I),
        ("a_w_in", [2, 1024, 2560], F32, I), ("a_q_gain", [2, 64], F32, I), ("a_k_gain", [2, 64], F32, I), ("a_sinks", [2, 16], F32, I),
        ("a_w_out", [2, 1024, 1024], F32, I),
        ("b_w_in", [1024, 8208], F32, I), ("b_conv_w", [4, 2048], F32, I), ("b_conv_b", [2048], F32, I), ("b_gate_bias", [16], F32, I),
        ("b_h_gain", [2048], F32, I), ("b_w_out", [2048, 1024], F32, I),
        ("c_w_in", [1024, 4096], F32, I), ("c_gamma", [4, 1024], F32, I), ("c_o_gain", [1024], F32, I), ("c_w_out", [1024, 1024], F32, I),
        ("ident", [128, 128], BF16, I), ("identf", [128, 128], F32, I), ("mask_cur", [128, 128], BF16, I), ("mask_prev", [128, 128], BF16, I),
        ("bdmask", [128, 128], BF16, I), ("rst", [128, 1024], F32, I),
        ("cos2_0", [(NS + 1) * 128, 64], F32, I), ("sinS_0", [(NS + 1) * 128, 64], F32, I), ("kvalid_0", [128, NS + 1], F32, I),
        ("cos2_3", [NS * 128, 64], F32, I), ("sinS_3", [NS * 128, 64], F32, I), ("kvalid_3", [128, NS], F32, I),
        ("rv8", [8, 128], F32, I), ("rv_row", [128, 128], F32, I), ("rv_col", [128, NS], F32, I),
        ("sel8", [128, 8], F32, I), ("sel9", [128, 9], F32, I), ("selp", [128, 8], F32, I),
    ])
    N = "Internal"
    hA = P.dram("hA", [(NS + 1) * 128, 1024], F32, N)
    hB = P.dram("hB", [NS * 128, 1024], F32, N)
    hC = P.dram("hC", [NS * 128, 1024], F32, N)
    hD0 = P.dram("hD0", [128, 1024], F32, N)
    hm = P.dram("hm", [NS * 128, 2048], F32, N)
    Sb_in = P.dram("Sb_in", [128, 2056], F32, N)
    rfb_in = P.dram("rfb_in", [8, 2], F32, N)
    Sc_in = P.dram("Sc_in", [128, 1024], F32, N)
    stB = P.cc_dram("stB", [136, 2056], F32)
    gB = P.cc_dram("gB", [8 * 136, 2056], F32)
    stC = P.cc_dram("stC", [128, 1032], F32)
    gC = P.cc_dram("gC", [8 * 128, 1032], F32)
    hh = P.cc_dram("hh", [128, 1024], F32)
    gH = P.cc_dram("gH", [8 * 128, 1024], F32)
    pb = [T(P, f"pb{i}", [128, 512], F32, psum=True) for i in range(8)]

    with P.scope():
        emit_attn(P, dict(h_in=e["x_in"], h_out=hA, w_in=e["a_w_in"][0], w_out=e["a_w_out"][0], norm_gain=e["norm_gain"][0],
                          q_gain=e["a_q_gain"][0], k_gain=e["a_k_gain"][0], sinks=e["a_sinks"][0], ident=e["ident"],
                          mask_cur=e["mask_cur"], mask_prev=e["mask_prev"], cos2=e["cos2_0"], sinS=e["sinS_0"], kvalid=e["kvalid_0"]),
                  NS + 1, pb, tag="a0")
    h1 = hA[128:(NS + 1) * 128, :]
    with P.scope():
        z = T(P, "zero", [128, 2056], F32)
        P.op(P.dve, lambda: nc.vector.memset(z[:], 0.0), writes=[z.b])
        P.dma("sp", Sb_in, z[:], reads=[z.b])
        rfi = T(P, "rfinit", [8, 2], F32)
        P.op(P.dve, lambda: nc.vector.memset(rfi[:, 0:1], NEG), writes=[rfi.b])
        P.op(P.dve, lambda: nc.vector.memset(rfi[:, 1:2], 0.0), writes=[rfi.b])
        P.dma("sp", rfb_in, rfi[:], reads=[rfi.b])
        P.dma("sp", Sc_in, z[:, 0:1024], reads=[z.b])
        P.dma("sp", stB[0:128, :], z[:], reads=[z.b])
        P.dma("sp", stB[128:136, :], z[0:8, :], reads=[z.b])
    iob = dict(h_in=h1, h_out=hB, w_in=e["b_w_in"], w_out=e["b_w_out"], norm_gain=e["norm_gain"][1], conv_w=e["b_conv_w"], conv_b=e["b_conv_b"],
               gate_bias=e["b_gate_bias"], h_gain=e["b_h_gain"], ident=e["ident"], identf=e["identf"], mask_cur=e["mask_cur"],
               rv8=e["rv8"], rv_col=e["rv_col"], S_in=Sb_in.rearrange("p (h e) -> p h e", h=8),
               S_out=stB[0:128, :].rearrange("p (h e) -> p h e", h=8), rf_in=rfb_in, rf_out=stB[128:136, 0:2], hm=hm)
    with P.scope():
        emit_mlstm_p1(P, iob, NS, pb, tag="b1s", state_only=True)
    P.allgather(stB, gB)
    with P.scope():
        sel = T(P, "selpb", [128, 8], F32)
        P.dma("sp", sel[:], e["selp"], writes=[sel.b])
        idf = T(P, "cb_idf", [8, 8], F32)
        P.dma("sp", idf[:], e["identf"][0:8, 0:8], writes=[idf.b])
        S = T(P, "cb_S", [128, 8, 257], F32)
        R = T(P, "cb_R", [128, 8], F32)
        Fn = T(P, "cb_Fn", [128, 8], F32)
        for t_ in (S, R, Fn):
            P.op(P.dve, lambda t_=t_: nc.vector.memset(t_[:], 0.0), writes=[t_.b])
        Sl = [T(P, f"cb_Sl{i}", [128, 8, 257], F32) for i in range(2)]
        rl = [T(P, f"cb_rl{i}", [128, 2, 8], F32) for i in range(2)]
        Rj = T(P, "cb_Rj", [128, 8], F32)
        Rn = T(P, "cb_Rn", [128, 8], F32)
        ca = T(P, "cb_ca", [128, 8], F32)
        cbb = T(P, "cb_cb", [128, 8], F32)
        nsel = T(P, "cb_nsel", [128, 8], F32)
        P.op(P.dve, lambda: nc.vector.tensor_scalar(out=nsel[:], in0=sel[:], scalar1=-NEG, scalar2=NEG, op0=ALU.mult, op1=ALU.add),
             reads=[sel.b], writes=[nsel.b])
        tmpS = T(P, "cb_tmpS", [128, 8, 257], F32)
        for i in range(8):
            sl = Sl[i % 2]; r_ = rl[i % 2]
            P.dma("sp", sl[:].rearrange("p h e -> p (h e)"), gB[i * 136:i * 136 + 128, :], writes=[sl.b])
            for c_ in range(2):
                P.dma("sp", r_[:, c_, :], gB[i * 136 + 128:i * 136 + 136, c_:c_ + 1].rearrange("h o -> (h o)").partition_broadcast(128),
                      writes=[r_.b], allow_slow_non_contiguous=True)
            P.op(P.dve, lambda r_=r_: nc.vector.tensor_tensor(out=Rj[:], in0=r_[:, 0, :], in1=Fn[:], op=ALU.add), reads=[r_.b, Fn.b], writes=[Rj.b])
            P.op(P.dve, lambda i=i: nc.vector.scalar_tensor_tensor(out=Rj[:], in0=Rj[:], scalar=sel[:, i:i + 1], in1=nsel[:, i:i + 1].broadcast_to([128, 8]),
                                                                   op0=ALU.mult, op1=ALU.add), reads=[Rj.b, sel.b, nsel.b], writes=[Rj.b])
            P.op(P.dve, lambda: nc.vector.tensor_tensor(out=Rn[:], in0=R[:], in1=Rj[:], op=ALU.max), reads=[R.b, Rj.b], writes=[Rn.b])
            P.op(P.dve, lambda: nc.vector.tensor_tensor(out=ca[:], in0=R[:], in1=Rn[:], op=ALU.subtract), reads=[R.b, Rn.b], writes=[ca.b])
            P.op(P.dve, lambda: nc.vector.tensor_tensor(out=cbb[:], in0=Rj[:], in1=Rn[:], op=ALU.subtract), reads=[Rj.b, Rn.b], writes=[cbb.b])
            P.op(P.act, lambda: nc.scalar.activation(out=ca[:], in_=ca[:], func=AF.Exp), reads=[ca.b], writes=[ca.b])
            P.op(P.act, lambda: nc.scalar.activation(out=cbb[:], in_=cbb[:], func=AF.Exp), reads=[cbb.b], writes=[cbb.b])
            P.op(P.dve, lambda: nc.vector.tensor_tensor(out=S[:], in0=S[:], in1=ca[:].unsqueeze(2).broadcast_to([128, 8, 257]), op=ALU.mult),
                 reads=[S.b, ca.b], writes=[S.b])
            P.op(P.pool, lambda sl=sl: nc.gpsimd.tensor_tensor(out=tmpS[:], in0=sl[:], in1=cbb[:].unsqueeze(2).broadcast_to([128, 8, 257]), op=ALU.mult),
                 reads=[sl.b, cbb.b], writes=[tmpS.b])
            P.op(P.dve, lambda: nc.vector.tensor_tensor(out=S[:], in0=S[:], in1=tmpS[:], op=ALU.add), reads=[S.b, tmpS.b], writes=[S.b])
            P.op(P.dve, lambda: nc.vector.tensor_copy(out=R[:], in_=Rn[:]), reads=[Rn.b], writes=[R.b])
            P.op(P.dve, lambda i=i, r_=r_: nc.vector.scalar_tensor_tensor(out=Fn[:], in0=r_[:, 1, :], scalar=sel[:, i:i + 1], in1=Fn[:], op0=ALU.mult, op1=ALU.add),
                 reads=[r_.b, sel.b, Fn.b], writes=[Fn.b])
        P.dma("sp", Sb_in, S[:].rearrange("p h e -> p (h e)"), reads=[S.b])
        rfo = T(P, "cb_rfo", [8, 2], F32)
        dg8 = T(P, "cb_dg8", [8, 8], F32)
        for c_, t_ in enumerate((R, Fn)):
            P.op(P.dve, lambda t_=t_: nc.vector.tensor_tensor(out=dg8[:], in0=t_[0:8, :], in1=idf[:], op=ALU.mult), reads=[t_.b, idf.b], writes=[dg8.b])
            P.op(P.dve, lambda c_=c_: nc.vector.tensor_reduce(out=rfo[:, c_:c_ + 1], in_=dg8[:], axis=AX.X, op=ALU.add), reads=[dg8.b], writes=[rfo.b])
        P.dma("sp", rfb_in, rfo[:], reads=[rfo.b])
    with P.scope():
        emit_mlstm_p1(P, iob, NS, pb, tag="b1f", state_only=False)
    with P.scope():
        emit_mlstm_p2(P, iob, NS, pb, tag="b2")
    ioc = dict(h_in=hB, h_out=hC, w_in=e["c_w_in"], w_out=e["c_w_out"], norm_gain=e["norm_gain"][2], gamma=e["c_gamma"], o_gain=e["c_o_gain"],
               ident=e["ident"], bdmask=e["bdmask"], rst=e["rst"], rv_row=e["rv_row"], rv_col=e["rv_col"],
               state_in=Sc_in.rearrange("p (h e) -> p h e", h=8), state_out=stC[:, 0:1024].rearrange("p (h e) -> p h e", h=8), atot_out=stC[:, 1024:1032])
    with P.scope():
        emit_hgrn(P, ioc, NS, pb, tag="cs", state_only=True)
    P.allgather(stC, gC)
    with P.scope():
        sel = T(P, "selpc", [128, 8], F32)
        P.dma("sp", sel[:], e["selp"], writes=[sel.b])
        S = T(P, "cc_S", [128, 8, 128], F32)
        P.op(P.dve, lambda: nc.vector.memset(S[:], 0.0), writes=[S.b])
        Sl = [T(P, f"cc_Sl{i}", [128, 1032], F32) for i in range(2)]
        E_ = T(P, "cc_E", [128, 8], F32)
        for i in range(8):
            sl = Sl[i % 2]
            P.dma("sp", sl[:], gC[i * 128:(i + 1) * 128, :], writes=[sl.b])
            P.op(P.act, lambda i=i, sl=sl: nc.scalar.activation(out=E_[:], in_=sl[:, 1024:1032], func=AF.Exp, scale=sel[:, i:i + 1]),
                 reads=[sl.b, sel.b], writes=[E_.b])
            P.op(P.dve, lambda: nc.vector.tensor_tensor(out=S[:], in0=S[:], in1=E_[:].unsqueeze(2).broadcast_to([128, 8, 128]), op=ALU.mult),
                 reads=[S.b, E_.b], writes=[S.b])
            P.op(P.dve, lambda i=i, sl=sl: nc.vector.scalar_tensor_tensor(out=S[:].rearrange("p h e -> p (h e)"), in0=sl[:, 0:1024], scalar=sel[:, i:i + 1],
                                                                          in1=S[:].rearrange("p h e -> p (h e)"), op0=ALU.mult, op1=ALU.add),
                 reads=[sl.b, sel.b, S.b], writes=[S.b])
        P.dma("sp", Sc_in, S[:].rearrange("p h e -> p (h e)"), reads=[S.b])
    with P.scope():
        emit_hgrn(P, ioc, NS, pb, tag="cf", state_only=False)
    with P.scope():
        P.dma("sp", hh, hC[32 * 128:33 * 128, :])
    P.allgather(hh, gH)
    with P.scope():
        sel = T(P, "selh", [128, 9], F32)
        P.dma("sp", sel[:], e["sel9"], writes=[sel.b])
        acc = _sel_accumulate(P, nc, "sh", gH, 128, 128, 1024, sel, 8, extra=hC[0:128, :])
        P.dma("sp", hC[0:128, :], acc[:], reads=[acc.b])
    with P.scope():
        outf = lambda s: (hD0 if s == 0 else e["out"][(s - 1) * 128:s * 128, :])
        emit_attn(P, dict(h_in=hC, h_out=outf, w_in=e["a_w_in"][1], w_out=e["a_w_out"][1], norm_gain=e["norm_gain"][3],
                          q_gain=e["a_q_gain"][1], k_gain=e["a_k_gain"][1], sinks=e["a_sinks"][1], ident=e["ident"],
                          mask_cur=e["mask_cur"], mask_prev=e["mask_prev"], cos2=e["cos2_3"], sinS=e["sinS_3"], kvalid=e["kvalid_3"]),
                  NS, pb, tag="a3")
    P.barrier()
    print("fused program: n_instr =", P.n_instr)
    return P.finish()


def _rope_tables(NS, blk0):
    half = 32
    inv = (np.float32(10000.0) ** (-np.arange(half, dtype=np.float32) / np.float32(half))).astype(np.float32)
    grow = blk0 * 128 + np.arange(NS * 128)
    gtok = (grow - 112).astype(np.float32)
    ang = (gtok[:, None] * inv[None, :]).astype(np.float32)
    cos = np.cos(ang).astype(np.float32)
    sin = np.sin(ang).astype(np.float32)
    kv = (grow >= 112).astype(np.float32).reshape(NS, 128).T
    return np.concatenate([cos, cos], 1), np.concatenate([-sin, sin], 1), np.ascontiguousarray(kv)


def kernel(x, meta, norm_gain, a_w_in, a_q_gain, a_k_gain, a_sinks, a_w_out,
           b_w_in, b_conv_w, b_conv_b, b_gate_bias, b_h_gain, b_w_out,
           c_w_in, c_gamma, c_o_gain, c_w_out):
    f32 = lambda a: np.ascontiguousarray(np.asarray(a, dtype=np.float32))
    x = f32(x); meta = f32(meta)
    B = x.shape[0]
    NS = NSLOT
    hpad = np.zeros((B, 130 * 128, 1024), np.float32)
    hpad[:, 128 + 112:256] = meta[None]
    hpad[:, 256:] = x
    nc = build_fused(NS)
    ca = attn_consts(NS, 0, ml_dtypes)
    cc = hgrn_consts(NS, 0, ml_dtypes)
    cb = mlstm_consts(NS, 0, ml_dtypes)
    shared = dict(norm_gain=f32(norm_gain), a_w_in=f32(a_w_in), a_q_gain=f32(a_q_gain), a_k_gain=f32(a_k_gain), a_sinks=f32(a_sinks),
                  a_w_out=f32(a_w_out), b_w_in=f32(b_w_in[0]), b_conv_w=f32(b_conv_w[0]), b_conv_b=f32(b_conv_b[0]),
                  b_gate_bias=f32(b_gate_bias[0]), b_h_gain=f32(b_h_gain[0]), b_w_out=f32(b_w_out[0]),
                  c_w_in=f32(c_w_in[0]), c_gamma=f32(c_gamma), c_o_gain=f32(c_o_gain[0]), c_w_out=f32(c_w_out[0]),
                  ident=ca["ident"], identf=cb["identf"], mask_cur=ca["mask_cur"], mask_prev=ca["mask_prev"],
                  bdmask=cc["bdmask"], rst=cc["rst"])
    in_maps = []
    cores = [(b, q) for b in range(B) for q in range(4)]
    for ci, (b, q) in enumerate(cores):
        m = dict(shared)
        r0 = 32 * q * 128
        m["x_in"] = np.ascontiguousarray(hpad[b, r0:r0 + (NS + 1) * 128])
        m["cos2_0"], m["sinS_0"], m["kvalid_0"] = _rope_tables(NS + 1, 32 * q - 1)
        m["cos2_3"], m["sinS_3"], m["kvalid_3"] = _rope_tables(NS, 32 * q)
        cq = mlstm_consts(NS, q, ml_dtypes)
        hq = hgrn_consts(NS, q, ml_dtypes)
        m["rv8"] = cq["rv8"]; m["rv_col"] = cq["rv_col"]; m["rv_row"] = hq["rv_row"]
        s8 = np.zeros((128, 8), np.float32)
        s9 = np.zeros((128, 9), np.float32)
        if q > 0:
            s8[:, ci - 1] = 1.0
            s9[:, ci - 1] = 1.0
        else:
            s9[:, 8] = 1.0
        sp_ = np.zeros((128, 8), np.float32)
        for j in range(q):
            sp_[:, 4 * b + j] = 1.0
        m["sel8"] = s8; m["sel9"] = s9; m["selp"] = sp_
        in_maps.append(m)
    res = run_bass_kernel_spmd(nc, in_maps, core_ids=list(range(8)))
    out = np.zeros((B, 16384, 1024), np.float32)
    for ci, (b, q) in enumerate(cores):
        out[b, q * 4096:(q + 1) * 4096] = res.results[ci]["out"]
    return out
```
